# Optimizing a Trainium2 kernel written in Bass

```python
import math
import jax, jax.numpy as jnp
from jax import lax
import numpy as np


D_MODEL = 1024
BATCH = 1
SEQ = 16384
DEPTH = 2

N_META = 16
CHUNK = 128
PAD = CHUNK - N_META
EPS = 1e-6

CONV_A_WIDTH = D_MODEL
CONV_A_K = 3
SSD_HEAD_DIM = 64
SSD_HEADS = D_MODEL // SSD_HEAD_DIM
SSD_INNER = SSD_HEADS * SSD_HEAD_DIM
SSD_GROUPS = 4
SSD_STATE = 128
SSD_CONV_K = 4
SSD_CONV_DIM = SSD_INNER + 2 * SSD_GROUPS * SSD_STATE
RET_HEADS = 4
RET_QK_DIM = 256
RET_V_DIM = D_MODEL // RET_HEADS
RET_WIDTH = RET_HEADS * RET_V_DIM
ROPE_BASE = 10000.0
SB_HEADS = 8
SB_HEAD_DIM = D_MODEL // SB_HEADS
SB_WIDTH = SB_HEADS * SB_HEAD_DIM
N_BRANCH = 4
BRANCH_WIDTH = D_MODEL
D_FF = ((8 * D_MODEL // 3 + 255) // 256) * 256

IN_SIZES = (
    CONV_A_WIDTH, CONV_A_WIDTH, CONV_A_WIDTH,
    SSD_INNER, SSD_CONV_DIM, SSD_HEADS,
    RET_HEADS * RET_QK_DIM, RET_HEADS * RET_QK_DIM,
    RET_WIDTH, RET_WIDTH,
    SB_WIDTH, SB_WIDTH, SB_WIDTH,
    N_BRANCH * D_MODEL,
)
IN_WIDTH = sum(IN_SIZES)
IN_SPLITS = tuple(np.cumsum(IN_SIZES)[:-1].tolist())

kernel_name = 'hybrid_gated_conv_ssd_retention_stickbreaking'


def rmsnorm(x, w):
    xf = x.astype(jnp.float32)
    y = xf * lax.rsqrt(jnp.mean(xf * xf, axis=-1, keepdims=True) + EPS)
    return (y * w.astype(jnp.float32)).astype(x.dtype)


def causal_dwconv(u, w):
    k_taps = w.shape[0]
    length = u.shape[1]
    up = jnp.pad(u, ((0, 0), (k_taps - 1, 0), (0, 0)))
    out = up[:, 0:length] * w[0]
    for i in range(1, k_taps):
        out = out + up[:, i:i + length] * w[i]
    return out


def short_conv_mixer(b_gate, c_gate, xa, conv_w, valid):
    vm = valid[None, :, None].astype(xa.dtype)
    u = c_gate * xa * vm
    return (b_gate * causal_dwconv(u, conv_w)).astype(xa.dtype)


def ssd_mixer(z, xbc, dt_raw, conv_w, conv_b, dt_bias, a_log, d_skip, norm_w, valid):
    f32 = jnp.float32
    b, L, _ = z.shape
    nc = L // CHUNK
    hpg = SSD_HEADS // SSD_GROUPS
    vm = valid[None, :, None].astype(xbc.dtype)
    xbc = jax.nn.silu(causal_dwconv(xbc * vm, conv_w) + conv_b)
    xs, bm, cm = jnp.split(xbc, (SSD_INNER, SSD_INNER + SSD_GROUPS * SSD_STATE), axis=-1)
    xs = (xs * vm).astype(f32).reshape(b, nc, CHUNK, SSD_GROUPS, hpg, SSD_HEAD_DIM)
    bc = bm.astype(f32).reshape(b, nc, CHUNK, SSD_GROUPS, SSD_STATE)
    cc = cm.astype(f32).reshape(b, nc, CHUNK, SSD_GROUPS, SSD_STATE)
    dt = jax.nn.softplus(dt_raw.astype(f32) + dt_bias.astype(f32))
    a = (-jnp.exp(a_log.astype(f32)) * dt).reshape(b, nc, CHUNK, SSD_GROUPS, hpg)
    xdt = xs * dt.reshape(b, nc, CHUNK, SSD_GROUPS, hpg)[..., None]
    acs = jnp.moveaxis(jnp.cumsum(a, axis=2), 2, -1)
    causal = jnp.tril(jnp.ones((CHUNK, CHUNK), dtype=bool))
    seg = jnp.exp(jnp.where(causal, acs[..., :, None] - acs[..., None, :], -jnp.inf))
    cb = jnp.einsum('bclgn,bcsgn->bcgls', cc, bc)
    y_diag = jnp.einsum('bcgjls,bcsgjp->bclgjp', cb[:, :, :, None] * seg, xdt)
    decay_states = jnp.exp(acs[..., -1:] - acs)
    states = jnp.einsum('bclgn,bcgjl,bclgjp->bcgjpn', bc, decay_states, xdt)
    chunk_decay = jnp.exp(acs[..., -1])

    def step(hstate, inp):
        st, dec = inp
        return hstate * dec[..., None, None] + st, hstate

    h0 = jnp.zeros((b, SSD_GROUPS, hpg, SSD_HEAD_DIM, SSD_STATE), f32)
    _, prev = lax.scan(step, h0, (jnp.moveaxis(states, 1, 0), jnp.moveaxis(chunk_decay, 1, 0)))
    prev = jnp.moveaxis(prev, 0, 1)
    y_off = jnp.einsum('bclgn,bcgjpn,bcgjl->bclgjp', cc, prev, jnp.exp(acs))
    y = y_diag + y_off + xs * d_skip.astype(f32).reshape(SSD_GROUPS, hpg)[..., None]
    y = y.reshape(b, L, SSD_INNER) * jax.nn.silu(z.astype(f32))
    yg = y.reshape(b, L, SSD_GROUPS, SSD_INNER // SSD_GROUPS)
    yg = yg * lax.rsqrt(jnp.mean(yg * yg, axis=-1, keepdims=True) + EPS)
    return (yg.reshape(b, L, SSD_INNER) * norm_w.astype(f32)).astype(z.dtype)


def rotate(x, pos):
    half = x.shape[-1] // 2
    inv = ROPE_BASE ** (-jnp.arange(half, dtype=jnp.float32) / half)
    ang = pos.astype(jnp.float32)[:, None] * inv[None, :]
    cos = jnp.cos(ang)[None, :, None, :]
    sin = jnp.sin(ang)[None, :, None, :]
    x1, x2 = x[..., :half], x[..., half:]
    return jnp.concatenate([x1 * cos - x2 * sin, x1 * sin + x2 * cos], axis=-1)


def retention_mixer(q, k, v, g, valid):
    f32 = jnp.float32
    b, L, _ = q.shape
    nc = L // CHUNK
    pos = jnp.arange(L)
    qr = rotate(q.astype(f32).reshape(b, L, RET_HEADS, RET_QK_DIM), pos)
    kr = rotate(k.astype(f32).reshape(b, L, RET_HEADS, RET_QK_DIM), pos) * (RET_QK_DIM ** -0.5)
    vr = v.astype(f32).reshape(b, L, RET_HEADS, RET_V_DIM) * valid.astype(f32)[None, :, None, None]
    log_gamma = jnp.log(1.0 - jnp.power(2.0, -5.0 - jnp.arange(RET_HEADS, dtype=f32)))
    idx = jnp.arange(CHUNK, dtype=f32)
    rel = idx[:, None] - idx[None, :]
    dmask = jnp.where(rel >= 0, jnp.exp(log_gamma[:, None, None] * jnp.maximum(rel, 0.0)), 0.0)
    qc = qr.reshape(b, nc, CHUNK, RET_HEADS, RET_QK_DIM)
    kc = kr.reshape(b, nc, CHUNK, RET_HEADS, RET_QK_DIM)
    vc = vr.reshape(b, nc, CHUNK, RET_HEADS, RET_V_DIM)
    scores = jnp.einsum('bclhd,bcshd->bchls', qc, kc) * dmask
    y_in = jnp.einsum('bchls,bcshe->bclhe', scores, vc)
    k_decay = jnp.exp(log_gamma[:, None] * (CHUNK - 1 - idx)[None, :])
    kv = jnp.einsum('bcshd,hs,bcshe->bchde', kc, k_decay, vc)
    chunk_decay = jnp.exp(log_gamma * CHUNK)

    def step(r, kv_c):
        return r * chunk_decay[:, None, None] + kv_c, r

    r0 = jnp.zeros((b, RET_HEADS, RET_QK_DIM, RET_V_DIM), f32)
    _, prev = lax.scan(step, r0, jnp.moveaxis(kv, 1, 0))
    prev = jnp.moveaxis(prev, 0, 1)
    q_decay = jnp.exp(log_gamma[None, :] * (idx + 1.0)[:, None])
    y_cr = jnp.einsum('bclhd,bchde->bclhe', qc, prev) * q_decay[:, :, None]
    y = (y_in + y_cr).reshape(b, L, RET_HEADS, RET_V_DIM)
    mu = jnp.mean(y, axis=-1, keepdims=True)
    var = jnp.mean(jnp.square(y - mu), axis=-1, keepdims=True)
    y = ((y - mu) * lax.rsqrt(var + EPS)).reshape(b, L, RET_WIDTH)
    return (y * jax.nn.silu(g.astype(f32))).astype(q.dtype)


def stick_breaking_mixer(q, k, v, valid):
    f32 = jnp.float32
    b, L, _ = q.shape
    nb = L // CHUNK
    qh = q.reshape(b, L, SB_HEADS, SB_HEAD_DIM)
    kh = k.reshape(b, L, SB_HEADS, SB_HEAD_DIM)
    vh = v.reshape(b, L, SB_HEADS, SB_HEAD_DIM)
    qb = jnp.moveaxis(qh.reshape(b, nb, CHUNK, SB_HEADS, SB_HEAD_DIM), 1, 0)
    key_pos = jnp.arange(L)
    scale = SB_HEAD_DIM ** -0.5

    def block(args):
        q_blk, i = args
        t = i * CHUNK + jnp.arange(CHUNK)
        z = jnp.einsum('bthd,bshd->bhts', q_blk, kh).astype(f32) * scale
        m = (key_pos[None, :] < t[:, None]) & valid[None, :]
        l_neg = jnp.where(m, jax.nn.log_sigmoid(-z), 0.0)
        log_w = jax.nn.log_sigmoid(z) + lax.cumsum(l_neg, axis=3, reverse=True) - l_neg
        w = jnp.where(m, jnp.exp(log_w), 0.0)
        return jnp.einsum('bhts,bshe->bthe', w.astype(vh.dtype), vh)

    out = lax.map(block, (qb, jnp.arange(nb)))
    return jnp.moveaxis(out, 0, 1).reshape(b, L, SB_WIDTH).astype(q.dtype)


def hybrid_layer(h_res, valid, w_in, conv_a, ssd_conv_w, ssd_conv_b, ssd_dt_bias, ssd_a_log,
                 ssd_d, ssd_norm, w_branch, w_out, w_ffn_in, w_ffn_out,
                 n_mix_pre, n_mix_post, n_ffn_pre, n_ffn_post):
    b, L, _ = h_res.shape
    h = rmsnorm(h_res, n_mix_pre)
    proj = jnp.einsum('bld,de->ble', h, w_in)
    (a_b, a_c, a_x, s_z, s_xbc, s_dt, r_q, r_k, r_v, r_g,
     sb_q, sb_k, sb_v, gate_logits) = jnp.split(proj, IN_SPLITS, axis=-1)
    y_a = short_conv_mixer(a_b, a_c, a_x, conv_a, valid)
    y_b = ssd_mixer(s_z, s_xbc, s_dt, ssd_conv_w, ssd_conv_b, ssd_dt_bias, ssd_a_log, ssd_d, ssd_norm, valid)
    y_c = retention_mixer(r_q, r_k, r_v, r_g, valid)
    y_d = stick_breaking_mixer(sb_q, sb_k, sb_v, valid)
    branches = jnp.stack([y_a, y_b, y_c, y_d], axis=2).astype(h.dtype)
    up = jnp.einsum('blnw,nwd->blnd', branches, w_branch)
    gates = jax.nn.sigmoid(gate_logits.reshape(b, L, N_BRANCH, D_MODEL))
    merged = jnp.sum(gates * up, axis=2)
    mix = jnp.einsum('bld,de->ble', merged, w_out)
    h_res = h_res + rmsnorm(mix, n_mix_post)
    f = jnp.einsum('bld,df->blf', rmsnorm(h_res, n_ffn_pre), w_ffn_in)
    f_gate, f_up = jnp.split(f, 2, axis=-1)
    f = jnp.einsum('blf,fd->bld', jax.nn.silu(f_gate) * f_up, w_ffn_out)
    return h_res + rmsnorm(f, n_ffn_post)


def setup_inputs(seed: int = 0) -> dict:
    key = jax.random.key(seed)
    ks = jax.random.split(key, 18)
    f32 = jnp.float32

    def nrm(k, shape, scale):
        return jax.random.normal(k, shape, f32) * scale

    dt0 = jnp.exp(jax.random.uniform(ks[6], (DEPTH, SSD_HEADS), f32, math.log(1e-3), math.log(1e-1)))
    return {
        'x': nrm(ks[0], (BATCH, SEQ, D_MODEL), 1.0),
        'meta': nrm(ks[1], (N_META, D_MODEL), 1.0),
        'w_in': nrm(ks[2], (DEPTH, D_MODEL, IN_WIDTH), D_MODEL ** -0.5),
        'conv_a': nrm(ks[3], (DEPTH, CONV_A_K, CONV_A_WIDTH), CONV_A_K ** -0.5),
        'ssd_conv_w': nrm(ks[4], (DEPTH, SSD_CONV_K, SSD_CONV_DIM), SSD_CONV_K ** -0.5),
        'ssd_conv_b': nrm(ks[5], (DEPTH, SSD_CONV_DIM), 0.02),
        'ssd_dt_bias': dt0 + jnp.log(-jnp.expm1(-dt0)),
        'ssd_a_log': jnp.log(jax.random.uniform(ks[7], (DEPTH, SSD_HEADS), f32, 1.0, 16.0)),
        'ssd_d': 1.0 + nrm(ks[8], (DEPTH, SSD_HEADS), 0.1),
        'ssd_norm': 1.0 + nrm(ks[9], (DEPTH, SSD_INNER), 0.02),
        'w_branch': nrm(ks[10], (DEPTH, N_BRANCH, BRANCH_WIDTH, D_MODEL), BRANCH_WIDTH ** -0.5),
        'w_out': nrm(ks[11], (DEPTH, D_MODEL, D_MODEL), D_MODEL ** -0.5),
        'w_ffn_in': nrm(ks[12], (DEPTH, D_MODEL, 2 * D_FF), D_MODEL ** -0.5),
        'w_ffn_out': nrm(ks[13], (DEPTH, D_FF, D_MODEL), D_FF ** -0.5),
        'norm_mix_pre': 1.0 + nrm(ks[14], (DEPTH, D_MODEL), 0.02),
        'norm_mix_post': 1.0 + nrm(ks[15], (DEPTH, D_MODEL), 0.02),
        'norm_ffn_pre': 1.0 + nrm(ks[16], (DEPTH, D_MODEL), 0.02),
        'norm_ffn_post': 1.0 + nrm(ks[17], (DEPTH, D_MODEL), 0.02),
    }


def reference(x, meta, w_in, conv_a, ssd_conv_w, ssd_conv_b, ssd_dt_bias, ssd_a_log, ssd_d,
              ssd_norm, w_branch, w_out, w_ffn_in, w_ffn_out,
              norm_mix_pre, norm_mix_post, norm_ffn_pre, norm_ffn_post):
    b = x.shape[0]
    dtype = x.dtype
    h = jnp.concatenate([
        jnp.zeros((b, PAD, D_MODEL), dtype),
        jnp.broadcast_to(meta.astype(dtype)[None], (b, N_META, D_MODEL)),
        x,
    ], axis=1)
    valid = jnp.arange(h.shape[1]) >= PAD
    for l in range(DEPTH):
        h = hybrid_layer(h, valid, w_in[l], conv_a[l], ssd_conv_w[l], ssd_conv_b[l],
                         ssd_dt_bias[l], ssd_a_log[l], ssd_d[l], ssd_norm[l], w_branch[l],
                         w_out[l], w_ffn_in[l], w_ffn_out[l], norm_mix_pre[l],
                         norm_mix_post[l], norm_ffn_pre[l], norm_ffn_post[l])
    return h[:, CHUNK:]
```

```python
import math
import numpy as np
import ml_dtypes
import concourse.bass as bass
import concourse.mybir as mybir
from concourse.bass_utils import run_bass_kernel_spmd

F32 = mybir.dt.float32
BF16 = mybir.dt.bfloat16
AF = mybir.ActivationFunctionType
ALU = mybir.AluOpType
EPOCH = 24000

L = 16512
NCH = 129
D = 1024
KC = 8
EPS = 1e-6
NCORE = 8
TOK = L // NCORE
HALF = TOK // 2
TT = 344
DFF = 2816


class Buf:
    __slots__ = ("t", "name", "w", "r", "dsem", "dcnt", "excl")

    def __init__(self, t, name, excl=False):
        self.t = t
        self.name = name
        self.w = None
        self.r = []
        self.dsem = None
        self.dcnt = 0
        self.excl = excl

    def __getitem__(self, idx):
        return self.t[idx]


class Sched:
    def __init__(self, nc):
        self.nc = nc
        self.engs = {"pe": nc.tensor, "act": nc.scalar, "dve": nc.vector,
                     "pool": nc.gpsimd, "sp": nc.sync}
        self.sems = {k: [] for k in self.engs}
        self.cnt = {k: 0 for k in self.engs}
        self.seen = {k: {} for k in self.engs}
        self.nsem = 0
        self.ninst = 0
        self.dsems = []
        self.sb_off = 16640
        self.nps = 0

    def new_sem(self, name):
        self.nsem += 1
        return self.nc.alloc_semaphore(name=f"{name}_{self.nsem}")

    def sb(self, name, shape, dt):
        nbytes = int(np.prod(shape[1:])) * (4 if dt == F32 else 2)
        nbytes = (nbytes + 63) // 64 * 64
        t = self.nc.alloc_sbuf_tensor_at(f"{name}_{self.ninst}_{self.sb_off}", list(shape), dt, offset=self.sb_off)
        self.sb_off += nbytes
        assert self.sb_off <= 229376, ("sbuf overflow", name, self.sb_off)
        return Buf(t, name)

    def ps(self, name, shape, dt=F32):
        self.nps += 1
        assert self.nps <= 8
        return Buf(self.nc.alloc_psum_tensor(name, list(shape), dt), name, excl=True)

    def _deps(self, reads, writes):
        deps = []
        w2 = list(writes)
        for b in reads:
            if b.excl:
                w2.append(b)
                continue
            if b.w is not None:
                deps.append(b.w)
        for b in w2:
            if b.w is not None:
                deps.append(b.w)
            deps.extend(b.r)
        return deps, w2

    def _wait(self, ek, deps):
        eng = self.engs[ek]
        seen = self.seen[ek]
        best = {}
        for (s, v) in deps:
            if seen.get(s, 0) >= v:
                continue
            if best.get(s, 0) < v:
                best[s] = v
        for s, v in best.items():
            eng.wait_ge(s, v)
            seen[s] = v

    def _record(self, dep, reads, writes):
        for b in writes:
            b.w = dep
            b.r = []
        for b in reads:
            b.r.append(dep)
            if len(b.r) > 8:
                m = {}
                for (s, v) in b.r:
                    if m.get(s, 0) < v:
                        m[s] = v
                b.r = list(m.items())

    def op(self, ek, fn, reads=(), writes=()):
        deps, w2 = self._deps(reads, writes)
        self._wait(ek, deps)
        n = self.cnt[ek]
        ep, off = divmod(n, EPOCH)
        while len(self.sems[ek]) <= ep:
            self.sems[ek].append(self.new_sem(ek))
        sem = self.sems[ek][ep]
        fn(self.engs[ek]).then_inc(sem, 1)
        self.cnt[ek] = n + 1
        dep = (sem, off + 1)
        self._record(dep, [b for b in reads if not b.excl], w2)
        self.ninst += 1
        return dep

    def dma(self, qk, pairs, reads=(), writes=(), sembuf=None):
        deps, w2 = self._deps(reads, writes)
        sb = sembuf if sembuf is not None else (list(reads) + list(writes))[0]
        if sb.dsem is None:
            sb.dsem = self.new_sem("d")
            self.dsems.append(sb)
        if sb.dcnt > 0:
            deps.append((sb.dsem, sb.dcnt))
        self._wait(qk, deps)
        eng = self.engs[qk]
        for (o, i) in pairs:
            eng.dma_start(out=o, in_=i).then_inc(sb.dsem, 16)
            sb.dcnt += 16
            self.ninst += 1
        dep = (sb.dsem, sb.dcnt)
        self._record(dep, [b for b in reads if not b.excl], w2)
        return dep

    def barrier(self):
        deps = []
        for k in self.engs:
            n = self.cnt[k]
            if n == 0:
                continue
            ep, off = divmod(n - 1, EPOCH)
            deps.append((self.sems[k][ep], off + 1))
        for b in self.dsems:
            deps.append((b.dsem, b.dcnt))
        for k in self.engs:
            self._wait(k, deps)

    def finish(self, bufs):
        deps = []
        for b in bufs:
            if b.w is not None:
                deps.append(b.w)
        self._wait("sp", deps)


C_ID, C_TRI, C_SLT, C_ONE, C_U, C_LM, C_MS = range(7)


def host_consts():
    i = np.arange(128)
    c = np.zeros((128, 7, 128), np.float32)
    c[:, C_ID, :] = np.eye(128)
    c[:, C_TRI, :] = (i[:, None] <= i[None, :])
    c[:, C_SLT, :] = (i[:, None] > i[None, :])
    c[:, C_ONE, :] = 1.0
    c[:, C_U, :] = (i[:, None] >= i[None, :])
    c[:, C_LM, :] = (i[:, None] < i[None, :])
    c[:, C_MS, :] = (i[None, :] > i[:, None])
    return c


def rope_tables():
    half = 128
    inv = np.power(np.float32(10000.0), -(np.arange(half, dtype=np.float32) / np.float32(half))).astype(np.float32)
    pos = np.arange(L, dtype=np.float32)
    ang = (pos[None, :] * inv[:, None]).astype(np.float32)
    return np.cos(ang.astype(np.float64)).astype(np.float32), np.sin(ang.astype(np.float64)).astype(np.float32)


NFM = 12
NTM = 258


def build_H():
    nc = bass.Bass("TRN2", target_bir_lowering=False)
    S = Sched(nc)
    ei = lambda n, s, dt=F32: nc.dram_tensor(n, list(s), dt, kind="ExternalInput")
    eo = lambda n, s, dt=F32: nc.dram_tensor(n, list(s), dt, kind="ExternalOutput")
    hT_d = ei("hT", [KC, 128, L])
    npre_d = ei("npre", [128, KC])
    wfm_d = ei("wfm", [KC, 128, NFM * 128])
    wtm_d = ei("wtm", [KC, 128, NTM])
    cva_d = ei("cva", [128, 3])
    scw_d = ei("scw", [128, 12])
    scb_d = ei("scb", [128, 3])
    dtb_d = ei("dtb", [128, 2])
    alog_d = ei("alog", [128, 2])
    drow_d = ei("drow", [128, 128])
    lgam_d = ei("lgam", [128, 1])
    cos_d = ei("cos", [128, L])
    sin_d = ei("sin", [128, L])
    cst_d = ei("cst", [128, 7, 128])
    yaT_d = eo("yaT", [128, L])
    yb_d = eo("yb", [L, 128])
    yc_d = eo("yc", [L, 128])
    ydT_d = eo("ydT", [128, L])
    scfm_d = nc.dram_tensor("scfm", [10, 128, L], F32)
    rv_d = nc.dram_tensor("rvs", [L, 128], BF16)
    OUT = [Buf(yaT_d, "yaTd"), Buf(yb_d, "ybd"), Buf(yc_d, "ycd"), Buf(ydT_d, "ydTd")]
    SCFM = [[Buf(scfm_d, f"scfm{b}_{i}") for i in range(33)] for b in range(10)]
    RVD = [Buf(rv_d, f"rvd{c}") for c in range(NCH)]

    CST = S.sb("cst", [128, 7, 128], F32)
    CSTB = S.sb("cstb", [128, 7, 128], BF16)
    SMALL = S.sb("small", [128, 32], F32)
    DROW = S.sb("drow", [128, 128], F32)
    DT = S.sb("dt", [128, 2 * NCH], F32)
    off_persist = S.sb_off
    QT = [S.sb(f"qt{i}", [128, 512], BF16) for i in range(33)]
    KT = [S.sb(f"kt{i}", [128, 512], BF16) for i in range(33)]
    SV = [S.sb(f"sv{i}", [128, 4, 128], BF16) for i in range(33)]
    PSB = [S.ps(f"ps{i}", [128, 512]) for i in range(7)]
    PSH = S.ps("psh", [128, 1024], BF16)
    base_off = S.sb_off

    S.dma("sp", [(CST[:, :, :], cst_d[:, :, :])], writes=[CST])
    S.dma("sp", [(SMALL[:, 0:3], cva_d[:, :]), (SMALL[:, 3:15], scw_d[:, :]), (SMALL[:, 15:18], scb_d[:, :]),
                 (SMALL[:, 18:20], dtb_d[:, :]), (SMALL[:, 20:22], alog_d[:, :]), (SMALL[:, 22:23], lgam_d[:, :]),
                 (SMALL[:, 24:32], npre_d[:, :])], writes=[SMALL])
    S.dma("sp", [(DROW[:, :], drow_d[:, :])], writes=[DROW])
    S.op("dve", lambda e: e.tensor_copy(CSTB[:, :, :], CST[:, :, :]), reads=[CST], writes=[CSTB])

    def tile_rng(i):
        c0 = i * 512
        return c0, min(512, L - c0)

    WFM = S.sb("wfm", [128, KC, NFM * 128], BF16)
    WTM = S.sb("wtm", [128, KC, NTM], BF16)
    WST = [S.sb(f"wst{i}", [128, NFM * 128], F32) for i in range(1)]
    for k in range(KC):
        st = WST[0]
        S.dma("sp", [(st[:, :], wfm_d[k, :, :])], writes=[st])
        S.op("pool", lambda e: e.tensor_copy(WFM[:, k, :], st[:, :]), reads=[st], writes=[WFM])
    for k in range(KC):
        st = WST[0]
        S.dma("sp", [(st[:, 0:NTM], wtm_d[k, :, :])], writes=[st])
        S.op("pool", lambda e: e.tensor_copy(WTM[:, k, :], st[:, 0:NTM]), reads=[st], writes=[WTM])
    HT = [S.sb(f"ht{i}", [128, KC, 512], F32) for i in range(2)]
    HN = [S.sb(f"hn{i}", [128, KC, 512], BF16) for i in range(2)]
    SQ = [S.sb(f"sq{i}", [128, 512], F32) for i in range(2)]
    R1 = S.sb("r1", [128, 512], F32)
    RS = S.sb("rs", [128, 512], F32)
    STG = [S.sb(f"stg{i}", [128, 512], F32) for i in range(3)]
    STV = [S.sb(f"stv{i}", [128, 128], BF16) for i in range(2)]
    stg_i = 0
    SCALE_Q = 1.0 / math.sqrt(128.0)
    for ti in range(33):
        c0, n = tile_rng(ti)
        ht = HT[ti % 2]
        hn = HN[ti % 2]
        S.dma("sp", [(ht[:, k, 0:n], hT_d[k, :, c0:c0 + n]) for k in range(KC)], writes=[ht])
        pss = PSB[0]
        for k in range(KC):
            sq = SQ[k % 2]
            S.op("pool", lambda e: e.tensor_tensor(sq[:, 0:n], ht[:, k, 0:n], ht[:, k, 0:n], ALU.mult), reads=[ht], writes=[sq])
            S.op("pe", lambda e: e.matmul(pss[:, 0:n], CST[:, C_ONE, :], sq[:, 0:n], start=(k == 0), stop=(k == KC - 1)),
                 reads=[CST, sq], writes=[pss])
        S.op("act", lambda e: e.activation(R1[:, 0:n], pss[:, 0:n], AF.Sqrt, bias=EPS, scale=1.0 / D), reads=[pss], writes=[R1])
        S.op("dve", lambda e: e.reciprocal(RS[:, 0:n], R1[:, 0:n]), reads=[R1], writes=[RS])
        for k in range(KC):
            S.op("dve", lambda e: e.scalar_tensor_tensor(hn[:, k, 0:n], ht[:, k, 0:n], SMALL[:, 24 + k:25 + k], RS[:, 0:n], ALU.mult, ALU.mult),
                 reads=[ht, SMALL, RS], writes=[hn])
        for blk in range(NFM):
            ps = PSB[1 + blk % 2]
            for k in range(KC):
                S.op("pe", lambda e: e.matmul(ps[:, 0:n], WFM[:, k, blk * 128:(blk + 1) * 128], hn[:, k, 0:n], start=(k == 0), stop=(k == KC - 1)),
                     reads=[WFM, hn], writes=[ps])
            if blk < 10:
                st = STG[stg_i % 3]
                stg_i += 1
                S.op("act", lambda e: e.activation(st[:, 0:n], ps[:, 0:n], AF.Copy), reads=[ps], writes=[st])
                S.dma("pool", [(scfm_d[blk, :, c0:c0 + n], st[:, 0:n])], reads=[st], writes=[SCFM[blk][ti]])
            elif blk == 10:
                S.op("act", lambda e: e.activation(QT[ti][:, 0:n], ps[:, 0:n], AF.Copy, scale=SCALE_Q), reads=[ps], writes=[QT[ti]])
            else:
                S.op("dve", lambda e: e.tensor_copy(KT[ti][:, 0:n], ps[:, 0:n]), reads=[ps], writes=[KT[ti]])
        for sub in range(n // 128):
            ch = ti * 4 + sub
            ps = PSB[3 + sub % 2]
            for k in range(KC):
                S.op("pe", lambda e: e.matmul(ps[:, 0:NTM], hn[:, k, sub * 128:(sub + 1) * 128], WTM[:, k, :], start=(k == 0), stop=(k == KC - 1)),
                     reads=[hn, WTM], writes=[ps])
            stv = STV[ch % 2]
            S.op("dve", lambda e: e.tensor_copy(stv[:, :], ps[:, 0:128]), reads=[ps], writes=[stv])
            if ch == 0:
                S.op("dve", lambda e: e.memset(stv[0:112, :], 0.0), writes=[stv])
            S.dma("pool", [(rv_d[ch * 128:(ch + 1) * 128, :], stv[:, :])], reads=[stv], writes=[RVD[ch]])
            S.op("act", lambda e: e.activation(SV[ti][:, sub, :], ps[:, 128:256], AF.Copy), reads=[ps], writes=[SV[ti]])
            for hh in range(2):
                S.op("dve", lambda e: e.tensor_copy(DT[:, hh * NCH + ch:hh * NCH + ch + 1], ps[:, 256 + hh:257 + hh]), reads=[ps], writes=[DT])

    S.barrier()
    S.sb_off = base_off
    EB = [S.sb(f"eb{i}", [128, 512], F32) for i in range(2)]
    SPB = [S.sb(f"spb{i}", [128, 512], BF16) for i in range(2)]
    ECB = [S.sb(f"ecb{i}", [128, 512], F32) for i in range(2)]
    WB = [S.sb(f"wb{i}", [128, 512], BF16) for i in range(2)]
    OS = [S.sb(f"os{i}", [128, 512], F32) for i in range(2)]
    it = 0
    for ti in range(33):
        c0, n = tile_rng(ti)
        i0 = ti * 4
        nb = n // 128
        acc = PSB[2]
        outp = PSB[3]
        first = True
        for j in range(i0 + nb - 1, -1, -1):
            cs = max(0, j - i0) * 128
            kt = KT[j // 4]
            ko = (j % 4) * 128
            z = PSB[it % 2]
            eb, spb, ecb, wb = EB[it % 2], SPB[it % 2], ECB[it % 2], WB[it % 2]
            it += 1
            S.op("pe", lambda e: e.matmul(z[:, cs:n], kt[:, ko:ko + 128], QT[ti][:, cs:n], start=True, stop=True),
                 reads=[kt, QT[ti]], writes=[z])
            S.op("act", lambda e: e.activation(eb[:, cs:n], z[:, cs:n], AF.Exp), reads=[z], writes=[eb])
            if j >= i0:
                S.op("dve", lambda e: e.tensor_tensor(eb[:, cs:cs + 128], eb[:, cs:cs + 128], CST[:, C_MS, :], ALU.mult),
                     reads=[eb, CST], writes=[eb])
            if j == 0:
                S.op("dve", lambda e: e.memset(eb[0:112, cs:n], 0.0), writes=[eb])
            S.op("act", lambda e: e.activation(spb[:, cs:n], eb[:, cs:n], AF.Ln, bias=1.0), reads=[eb], writes=[spb])
            S.op("pe", lambda e: e.matmul(acc[:, cs:n], CSTB[:, C_U, :], spb[:, cs:n], start=first, stop=False, skip_group_check=True),
                 reads=[CSTB, spb], writes=[acc])
            S.op("act", lambda e: e.activation(ecb[:, cs:n], acc[:, cs:n], AF.Exp, scale=-1.0), reads=[acc], writes=[ecb])
            S.op("pe", lambda e: e.matmul(acc[:, cs:n], CSTB[:, C_LM, :], spb[:, cs:n], start=False, stop=False, skip_group_check=True),
                 reads=[CSTB, spb], writes=[acc])
            S.op("dve", lambda e: e.tensor_tensor(wb[:, cs:n], eb[:, cs:n], ecb[:, cs:n], ALU.mult), reads=[eb, ecb], writes=[wb])
            svb = SV[j // 4]
            S.op("pe", lambda e: e.matmul(outp[:, cs:n], svb[:, j % 4, :], wb[:, cs:n], start=first, stop=(j == 0), skip_group_check=True),
                 reads=[svb, wb], writes=[outp])
            first = False
        os_ = OS[ti % 2]
        S.op("dve", lambda e: e.tensor_copy(os_[:, 0:n], outp[:, 0:n]), reads=[outp], writes=[os_])
        S.dma("pool", [(ydT_d[:, c0:c0 + n], os_[:, 0:n])], reads=[os_], writes=[OUT[3]])

    S.barrier()
    S.sb_off = off_persist
    CW = 2048
    CIN = [[S.sb(f"cin{b}_{i}", [128, CW], F32) for i in range(2)] for b in range(3)]
    UB = [S.sb(f"ub{i}", [128, CW + 2], F32) for i in range(2)]
    ACCB = S.sb("accb", [128, CW], F32)
    YO = [S.sb(f"yo{i}", [128, CW], F32) for i in range(2)]
    ntile = (L + CW - 1) // CW
    for t in range(ntile):
        c0 = t * CW
        n = min(CW, L - c0)
        cb_, cc_, cx_ = CIN[0][t % 2], CIN[1][t % 2], CIN[2][t % 2]
        S.dma("sp", [(cb_[:, 0:n], scfm_d[0, :, c0:c0 + n])], writes=[cb_])
        S.dma("sp", [(cc_[:, 0:n], scfm_d[1, :, c0:c0 + n])], writes=[cc_])
        S.dma("sp", [(cx_[:, 0:n], scfm_d[2, :, c0:c0 + n])], writes=[cx_])
        u = UB[t % 2]
        S.op("dve", lambda e: e.tensor_tensor(u[:, 2:2 + n], cc_[:, 0:n], cx_[:, 0:n], ALU.mult), reads=[cc_, cx_], writes=[u])
        if t == 0:
            S.op("dve", lambda e: e.memset(u[:, 0:2 + 112], 0.0), writes=[u])
        else:
            up = UB[(t - 1) % 2]
            S.op("pool", lambda e: e.tensor_copy(u[:, 0:2], up[:, CW:CW + 2]), reads=[up], writes=[u])
        S.op("dve", lambda e: e.tensor_single_scalar(ACCB[:, 0:n], u[:, 0:n], SMALL[:, 0:1], ALU.mult), reads=[u, SMALL], writes=[ACCB])
        for i in (1, 2):
            S.op("dve", lambda e: e.scalar_tensor_tensor(ACCB[:, 0:n], u[:, i:i + n], SMALL[:, i:i + 1], ACCB[:, 0:n], ALU.mult, ALU.add),
                 reads=[u, SMALL, ACCB], writes=[ACCB])
        yo = YO[t % 2]
        S.op("pool", lambda e: e.tensor_tensor(yo[:, 0:n], ACCB[:, 0:n], cb_[:, 0:n], ALU.mult), reads=[ACCB, cb_], writes=[yo])
        S.dma("pool", [(yaT_d[:, c0:c0 + n], yo[:, 0:n])], reads=[yo], writes=[OUT[0]])

    S.barrier()
    S.sb_off = off_persist
    RA = [S.sb(f"ra{i}", [128, 128], F32) for i in range(2)]
    SEG = [S.sb(f"seg{i}", [128, 128], F32) for i in range(3)]
    SEGM = [S.sb(f"segm{i}", [128, 128], F32) for i in range(3)]
    PB = [S.sb(f"pb{i}", [128, 128], BF16) for i in range(2)]
    TMPY = [S.sb(f"tmpy{i}", [128, 128], F32) for i in range(2)]
    YB = [S.sb(f"yb{i}", [128, 128], F32) for i in range(2)]
    PS_D, PS_ST, PS_X, PS_YD, PS_YO, PS_S = PSB[0], PSB[1], PSB[2], PSB[3], PSB[4], PSB[5]
    cnt = {"d": 0, "p": 0}

    def decay(a_ap, a_buf, scale=None, slot=None):
        i = cnt["d"]
        cnt["d"] += 1
        ra = RA[i % 2]
        sg = SEG[slot if slot is not None else i % 2]
        sm = SEGM[slot if slot is not None else i % 2]
        S.op("dve", lambda e: e.tensor_single_scalar(ra[:, :], CST[:, C_TRI, :], a_ap, ALU.mult), reads=[CST, a_buf], writes=[ra])
        S.op("pe", lambda e: e.matmul(PS_D[:, 0:128], CST[:, C_SLT, :], ra[:, :], start=True, stop=True), reads=[CST, ra], writes=[PS_D])
        S.op("act", lambda e: e.activation(sg[:, :], PS_D[:, 0:128], AF.Exp), reads=[PS_D], writes=[sg])
        if scale is None:
            S.op("pool", lambda e: e.tensor_tensor(sm[:, :], sg[:, :], CST[:, C_TRI, :], ALU.mult), reads=[sg, CST], writes=[sm])
        else:
            S.op("dve", lambda e: e.scalar_tensor_tensor(sm[:, :], sg[:, :], scale, CST[:, C_TRI, :], ALU.mult, ALU.mult), reads=[sg, CST], writes=[sm])
        return sg, sm

    def lin_head(q_list, kd_list, v_ap, v_buf, sm, ecol_ap, ecol_buf, etot_ap, etot_buf, S_list, Sbf_list, ybuf, y0, dv):
        i = cnt["p"]
        cnt["p"] += 1
        pb = PB[i % 2]
        tm = TMPY[i % 2]
        S.op("dve", lambda e: e.tensor_tensor(pb[:, :], PS_ST[:, 0:128], sm[:, :], ALU.mult), reads=[PS_ST, sm], writes=[pb])
        S.op("pe", lambda e: e.matmul(PS_YD[:, 0:dv], pb[:, :], v_ap, start=True, stop=True), reads=[pb, v_buf], writes=[PS_YD])
        nk = len(q_list)
        for kk in range(nk):
            qa, qb = q_list[kk]
            S.op("pe", lambda e: e.matmul(PS_YO[:, 0:dv], qa, Sbf_list[kk][:, 0:dv], start=(kk == 0), stop=(kk == nk - 1)),
                 reads=[qb, Sbf_list[kk]], writes=[PS_YO])
        S.op("act", lambda e: e.activation(tm[:, 0:dv], PS_YD[:, 0:dv], AF.Copy), reads=[PS_YD], writes=[tm])
        S.op("dve", lambda e: e.scalar_tensor_tensor(ybuf[:, y0:y0 + dv], PS_YO[:, 0:dv], ecol_ap, tm[:, 0:dv], ALU.mult, ALU.add),
             reads=[PS_YO, ecol_buf, tm], writes=[ybuf])
        for kk in range(nk):
            ka, kb = kd_list[kk]
            S.op("pe", lambda e: e.matmul(PS_S[:, 0:dv], ka, v_ap, start=True, stop=True), reads=[kb, v_buf], writes=[PS_S])
            S.op("dve", lambda e: e.scalar_tensor_tensor(S_list[kk][:, 0:dv], S_list[kk][:, 0:dv], etot_ap, PS_S[:, 0:dv], ALU.mult, ALU.add),
                 reads=[S_list[kk], etot_buf, PS_S], writes=[S_list[kk]])
            S.op("pool", lambda e: e.tensor_copy(Sbf_list[kk][:, 0:dv], S_list[kk][:, 0:dv]), reads=[S_list[kk]], writes=[Sbf_list[kk]])

    TMPD = S.sb("tmpd", [128, 2 * NCH], F32)
    DTS = S.sb("dts", [128, 2 * NCH], F32)
    AALL = S.sb("aall", [128, 2 * NCH], F32)
    ECOL = S.sb("ecol", [128, 2 * NCH], F32)
    ETOT = S.sb("etot", [128, 2 * NCH], F32)
    NA = S.sb("na", [128, 2], F32)
    for h in range(2):
        sl = slice(h * NCH, (h + 1) * NCH)
        S.op("act", lambda e: e.activation(TMPD[:, sl], DT[:, sl], AF.Exp, bias=SMALL[:, 18 + h:19 + h]), reads=[DT, SMALL], writes=[TMPD])
        S.op("act", lambda e: e.activation(DTS[:, sl], TMPD[:, sl], AF.Ln, bias=1.0), reads=[TMPD], writes=[DTS])
    S.op("act", lambda e: e.activation(NA[:, :], SMALL[:, 20:22], AF.Exp), reads=[SMALL], writes=[NA])
    S.op("dve", lambda e: e.tensor_single_scalar(NA[:, :], NA[:, :], -1.0, ALU.mult), reads=[NA], writes=[NA])
    for h in range(2):
        sl = slice(h * NCH, (h + 1) * NCH)
        S.op("dve", lambda e: e.tensor_single_scalar(AALL[:, sl], DTS[:, sl], NA[:, h:h + 1], ALU.mult), reads=[DTS, NA], writes=[AALL])
    S.op("pe", lambda e: e.matmul(PS_X[:, 0:2 * NCH], CST[:, C_TRI, :], AALL[:, :], start=True, stop=True), reads=[CST, AALL], writes=[PS_X])
    S.op("act", lambda e: e.activation(ECOL[:, :], PS_X[:, 0:2 * NCH], AF.Exp), reads=[PS_X], writes=[ECOL])
    S.op("pe", lambda e: e.matmul(PS_X[:, 0:2 * NCH], CST[:, C_ONE, :], AALL[:, :], start=True, stop=True), reads=[CST, AALL], writes=[PS_X])
    S.op("act", lambda e: e.activation(ETOT[:, :], PS_X[:, 0:2 * NCH], AF.Exp), reads=[PS_X], writes=[ETOT])

    XR = [[S.sb(f"xr{b}_{i}", [128, 515], F32) for i in range(2)] for b in range(3)]
    ACS = S.sb("acs", [128, 512], F32)
    XS = [S.sb(f"xs{i}", [128, 512], F32) for i in range(2)]
    BTt = [S.sb(f"btt{i}", [128, 512], BF16) for i in range(2)]
    CTt = [S.sb(f"ctt{i}", [128, 512], BF16) for i in range(2)]
    XDT = [S.sb(f"xdt{i}", [128, 64], BF16) for i in range(2)]
    KD = [S.sb(f"kd{i}", [128, 128], BF16) for i in range(4)]
    TMP2 = S.sb("tmp2", [128, 128], F32)
    SST = [S.sb(f"sst{i}", [128, 64], F32) for i in range(2)]
    SSB = [S.sb(f"ssb{i}", [128, 64], BF16) for i in range(2)]
    for h in range(2):
        S.op("dve", lambda e: e.memset(SST[h][:, :], 0.0), writes=[SST[h]])
        S.op("dve", lambda e: e.memset(SSB[h][:, :], 0.0), writes=[SSB[h]])
    for ti in range(33):
        c0, n = tile_rng(ti)
        outs = [XS[ti % 2], BTt[ti % 2], CTt[ti % 2]]
        for b in range(3):
            xr = XR[b][ti % 2]
            if ti == 0:
                S.dma("sp", [(xr[:, 3:3 + n], scfm_d[3 + b, :, 0:n])], writes=[xr])
                S.op("dve", lambda e: e.memset(xr[:, 0:3 + 112], 0.0), writes=[xr])
            else:
                S.dma("sp", [(xr[:, 0:3 + n], scfm_d[3 + b, :, c0 - 3:c0 + n])], writes=[xr])
            w0 = 3 + 4 * b
            S.op("dve", lambda e: e.tensor_single_scalar(ACS[:, 0:n], xr[:, 0:n], SMALL[:, w0:w0 + 1], ALU.mult), reads=[xr, SMALL], writes=[ACS])
            for i in (1, 2, 3):
                S.op("dve", lambda e: e.scalar_tensor_tensor(ACS[:, 0:n], xr[:, i:i + n], SMALL[:, w0 + i:w0 + i + 1], ACS[:, 0:n], ALU.mult, ALU.add),
                     reads=[xr, SMALL, ACS], writes=[ACS])
            S.op("act", lambda e: e.activation(outs[b][:, 0:n], ACS[:, 0:n], AF.Silu, bias=SMALL[:, 15 + b:16 + b]), reads=[ACS, SMALL], writes=[outs[b]])
        xs, bt, ct = outs
        if ti == 0:
            S.op("dve", lambda e: e.memset(xs[:, 0:112], 0.0), writes=[xs])
        for sub in range(n // 128):
            c = ti * 4 + sub
            cs_ = slice(sub * 128, (sub + 1) * 128)
            S.op("pe", lambda e: e.matmul(PS_ST[:, 0:128], bt[:, cs_], ct[:, cs_], start=True, stop=True), reads=[bt, ct], writes=[PS_ST])
            S.op("pe", lambda e: e.transpose(PSH[:, 0:128], bt[:, cs_], CSTB[:, C_ID, :]), reads=[bt, CSTB], writes=[PSH])
            S.op("pe", lambda e: e.transpose(PS_X[:, 0:128], xs[:, cs_], CST[:, C_ID, :]), reads=[xs, CST], writes=[PS_X])
            yb_ = YB[c % 2]
            for h in range(2):
                col = h * NCH + c
                sg, sm = decay(AALL[:, col:col + 1], AALL)
                xdt = XDT[h]
                kd = KD[h]
                S.op("dve", lambda e: e.tensor_single_scalar(xdt[:, :], PS_X[:, 64 * h:64 * h + 64], DTS[:, col:col + 1], ALU.mult),
                     reads=[PS_X, DTS], writes=[xdt])
                S.op("act", lambda e: e.activation(kd[:, :], PSH[:, 0:128], AF.Copy, scale=sg[:, 127:128]), reads=[PSH, sg], writes=[kd])
                lin_head([(ct[:, cs_], ct)], [(kd[:, :], kd)], xdt[:, :], xdt, sm, ECOL[:, col:col + 1], ECOL,
                         ETOT[:, col:col + 1], ETOT, [SST[h]], [SSB[h]], yb_, 64 * h, 64)
            S.op("dve", lambda e: e.tensor_tensor(TMP2[:, :], PS_X[:, 0:128], DROW[:, :], ALU.mult), reads=[PS_X, DROW], writes=[TMP2])
            S.op("pool", lambda e: e.tensor_tensor(yb_[:, :], yb_[:, :], TMP2[:, :], ALU.add), reads=[yb_, TMP2], writes=[yb_])
            S.dma("pool", [(yb_d[c * 128:(c + 1) * 128, :], yb_[:, :])], reads=[yb_], writes=[OUT[1]])

    RIN = [[S.sb(f"rin{b}_{i}", [128, 512], F32) for i in range(2)] for b in range(6)]
    RT = [S.sb(f"rt{i}", [128, 512], F32) for i in range(4)]
    QR = [[S.sb(f"qr{k}_{i}", [128, 512], BF16) for i in range(2)] for k in range(2)]
    KR = [[S.sb(f"kr{k}_{i}", [128, 512], BF16) for i in range(2)] for k in range(2)]
    RVT = [S.sb(f"rvt{i}", [128, 128], BF16) for i in range(3)]
    RST = [S.sb(f"rst{i}", [128, 128], F32) for i in range(2)]
    RSB = [S.sb(f"rsb{i}", [128, 128], BF16) for i in range(2)]
    RC = S.sb("rc", [128, 4], F32)
    for k in range(2):
        S.op("dve", lambda e: e.memset(RST[k][:, :], 0.0), writes=[RST[k]])
        S.op("dve", lambda e: e.memset(RSB[k][:, :], 0.0), writes=[RSB[k]])
    sgR, smR = decay(SMALL[:, 22:23], SMALL, scale=1.0 / 16.0, slot=2)
    S.op("pe", lambda e: e.matmul(PS_X[:, 0:1], CST[:, C_TRI, :], SMALL[:, 22:23], start=True, stop=True), reads=[CST, SMALL], writes=[PS_X])
    S.op("act", lambda e: e.activation(RC[:, 0:1], PS_X[:, 0:1], AF.Exp), reads=[PS_X], writes=[RC])
    S.op("pe", lambda e: e.matmul(PS_X[:, 0:1], CST[:, C_ONE, :], SMALL[:, 22:23], start=True, stop=True), reads=[CST, SMALL], writes=[PS_X])
    S.op("act", lambda e: e.activation(RC[:, 1:2], PS_X[:, 0:1], AF.Exp), reads=[PS_X], writes=[RC])
    S.op("dve", lambda e: e.tensor_single_scalar(RC[:, 2:3], sgR[:, 127:128], 1.0 / 16.0, ALU.mult), reads=[sgR], writes=[RC])
    for ti in range(33):
        c0, n = tile_rng(ti)
        rin = [RIN[b][ti % 2] for b in range(6)]
        for b in range(4):
            S.dma("sp", [(rin[b][:, 0:n], scfm_d[6 + b, :, c0:c0 + n])], writes=[rin[b]])
        S.dma("sp", [(rin[4][:, 0:n], cos_d[:, c0:c0 + n])], writes=[rin[4]])
        S.dma("sp", [(rin[5][:, 0:n], sin_d[:, c0:c0 + n])], writes=[rin[5]])
        for (x0, x1, dst) in ((rin[0], rin[1], QR), (rin[2], rin[3], KR)):
            d0, d1 = dst[0][ti % 2], dst[1][ti % 2]
            S.op("dve", lambda e: e.tensor_tensor(RT[0][:, 0:n], x0[:, 0:n], rin[4][:, 0:n], ALU.mult), reads=[x0, rin[4]], writes=[RT[0]])
            S.op("pool", lambda e: e.tensor_tensor(RT[1][:, 0:n], x1[:, 0:n], rin[5][:, 0:n], ALU.mult), reads=[x1, rin[5]], writes=[RT[1]])
            S.op("dve", lambda e: e.tensor_tensor(d0[:, 0:n], RT[0][:, 0:n], RT[1][:, 0:n], ALU.subtract), reads=[RT[0], RT[1]], writes=[d0])
            S.op("pool", lambda e: e.tensor_tensor(RT[2][:, 0:n], x0[:, 0:n], rin[5][:, 0:n], ALU.mult), reads=[x0, rin[5]], writes=[RT[2]])
            S.op("dve", lambda e: e.tensor_tensor(RT[3][:, 0:n], x1[:, 0:n], rin[4][:, 0:n], ALU.mult), reads=[x1, rin[4]], writes=[RT[3]])
            S.op("dve", lambda e: e.tensor_tensor(d1[:, 0:n], RT[2][:, 0:n], RT[3][:, 0:n], ALU.add), reads=[RT[2], RT[3]], writes=[d1])
        qr = [QR[0][ti % 2], QR[1][ti % 2]]
        kr = [KR[0][ti % 2], KR[1][ti % 2]]
        for sub in range(n // 128):
            c = ti * 4 + sub
            cs_ = slice(sub * 128, (sub + 1) * 128)
            rv = RVT[c % 3]
            S.dma("sp", [(rv[:, :], rv_d[c * 128:(c + 1) * 128, :])], writes=[rv])
            for kk in range(2):
                S.op("pe", lambda e: e.matmul(PS_ST[:, 0:128], kr[kk][:, cs_], qr[kk][:, cs_], start=(kk == 0), stop=(kk == 1)),
                     reads=[kr[kk], qr[kk]], writes=[PS_ST])
            kds = []
            for kk in range(2):
                S.op("pe", lambda e: e.transpose(PSH[:, kk * 128:(kk + 1) * 128], kr[kk][:, cs_], CSTB[:, C_ID, :]), reads=[kr[kk], CSTB], writes=[PSH])
                kd = KD[2 * (c % 2) + kk]
                S.op("act", lambda e: e.activation(kd[:, :], PSH[:, kk * 128:(kk + 1) * 128], AF.Copy, scale=RC[:, 2:3]), reads=[PSH, RC], writes=[kd])
                kds.append((kd[:, :], kd))
            yb_ = YB[c % 2]
            lin_head([(qr[0][:, cs_], qr[0]), (qr[1][:, cs_], qr[1])], kds, rv[:, :], rv, smR, RC[:, 0:1], RC, RC[:, 1:2], RC,
                     RST, RSB, yb_, 0, 128)
            S.dma("pool", [(yc_d[c * 128:(c + 1) * 128, :], yb_[:, :])], reads=[yb_], writes=[OUT[2]])

    S.finish(OUT)
    return nc, S


IN_SIZES = (1024, 1024, 1024, 1024, 2048, 16, 1024, 1024, 1024, 1024, 1024, 1024, 1024, 4096)
IN_OFFS = np.concatenate([[0], np.cumsum(IN_SIZES)]).astype(int)
_CONST_CACHE = {}


def _consts():
    if not _CONST_CACHE:
        _CONST_CACHE["cst"] = host_consts()
        c, s = rope_tables()
        _CONST_CACHE["cos"] = c
        _CONST_CACHE["sin"] = s
    return _CONST_CACHE


def fm(a):
    t = np.ascontiguousarray(a.T)
    return t.reshape(t.shape[0] // 128, 128, t.shape[1])


def colvec(v):
    return np.ascontiguousarray(v.reshape(-1, 128).T)


def prep_H(h, w_in, conv_a, ssd_conv_w, ssd_conv_b, ssd_dt_bias, ssd_a_log, ssd_d, npre):
    cs = _consts()
    hT = fm(h)
    maps = []
    O = IN_OFFS
    for c in range(NCORE):
        g = c // 2
        hh = c // 2
        fmcols = np.concatenate([
            np.arange(O[0] + 128 * c, O[0] + 128 * c + 128),
            np.arange(O[1] + 128 * c, O[1] + 128 * c + 128),
            np.arange(O[2] + 128 * c, O[2] + 128 * c + 128),
            np.arange(O[4] + 128 * c, O[4] + 128 * c + 128),
            np.arange(O[4] + 1024 + 128 * g, O[4] + 1024 + 128 * g + 128),
            np.arange(O[4] + 1536 + 128 * g, O[4] + 1536 + 128 * g + 128),
            np.arange(O[6] + 256 * hh, O[6] + 256 * hh + 256),
            np.arange(O[7] + 256 * hh, O[7] + 256 * hh + 256),
            np.arange(O[10] + 128 * c, O[10] + 128 * c + 128),
            np.arange(O[11] + 128 * c, O[11] + 128 * c + 128),
        ])
        tmcols = np.concatenate([
            np.arange(O[8] + 128 * c, O[8] + 128 * c + 128),
            np.arange(O[12] + 128 * c, O[12] + 128 * c + 128),
            np.arange(O[5] + 2 * c, O[5] + 2 * c + 2),
        ])
        xcols = [np.arange(128 * c, 128 * c + 128), np.arange(1024 + 128 * g, 1024 + 128 * g + 128),
                 np.arange(1536 + 128 * g, 1536 + 128 * g + 128)]
        scw = np.concatenate([ssd_conv_w[:, xc].T for xc in xcols], axis=1)
        scb = np.stack([ssd_conv_b[xc] for xc in xcols], axis=1)
        drow = np.broadcast_to(np.repeat(ssd_d[2 * c:2 * c + 2], 64)[None, :], (128, 128))
        lg = math.log(1.0 - 2.0 ** (-5.0 - hh))
        maps.append({
            "hT": hT,
            "npre": colvec(npre),
            "wfm": np.ascontiguousarray(w_in[:, fmcols].reshape(KC, 128, NFM * 128)),
            "wtm": np.ascontiguousarray(w_in[:, tmcols].reshape(KC, 128, NTM)),
            "cva": np.ascontiguousarray(conv_a[:, 128 * c:128 * c + 128].T),
            "scw": np.ascontiguousarray(scw),
            "scb": np.ascontiguousarray(scb),
            "dtb": np.ascontiguousarray(np.broadcast_to(ssd_dt_bias[None, 2 * c:2 * c + 2], (128, 2))),
            "alog": np.ascontiguousarray(np.broadcast_to(ssd_a_log[None, 2 * c:2 * c + 2], (128, 2))),
            "drow": np.ascontiguousarray(drow),
            "lgam": np.full((128, 1), lg, np.float32),
            "cos": cs["cos"], "sin": cs["sin"], "cst": cs["cst"],
        })
    return maps


def gather_H(res):
    ya = np.concatenate([r["yaT"].T for r in res], axis=1)
    yb = np.concatenate([r["yb"] for r in res], axis=1)
    yc = np.concatenate([r["yc"] for r in res], axis=1)
    yd = np.concatenate([r["ydT"].T for r in res], axis=1)
    return ya, yb, yc, yd


NTH = TOK // 3
NTT = NTH // TT


def build_T():
    nc = bass.Bass("TRN2", target_bir_lowering=False)
    S = Sched(nc)
    ei = lambda n, s, dt=F32: nc.dram_tensor(n, list(s), dt, kind="ExternalInput")
    hT_d = ei("hT", [KC, 128, TOK])
    yin_d = [ei(nm, [KC, 128, TOK]) for nm in ("ya", "yb", "yc", "yd")]
    wzg_d = ei("wzg", [D, 6144])
    wbr_d = ei("wbr", [4, D, D])
    wout_d = ei("wout", [D, D])
    wfi_d = ei("wfi", [D, 2 * DFF])
    wfo_d = ei("wfo", [DFF, D])
    vec_d = ei("vecs", [128, 40])
    cst_d = ei("cst", [128, 7, 128])
    hout_d = nc.dram_tensor("hout", [KC, 128, TOK], F32, kind="ExternalOutput")
    HOUT = Buf(hout_d, "houtd")

    CST = S.sb("cst", [128, 7, 128], F32)
    VEC = S.sb("vec", [128, 40], F32)
    S.dma("sp", [(CST[:, :, :], cst_d[:, :, :])], writes=[CST])
    S.dma("sp", [(VEC[:, :], vec_d[:, :])], writes=[VEC])
    V_PRE, V_SSD, V_POST, V_FPRE, V_FPOST = 0, 8, 16, 24, 32
    H = [S.sb(f"h{k}", [128, NTH], F32) for k in range(KC)]
    HN = [S.sb(f"hn{k}", [128, NTH], BF16) for k in range(KC)]
    BIGF = [S.sb(f"bf{k}", [128, NTH], F32) for k in range(KC)]
    BRF = [S.sb(f"br{k}", [128, NTH], BF16) for k in range(32)]
    BR = [BRF[8 * n:8 * n + 8] for n in range(4)]
    HID = BRF[0:22]
    MG = [S.sb(f"mg{k}", [128, NTH], BF16) for k in range(KC)]
    WS = [S.sb(f"ws{i}", [128, 11, 128], F32) for i in range(2)]
    WB = [S.sb(f"wb{i}", [128, 22, 128], BF16) for i in range(2)]
    SQ = [S.sb(f"sq{i}", [128, TT], F32) for i in range(2)]
    R1 = S.sb("r1", [128, TT], F32)
    RS = [S.sb(f"rs{i}", [128, TT], F32) for i in range(2)]
    STG = [S.sb(f"stg{i}", [128, TT], F32) for i in range(4)]
    SIG = [S.sb(f"sig{i}", [128, TT], F32) for i in range(2)]
    TMPF = [S.sb(f"tmpf{i}", [128, TT], F32) for i in range(2)]
    ACC = [S.sb(f"acc{i}", [128, TT], F32) for i in range(NTT)]
    MEAN = S.sb("mean", [128, TT], F32)
    DD = [S.sb(f"dd{i}", [128, TT], F32) for i in range(2)]
    PSB = [S.ps(f"ps{i}", [128, 512]) for i in range(8)]
    PS_SS = PSB[0]
    ctr = {"w": 0, "ws": 0, "ps": 0, "rs": 0, "sq": 0, "stg": 0, "sig": 0, "tmp": 0}

    def nxt(key, lst):
        i = ctr[key]
        ctr[key] += 1
        return lst[i % len(lst)]

    def load_w(wd2, kc, cb):
        wb = nxt("w", WB)
        v = wd2.rearrange("(k p) n -> p k n", p=128)
        for k0 in range(0, kc, 11):
            k1 = min(kc, k0 + 11)
            ws = nxt("ws", WS)
            S.dma("sp", [(ws[:, 0:k1 - k0, :], v[:, k0:k1, cb * 128:(cb + 1) * 128])], writes=[ws])
            S.op("pool", lambda e: e.tensor_copy(wb[:, k0:k1, :], ws[:, 0:k1 - k0, :]), reads=[ws], writes=[wb])
        return wb

    def tsl(t):
        return slice(t * TT, (t + 1) * TT)

    def gemm(wb, kc, X, t):
        ps = PSB[1 + ctr["ps"] % 6]
        ctr["ps"] += 1
        for k in range(kc):
            S.op("pe", lambda e: e.matmul(ps[:, 0:TT], wb[:, k, :], X[k][:, tsl(t)], start=(k == 0), stop=(k == kc - 1)),
                 reads=[wb, X[k]], writes=[ps])
        return ps

    def rstd_tile(chunks, t, nfeat, sl_fn=None):
        n = len(chunks)
        for i, (b, ap) in enumerate(chunks):
            sq = nxt("sq", SQ)
            S.op("pool", lambda e: e.tensor_tensor(sq[:, :], ap, ap, ALU.mult), reads=[b], writes=[sq])
            S.op("pe", lambda e: e.matmul(PS_SS[:, 0:TT], CST[:, C_ONE, :], sq[:, :], start=(i == 0), stop=(i == n - 1)),
                 reads=[CST, sq], writes=[PS_SS])
        S.op("act", lambda e: e.activation(R1[:, :], PS_SS[:, 0:TT], AF.Sqrt, bias=EPS, scale=1.0 / nfeat), reads=[PS_SS], writes=[R1])
        rs = nxt("rs", RS)
        S.op("dve", lambda e: e.reciprocal(rs[:, :], R1[:, :]), reads=[R1], writes=[rs])
        return rs

    def norm_to_hn(voff):
        for t in range(NTT):
            rs = rstd_tile([(H[k], H[k][:, tsl(t)]) for k in range(KC)], t, D)
            for k in range(KC):
                S.op("dve", lambda e: e.scalar_tensor_tensor(HN[k][:, tsl(t)], H[k][:, tsl(t)], VEC[:, voff + k:voff + k + 1], rs[:, :], ALU.mult, ALU.mult),
                     reads=[H[k], VEC, rs], writes=[HN[k]])

    def add_normed(voff):
        for t in range(NTT):
            rs = rstd_tile([(BIGF[k], BIGF[k][:, tsl(t)]) for k in range(KC)], t, D)
            for k in range(KC):
                tm = nxt("tmp", TMPF)
                S.op("dve", lambda e: e.scalar_tensor_tensor(tm[:, :], BIGF[k][:, tsl(t)], VEC[:, voff + k:voff + k + 1], rs[:, :], ALU.mult, ALU.mult),
                     reads=[BIGF[k], VEC, rs], writes=[tm])
                S.op("pool", lambda e: e.tensor_tensor(H[k][:, tsl(t)], H[k][:, tsl(t)], tm[:, :], ALU.add), reads=[H[k], tm], writes=[H[k]])

    for hf in range(TOK // NTH):
        off = hf * NTH
        for k in range(KC):
            S.dma("sp", [(H[k][:, :], hT_d[k, :, off:off + NTH])], writes=[H[k]])
        norm_to_hn(V_PRE)
        for cb in range(KC):
            wb = load_w(wzg_d, KC, cb)
            for t in range(NTT):
                ps = gemm(wb, KC, HN, t)
                sg = nxt("sig", SIG)
                S.op("act", lambda e: e.activation(sg[:, :], ps[:, 0:TT], AF.Silu), reads=[ps], writes=[sg])
                st = nxt("stg", STG)
                S.dma("sp", [(st[:, :], yin_d[1][cb, :, off + t * TT:off + (t + 1) * TT])], writes=[st])
                S.op("dve", lambda e: e.tensor_tensor(BIGF[cb][:, tsl(t)], st[:, :], sg[:, :], ALU.mult), reads=[st, sg], writes=[BIGF[cb]])
        for t in range(NTT):
            for g in range(4):
                rs = rstd_tile([(BIGF[2 * g + i], BIGF[2 * g + i][:, tsl(t)]) for i in range(2)], t, 256)
                for i in range(2):
                    k = 2 * g + i
                    S.op("dve", lambda e: e.scalar_tensor_tensor(BR[1][k][:, tsl(t)], BIGF[k][:, tsl(t)], VEC[:, V_SSD + k:V_SSD + k + 1], rs[:, :], ALU.mult, ALU.mult),
                         reads=[BIGF[k], VEC, rs], writes=[BR[1][k]])
        for cb in range(KC):
            wb = load_w(wzg_d, KC, KC + cb)
            for t in range(NTT):
                ps = gemm(wb, KC, HN, t)
                S.op("act", lambda e: e.activation(BIGF[cb][:, tsl(t)], ps[:, 0:TT], AF.Silu), reads=[ps], writes=[BIGF[cb]])
        for t in range(NTT):
            for g in range(4):
                ycs = []
                for i in range(2):
                    st = nxt("stg", STG)
                    S.dma("sp", [(st[:, :], yin_d[2][2 * g + i, :, off + t * TT:off + (t + 1) * TT])], writes=[st])
                    ycs.append(st)
                for i in range(2):
                    S.op("pe", lambda e: e.matmul(PS_SS[:, 0:TT], CST[:, C_ONE, :], ycs[i][:, :], start=(i == 0), stop=(i == 1)),
                         reads=[CST, ycs[i]], writes=[PS_SS])
                S.op("act", lambda e: e.activation(MEAN[:, :], PS_SS[:, 0:TT], AF.Copy, scale=1.0 / 256.0), reads=[PS_SS], writes=[MEAN])
                for i in range(2):
                    S.op("dve", lambda e: e.tensor_tensor(DD[i][:, :], ycs[i][:, :], MEAN[:, :], ALU.subtract), reads=[ycs[i], MEAN], writes=[DD[i]])
                rs = rstd_tile([(DD[i], DD[i][:, :]) for i in range(2)], t, 256)
                for i in range(2):
                    k = 2 * g + i
                    tm = nxt("tmp", TMPF)
                    S.op("dve", lambda e: e.tensor_tensor(tm[:, :], DD[i][:, :], rs[:, :], ALU.mult), reads=[DD[i], rs], writes=[tm])
                    S.op("dve", lambda e: e.tensor_tensor(BR[2][k][:, tsl(t)], tm[:, :], BIGF[k][:, tsl(t)], ALU.mult), reads=[tm, BIGF[k]], writes=[BR[2][k]])
        for (bi, yi) in ((0, 0), (3, 3)):
            for cb in range(KC):
                for t in range(NTT):
                    st = nxt("stg", STG)
                    S.dma("sp", [(st[:, :], yin_d[yi][cb, :, off + t * TT:off + (t + 1) * TT])], writes=[st])
                    S.op("pool", lambda e: e.tensor_copy(BR[bi][cb][:, tsl(t)], st[:, :]), reads=[st], writes=[BR[bi][cb]])
        for m in range(KC):
            for n in range(4):
                wbg = load_w(wzg_d, KC, 2 * KC + n * KC + m)
                wbu = load_w(wbr_d[n], KC, m)
                for t in range(NTT):
                    psg = gemm(wbg, KC, HN, t)
                    psu = gemm(wbu, KC, BR[n], t)
                    sg = nxt("sig", SIG)
                    S.op("act", lambda e: e.activation(sg[:, :], psg[:, 0:TT], AF.Sigmoid), reads=[psg], writes=[sg])
                    if n == 0:
                        S.op("dve", lambda e: e.tensor_tensor(ACC[t][:, :], sg[:, :], psu[:, 0:TT], ALU.mult), reads=[sg, psu], writes=[ACC[t]])
                    else:
                        tm = nxt("tmp", TMPF)
                        S.op("dve", lambda e: e.tensor_tensor(tm[:, :], sg[:, :], psu[:, 0:TT], ALU.mult), reads=[sg, psu], writes=[tm])
                        S.op("pool", lambda e: e.tensor_tensor(ACC[t][:, :], ACC[t][:, :], tm[:, :], ALU.add), reads=[ACC[t], tm], writes=[ACC[t]])
            for t in range(NTT):
                S.op("pool", lambda e: e.tensor_copy(MG[m][:, tsl(t)], ACC[t][:, :]), reads=[ACC[t]], writes=[MG[m]])
        for cb in range(KC):
            wb = load_w(wout_d, KC, cb)
            for t in range(NTT):
                ps = gemm(wb, KC, MG, t)
                S.op("act", lambda e: e.activation(BIGF[cb][:, tsl(t)], ps[:, 0:TT], AF.Copy), reads=[ps], writes=[BIGF[cb]])
        add_normed(V_POST)
        norm_to_hn(V_FPRE)
        for j in range(22):
            wbg = load_w(wfi_d, KC, j)
            wbu = load_w(wfi_d, KC, 22 + j)
            for t in range(NTT):
                psg = gemm(wbg, KC, HN, t)
                psu = gemm(wbu, KC, HN, t)
                sg = nxt("sig", SIG)
                S.op("act", lambda e: e.activation(sg[:, :], psg[:, 0:TT], AF.Silu), reads=[psg], writes=[sg])
                S.op("dve", lambda e: e.tensor_tensor(HID[j][:, tsl(t)], sg[:, :], psu[:, 0:TT], ALU.mult), reads=[sg, psu], writes=[HID[j]])
        for cb in range(KC):
            wb = load_w(wfo_d, 22, cb)
            for t in range(NTT):
                ps = gemm(wb, 22, HID, t)
                S.op("act", lambda e: e.activation(BIGF[cb][:, tsl(t)], ps[:, 0:TT], AF.Copy), reads=[ps], writes=[BIGF[cb]])
        add_normed(V_FPOST)
        for k in range(KC):
            S.dma("pool", [(hout_d[k, :, off:off + NTH], H[k][:, :])], reads=[H[k]], writes=[HOUT])
    S.finish([HOUT])
    return nc, S


def prep_T(h, ya, yb, yc, yd, w_in, w_branch, w_out, w_ffn_in, w_ffn_out, npre, ssdn, npost, nfpre, nfpost):
    cs = _consts()
    O = IN_OFFS
    wzg = np.ascontiguousarray(np.concatenate([w_in[:, O[3]:O[3] + 1024], w_in[:, O[9]:O[9] + 1024], w_in[:, O[13]:O[13] + 4096]], axis=1))
    vecs = np.ascontiguousarray(np.concatenate([colvec(v) for v in (npre, ssdn, npost, nfpre, nfpost)], axis=1))
    maps = []
    for c in range(NCORE):
        sl = slice(c * TOK, (c + 1) * TOK)
        maps.append({
            "hT": fm(h[sl]), "ya": fm(ya[sl]), "yb": fm(yb[sl]), "yc": fm(yc[sl]), "yd": fm(yd[sl]),
            "wzg": wzg, "wbr": np.ascontiguousarray(w_branch), "wout": np.ascontiguousarray(w_out),
            "wfi": np.ascontiguousarray(w_ffn_in), "wfo": np.ascontiguousarray(w_ffn_out),
            "vecs": vecs, "cst": cs["cst"],
        })
    return maps


def gather_T(res):
    return np.concatenate([r["hout"].reshape(D, TOK).T for r in res], axis=0)


_PROG = {}


def kernel(x, meta, w_in, conv_a, ssd_conv_w, ssd_conv_b, ssd_dt_bias, ssd_a_log, ssd_d, ssd_norm,
           w_branch, w_out, w_ffn_in, w_ffn_out, norm_mix_pre, norm_mix_post, norm_ffn_pre, norm_ffn_post):
    f = lambda a: np.asarray(a, dtype=np.float32)
    x, meta = f(x), f(meta)
    h = np.concatenate([np.zeros((L - 16 - x.shape[1], D), np.float32), meta, x[0]], axis=0)
    if "H" not in _PROG:
        _PROG["H"] = build_H()[0]
        _PROG["T"] = build_T()[0]
    cores = list(range(NCORE))
    for l in range(2):
        mH = prep_H(h, f(w_in[l]), f(conv_a[l]), f(ssd_conv_w[l]), f(ssd_conv_b[l]), f(ssd_dt_bias[l]),
                    f(ssd_a_log[l]), f(ssd_d[l]), f(norm_mix_pre[l]))
        rH = run_bass_kernel_spmd(_PROG["H"], mH, core_ids=cores)
        ya, yb, yc, yd = gather_H(rH.results)
        mT = prep_T(h, ya, yb, yc, yd, f(w_in[l]), f(w_branch[l]), f(w_out[l]), f(w_ffn_in[l]), f(w_ffn_out[l]),
                    f(norm_mix_pre[l]), f(ssd_norm[l]), f(norm_mix_post[l]), f(norm_ffn_pre[l]), f(norm_ffn_post[l]))
        rT = run_bass_kernel_spmd(_PROG["T"], mT, core_ids=cores)
        h = gather_T(rT.results)
    return np.ascontiguousarray(h[128:][None]).astype(np.float32)
```

```python
import math
import numpy as np
import ml_dtypes
import concourse.bass as bass
import concourse.mybir as mybir
from concourse.bass_utils import run_bass_kernel_spmd

F32 = mybir.dt.float32
BF16 = mybir.dt.bfloat16
AF = mybir.ActivationFunctionType
ALU = mybir.AluOpType
EPOCH = 24000

L = 16512
NCH = 129
D = 1024
KC = 8
EPS = 1e-6
NCORE = 8
TOK = L // NCORE
HALF = TOK // 2
TT = 344
DFF = 2816


class Buf:
    __slots__ = ("t", "name", "w", "r", "dsem", "dcnt", "excl")

    def __init__(self, t, name, excl=False):
        self.t = t
        self.name = name
        self.w = None
        self.r = []
        self.dsem = None
        self.dcnt = 0
        self.excl = excl

    def __getitem__(self, idx):
        return self.t[idx]


class Sched:
    def __init__(self, nc):
        self.nc = nc
        self.engs = {"pe": nc.tensor, "act": nc.scalar, "dve": nc.vector,
                     "pool": nc.gpsimd, "sp": nc.sync}
        self.sems = {k: [] for k in self.engs}
        self.cnt = {k: 0 for k in self.engs}
        self.seen = {k: {} for k in self.engs}
        self.nsem = 0
        self.ninst = 0
        self.dsems = []
        self.sb_off = 16640
        self.nps = 0

    def new_sem(self, name):
        self.nsem += 1
        return self.nc.alloc_semaphore(name=f"{name}_{self.nsem}")

    def sb(self, name, shape, dt):
        nbytes = int(np.prod(shape[1:])) * (4 if dt == F32 else 2)
        nbytes = (nbytes + 63) // 64 * 64
        t = self.nc.alloc_sbuf_tensor_at(f"{name}_{self.ninst}_{self.sb_off}", list(shape), dt, offset=self.sb_off)
        self.sb_off += nbytes
        assert self.sb_off <= 229376, ("sbuf overflow", name, self.sb_off)
        return Buf(t, name)

    def ps(self, name, shape, dt=F32):
        self.nps += 1
        assert self.nps <= 8
        return Buf(self.nc.alloc_psum_tensor(name, list(shape), dt), name, excl=True)

    def _deps(self, reads, writes):
        deps = []
        w2 = list(writes)
        for b in reads:
            if b.excl:
                w2.append(b)
                continue
            if b.w is not None:
                deps.append(b.w)
        for b in w2:
            if b.w is not None:
                deps.append(b.w)
            deps.extend(b.r)
        return deps, w2

    def _wait(self, ek, deps):
        eng = self.engs[ek]
        seen = self.seen[ek]
        best = {}
        for (s, v) in deps:
            if seen.get(s, 0) >= v:
                continue
            if best.get(s, 0) < v:
                best[s] = v
        for s, v in best.items():
            eng.wait_ge(s, v)
            seen[s] = v

    def _record(self, dep, reads, writes):
        for b in writes:
            b.w = dep
            b.r = []
        for b in reads:
            b.r.append(dep)
            if len(b.r) > 8:
                m = {}
                for (s, v) in b.r:
                    if m.get(s, 0) < v:
                        m[s] = v
                b.r = list(m.items())

    def op(self, ek, fn, reads=(), writes=()):
        deps, w2 = self._deps(reads, writes)
        self._wait(ek, deps)
        n = self.cnt[ek]
        ep, off = divmod(n, EPOCH)
        while len(self.sems[ek]) <= ep:
            self.sems[ek].append(self.new_sem(ek))
        sem = self.sems[ek][ep]
        fn(self.engs[ek]).then_inc(sem, 1)
        self.cnt[ek] = n + 1
        dep = (sem, off + 1)
        self._record(dep, [b for b in reads if not b.excl], w2)
        self.ninst += 1
        return dep

    def dma(self, qk, pairs, reads=(), writes=(), sembuf=None):
        deps, w2 = self._deps(reads, writes)
        sb = sembuf if sembuf is not None else (list(reads) + list(writes))[0]
        if sb.dsem is None:
            sb.dsem = self.new_sem("d")
            self.dsems.append(sb)
        if sb.dcnt > 0:
            deps.append((sb.dsem, sb.dcnt))
        self._wait(qk, deps)
        eng = self.engs[qk]
        for (o, i) in pairs:
            eng.dma_start(out=o, in_=i).then_inc(sb.dsem, 16)
            sb.dcnt += 16
            self.ninst += 1
        dep = (sb.dsem, sb.dcnt)
        self._record(dep, [b for b in reads if not b.excl], w2)
        return dep

    def cc(self, kind, in_ap, out_ap, groups, reads=(), writes=()):
        deps, w2 = self._deps(reads, writes)
        sb = list(writes)[0]
        if sb.dsem is None:
            sb.dsem = self.new_sem("c")
            self.dsems.append(sb)
        if sb.dcnt > 0:
            deps.append((sb.dsem, sb.dcnt))
        self._wait("pool", deps)
        self.engs["pool"].collective_compute(kind, ALU.bypass, replica_groups=groups, ins=[in_ap], outs=[out_ap]).then_inc(sb.dsem, 16)
        sb.dcnt += 16
        self.ninst += 1
        dep = (sb.dsem, sb.dcnt)
        self._record(dep, [b for b in reads if not b.excl], w2)
        return dep

    def barrier(self):
        deps = []
        for k in self.engs:
            n = self.cnt[k]
            if n == 0:
                continue
            ep, off = divmod(n - 1, EPOCH)
            deps.append((self.sems[k][ep], off + 1))
        for b in self.dsems:
            deps.append((b.dsem, b.dcnt))
        for k in self.engs:
            self._wait(k, deps)

    def finish(self, bufs):
        deps = []
        for b in bufs:
            if b.w is not None:
                deps.append(b.w)
        self._wait("sp", deps)


C_ID, C_TRI, C_SLT, C_ONE, C_U, C_LM, C_MS = range(7)


def run_threads(gens):
    gens = list(gens)
    while gens:
        for g in list(gens):
            try:
                next(g)
            except StopIteration:
                gens.remove(g)


def host_consts():
    i = np.arange(128)
    c = np.zeros((128, 7, 128), np.float32)
    c[:, C_ID, :] = np.eye(128)
    c[:, C_TRI, :] = (i[:, None] <= i[None, :])
    c[:, C_SLT, :] = (i[:, None] > i[None, :])
    c[:, C_ONE, :] = 1.0
    c[:, C_U, :] = (i[:, None] >= i[None, :])
    c[:, C_LM, :] = (i[:, None] < i[None, :])
    c[:, C_MS, :] = (i[None, :] > i[:, None])
    return c


def rope_tables():
    half = 128
    inv = np.power(np.float32(10000.0), -(np.arange(half, dtype=np.float32) / np.float32(half))).astype(np.float32)
    pos = np.arange(L, dtype=np.float32)
    ang = (pos[None, :] * inv[:, None]).astype(np.float32)
    return np.cos(ang.astype(np.float64)).astype(np.float32), np.sin(ang.astype(np.float64)).astype(np.float32)


NFM = 12
NTM = 258


def build_H():
    nc = bass.Bass("TRN2", target_bir_lowering=False)
    S = Sched(nc)
    ei = lambda n, s, dt=F32: nc.dram_tensor(n, list(s), dt, kind="ExternalInput")
    eo = lambda n, s, dt=F32: nc.dram_tensor(n, list(s), dt, kind="ExternalOutput")
    hT_d = ei("hT", [KC, 128, L])
    npre_d = ei("npre", [128, KC])
    wfm_d = ei("wfm", [KC, 128, NFM * 128])
    wtm_d = ei("wtm", [KC, 128, NTM])
    cva_d = ei("cva", [128, 3])
    scw_d = ei("scw", [128, 12])
    scb_d = ei("scb", [128, 3])
    dtb_d = ei("dtb", [128, 2])
    alog_d = ei("alog", [128, 2])
    drow_d = ei("drow", [128, 128])
    lgam_d = ei("lgam", [128, 1])
    cos_d = ei("cos", [128, L])
    sin_d = ei("sin", [128, L])
    cst_d = ei("cst", [128, 7, 128])
    yaT_d = eo("yaT", [128, L])
    yb_d = eo("yb", [L, 128])
    yc_d = eo("yc", [L, 128])
    ydT_d = eo("ydT", [128, L])
    scfm_d = nc.dram_tensor("scfm", [10, 128, L], F32)
    rv_d = nc.dram_tensor("rvs", [L, 128], BF16)
    OUT = [Buf(yaT_d, "yaTd"), Buf(yb_d, "ybd"), Buf(yc_d, "ycd"), Buf(ydT_d, "ydTd")]
    SCFM = [[Buf(scfm_d, f"scfm{b}_{i}") for i in range(33)] for b in range(10)]
    RVD = [Buf(rv_d, f"rvd{c}") for c in range(NCH)]

    CST = S.sb("cst", [128, 7, 128], F32)
    CSTB = S.sb("cstb", [128, 7, 128], BF16)
    SMALL = S.sb("small", [128, 32], F32)
    DROW = S.sb("drow", [128, 128], F32)
    DT = S.sb("dt", [128, 2 * NCH], F32)
    off_persist = S.sb_off
    QT = [S.sb(f"qt{i}", [128, 512], BF16) for i in range(33)]
    KT = [S.sb(f"kt{i}", [128, 512], BF16) for i in range(33)]
    SV = [S.sb(f"sv{i}", [128, 4, 128], BF16) for i in range(33)]
    off_qkv = S.sb_off
    PSB = [S.ps(f"ps{i}", [128, 512]) for i in range(7)]
    PSH = S.ps("psh", [128, 1024], BF16)
    base_off = S.sb_off

    S.dma("sp", [(CST[:, :, :], cst_d[:, :, :])], writes=[CST])
    S.dma("sp", [(SMALL[:, 0:3], cva_d[:, :]), (SMALL[:, 3:15], scw_d[:, :]), (SMALL[:, 15:18], scb_d[:, :]),
                 (SMALL[:, 18:20], dtb_d[:, :]), (SMALL[:, 20:22], alog_d[:, :]), (SMALL[:, 22:23], lgam_d[:, :]),
                 (SMALL[:, 24:32], npre_d[:, :])], writes=[SMALL])
    S.dma("sp", [(DROW[:, :], drow_d[:, :])], writes=[DROW])
    S.op("dve", lambda e: e.tensor_copy(CSTB[:, :, :], CST[:, :, :]), reads=[CST], writes=[CSTB])

    def tile_rng(i):
        c0 = i * 512
        return c0, min(512, L - c0)

    WFM = S.sb("wfm", [128, KC, NFM * 128], BF16)
    WTM = S.sb("wtm", [128, KC, NTM], BF16)
    WST = [S.sb(f"wst{i}", [128, NFM * 128], F32) for i in range(1)]
    for k in range(KC):
        st = WST[0]
        S.dma("sp", [(st[:, :], wfm_d[k, :, :])], writes=[st])
        S.op("pool", lambda e: e.tensor_copy(WFM[:, k, :], st[:, :]), reads=[st], writes=[WFM])
    for k in range(KC):
        st = WST[0]
        S.dma("sp", [(st[:, 0:NTM], wtm_d[k, :, :])], writes=[st])
        S.op("pool", lambda e: e.tensor_copy(WTM[:, k, :], st[:, 0:NTM]), reads=[st], writes=[WTM])
    HT = [S.sb(f"ht{i}", [128, KC, 512], F32) for i in range(2)]
    HN = [S.sb(f"hn{i}", [128, KC, 512], BF16) for i in range(2)]
    SQ = [S.sb(f"sq{i}", [128, 512], F32) for i in range(2)]
    R1 = S.sb("r1", [128, 512], F32)
    RS = S.sb("rs", [128, 512], F32)
    STG = [S.sb(f"stg{i}", [128, 512], F32) for i in range(3)]
    STV = [S.sb(f"stv{i}", [128, 128], BF16) for i in range(2)]
    stg_i = 0
    SCALE_Q = 1.0 / math.sqrt(128.0)
    for ti in range(33):
        c0, n = tile_rng(ti)
        ht = HT[ti % 2]
        hn = HN[ti % 2]
        S.dma("sp", [(ht[:, k, 0:n], hT_d[k, :, c0:c0 + n]) for k in range(KC)], writes=[ht])
        pss = PSB[0]
        for k in range(KC):
            sq = SQ[k % 2]
            S.op("pool", lambda e: e.tensor_tensor(sq[:, 0:n], ht[:, k, 0:n], ht[:, k, 0:n], ALU.mult), reads=[ht], writes=[sq])
            S.op("pe", lambda e: e.matmul(pss[:, 0:n], CST[:, C_ONE, :], sq[:, 0:n], start=(k == 0), stop=(k == KC - 1)),
                 reads=[CST, sq], writes=[pss])
        S.op("act", lambda e: e.activation(R1[:, 0:n], pss[:, 0:n], AF.Sqrt, bias=EPS, scale=1.0 / D), reads=[pss], writes=[R1])
        S.op("dve", lambda e: e.reciprocal(RS[:, 0:n], R1[:, 0:n]), reads=[R1], writes=[RS])
        for k in range(KC):
            S.op("dve", lambda e: e.scalar_tensor_tensor(hn[:, k, 0:n], ht[:, k, 0:n], SMALL[:, 24 + k:25 + k], RS[:, 0:n], ALU.mult, ALU.mult),
                 reads=[ht, SMALL, RS], writes=[hn])
        for blk in range(NFM):
            ps = PSB[1 + blk % 2]
            for k in range(KC):
                S.op("pe", lambda e: e.matmul(ps[:, 0:n], WFM[:, k, blk * 128:(blk + 1) * 128], hn[:, k, 0:n], start=(k == 0), stop=(k == KC - 1)),
                     reads=[WFM, hn], writes=[ps])
            if blk < 10:
                st = STG[stg_i % 3]
                stg_i += 1
                S.op("act", lambda e: e.activation(st[:, 0:n], ps[:, 0:n], AF.Copy), reads=[ps], writes=[st])
                S.dma("pool", [(scfm_d[blk, :, c0:c0 + n], st[:, 0:n])], reads=[st], writes=[SCFM[blk][ti]])
            elif blk == 10:
                S.op("act", lambda e: e.activation(QT[ti][:, 0:n], ps[:, 0:n], AF.Copy, scale=SCALE_Q), reads=[ps], writes=[QT[ti]])
            else:
                S.op("dve", lambda e: e.tensor_copy(KT[ti][:, 0:n], ps[:, 0:n]), reads=[ps], writes=[KT[ti]])
        for sub in range(n // 128):
            ch = ti * 4 + sub
            ps = PSB[3 + sub % 2]
            for k in range(KC):
                S.op("pe", lambda e: e.matmul(ps[:, 0:NTM], hn[:, k, sub * 128:(sub + 1) * 128], WTM[:, k, :], start=(k == 0), stop=(k == KC - 1)),
                     reads=[hn, WTM], writes=[ps])
            stv = STV[ch % 2]
            S.op("dve", lambda e: e.tensor_copy(stv[:, :], ps[:, 0:128]), reads=[ps], writes=[stv])
            if ch == 0:
                S.op("dve", lambda e: e.memset(stv[0:112, :], 0.0), writes=[stv])
            S.dma("pool", [(rv_d[ch * 128:(ch + 1) * 128, :], stv[:, :])], reads=[stv], writes=[RVD[ch]])
            S.op("act", lambda e: e.activation(SV[ti][:, sub, :], ps[:, 128:256], AF.Copy), reads=[ps], writes=[SV[ti]])
            for hh in range(2):
                S.op("dve", lambda e: e.tensor_copy(DT[:, hh * NCH + ch:hh * NCH + ch + 1], ps[:, 256 + hh:257 + hh]), reads=[ps], writes=[DT])

    S.barrier()
    S.sb_off = off_qkv

    def sb_thread(tiles, z, acc, outp, tg):
        EB = [S.sb(f"eb{tg}{i}", [128, 512], F32) for i in range(2)]
        SPB = [S.sb(f"spb{tg}{i}", [128, 512], BF16) for i in range(2)]
        ECB = [S.sb(f"ecb{tg}{i}", [128, 512], F32) for i in range(2)]
        WB = [S.sb(f"wb{tg}{i}", [128, 512], BF16) for i in range(2)]
        OS = [S.sb(f"os{tg}{i}", [128, 512], F32) for i in range(2)]
        seq = []
        for ti in tiles:
            c0, n = tile_rng(ti)
            i0 = ti * 4
            nb = n // 128
            for j in range(i0 + nb - 1, -1, -1):
                seq.append((ti, j, c0, n, max(0, j - i0) * 128, j == i0 + nb - 1))

        def qk(idx):
            ti, j, c0, n, cs, first = seq[idx]
            kt = KT[j // 4]
            ko = (j % 4) * 128
            S.op("pe", lambda e: e.matmul(z[:, cs:n], kt[:, ko:ko + 128], QT[ti][:, cs:n], start=True, stop=True),
                 reads=[kt, QT[ti]], writes=[z])

        qk(0)
        yield
        nout = 0
        for idx, (ti, j, c0, n, cs, first) in enumerate(seq):
            i0 = ti * 4
            eb, spb, ecb, wb = EB[idx % 2], SPB[idx % 2], ECB[idx % 2], WB[idx % 2]
            S.op("act", lambda e: e.activation(eb[:, cs:n], z[:, cs:n], AF.Exp), reads=[z], writes=[eb])
            if j >= i0:
                S.op("dve", lambda e: e.tensor_tensor(eb[:, cs:cs + 128], eb[:, cs:cs + 128], CST[:, C_MS, :], ALU.mult),
                     reads=[eb, CST], writes=[eb])
            if j == 0:
                S.op("dve", lambda e: e.memset(eb[0:112, cs:n], 0.0), writes=[eb])
            yield
            if idx + 1 < len(seq):
                qk(idx + 1)
            S.op("act", lambda e: e.activation(spb[:, cs:n], eb[:, cs:n], AF.Ln, bias=1.0), reads=[eb], writes=[spb])
            yield
            S.op("pe", lambda e: e.matmul(acc[:, cs:n], CSTB[:, C_U, :], spb[:, cs:n], start=first, stop=False, skip_group_check=True),
                 reads=[CSTB, spb], writes=[acc])
            yield
            S.op("act", lambda e: e.activation(ecb[:, cs:n], acc[:, cs:n], AF.Exp, scale=-1.0), reads=[acc], writes=[ecb])
            yield
            S.op("pe", lambda e: e.matmul(acc[:, cs:n], CSTB[:, C_LM, :], spb[:, cs:n], start=False, stop=False, skip_group_check=True),
                 reads=[CSTB, spb], writes=[acc])
            S.op("dve", lambda e: e.tensor_tensor(wb[:, cs:n], eb[:, cs:n], ecb[:, cs:n], ALU.mult), reads=[eb, ecb], writes=[wb])
            yield
            svb = SV[j // 4]
            S.op("pe", lambda e: e.matmul(outp[:, cs:n], svb[:, j % 4, :], wb[:, cs:n], start=first, stop=(j == 0), skip_group_check=True),
                 reads=[svb, wb], writes=[outp])
            if j == 0:
                os_ = OS[nout % 2]
                nout += 1
                S.op("dve", lambda e: e.tensor_copy(os_[:, 0:n], outp[:, 0:n]), reads=[outp], writes=[os_])
                S.dma("pool", [(ydT_d[:, c0:c0 + n], os_[:, 0:n])], reads=[os_], writes=[OUT[3]])
            yield

    run_threads([sb_thread(list(range(0, 33, 2)), PSB[0], PSB[2], PSB[3], "a"),
                 sb_thread(list(range(1, 33, 2)), PSB[1], PSB[4], PSB[5], "b")])

    S.barrier()
    S.sb_off = off_persist
    CW = 2048
    CIN = [[S.sb(f"cin{b}_{i}", [128, CW], F32) for i in range(2)] for b in range(3)]
    UB = [S.sb(f"ub{i}", [128, CW + 2], F32) for i in range(2)]
    ACCB = S.sb("accb", [128, CW], F32)
    YO = [S.sb(f"yo{i}", [128, CW], F32) for i in range(2)]
    ntile = (L + CW - 1) // CW
    for t in range(ntile):
        c0 = t * CW
        n = min(CW, L - c0)
        cb_, cc_, cx_ = CIN[0][t % 2], CIN[1][t % 2], CIN[2][t % 2]
        S.dma("sp", [(cb_[:, 0:n], scfm_d[0, :, c0:c0 + n])], writes=[cb_])
        S.dma("sp", [(cc_[:, 0:n], scfm_d[1, :, c0:c0 + n])], writes=[cc_])
        S.dma("sp", [(cx_[:, 0:n], scfm_d[2, :, c0:c0 + n])], writes=[cx_])
        u = UB[t % 2]
        S.op("dve", lambda e: e.tensor_tensor(u[:, 2:2 + n], cc_[:, 0:n], cx_[:, 0:n], ALU.mult), reads=[cc_, cx_], writes=[u])
        if t == 0:
            S.op("dve", lambda e: e.memset(u[:, 0:2 + 112], 0.0), writes=[u])
        else:
            up = UB[(t - 1) % 2]
            S.op("pool", lambda e: e.tensor_copy(u[:, 0:2], up[:, CW:CW + 2]), reads=[up], writes=[u])
        S.op("dve", lambda e: e.tensor_single_scalar(ACCB[:, 0:n], u[:, 0:n], SMALL[:, 0:1], ALU.mult), reads=[u, SMALL], writes=[ACCB])
        for i in (1, 2):
            S.op("dve", lambda e: e.scalar_tensor_tensor(ACCB[:, 0:n], u[:, i:i + n], SMALL[:, i:i + 1], ACCB[:, 0:n], ALU.mult, ALU.add),
                 reads=[u, SMALL, ACCB], writes=[ACCB])
        yo = YO[t % 2]
        S.op("pool", lambda e: e.tensor_tensor(yo[:, 0:n], ACCB[:, 0:n], cb_[:, 0:n], ALU.mult), reads=[ACCB, cb_], writes=[yo])
        S.dma("pool", [(yaT_d[:, c0:c0 + n], yo[:, 0:n])], reads=[yo], writes=[OUT[0]])

    S.barrier()
    S.sb_off = off_persist
    RA = [S.sb(f"ra{i}", [128, 128], F32) for i in range(2)]
    SEG = [S.sb(f"seg{i}", [128, 128], F32) for i in range(3)]
    SEGM = [S.sb(f"segm{i}", [128, 128], F32) for i in range(3)]
    PB = [S.sb(f"pb{i}", [128, 128], BF16) for i in range(2)]
    TMPY = [S.sb(f"tmpy{i}", [128, 128], F32) for i in range(2)]
    YB = [S.sb(f"yb{i}", [128, 128], F32) for i in range(2)]
    PS_D, PS_ST, PS_X, PS_YD, PS_YO, PS_S = PSB[0], PSB[1], PSB[2], PSB[3], PSB[4], PSB[5]
    cnt = {"d": 0, "p": 0}

    def decay(a_ap, a_buf, scale=None, slot=None):
        i = cnt["d"]
        cnt["d"] += 1
        ra = RA[i % 2]
        sg = SEG[slot if slot is not None else i % 2]
        sm = SEGM[slot if slot is not None else i % 2]
        S.op("dve", lambda e: e.tensor_single_scalar(ra[:, :], CST[:, C_TRI, :], a_ap, ALU.mult), reads=[CST, a_buf], writes=[ra])
        S.op("pe", lambda e: e.matmul(PS_D[:, 0:128], CST[:, C_SLT, :], ra[:, :], start=True, stop=True), reads=[CST, ra], writes=[PS_D])
        S.op("act", lambda e: e.activation(sg[:, :], PS_D[:, 0:128], AF.Exp), reads=[PS_D], writes=[sg])
        if scale is None:
            S.op("pool", lambda e: e.tensor_tensor(sm[:, :], sg[:, :], CST[:, C_TRI, :], ALU.mult), reads=[sg, CST], writes=[sm])
        else:
            S.op("dve", lambda e: e.scalar_tensor_tensor(sm[:, :], sg[:, :], scale, CST[:, C_TRI, :], ALU.mult, ALU.mult), reads=[sg, CST], writes=[sm])
        return sg, sm

    def lin_head(q_list, kd_list, v_ap, v_buf, sm, ecol_ap, ecol_buf, etot_ap, etot_buf, S_list, Sbf_list, ybuf, y0, dv):
        i = cnt["p"]
        cnt["p"] += 1
        pb = PB[i % 2]
        tm = TMPY[i % 2]
        S.op("dve", lambda e: e.tensor_tensor(pb[:, :], PS_ST[:, 0:128], sm[:, :], ALU.mult), reads=[PS_ST, sm], writes=[pb])
        S.op("pe", lambda e: e.matmul(PS_YD[:, 0:dv], pb[:, :], v_ap, start=True, stop=True), reads=[pb, v_buf], writes=[PS_YD])
        nk = len(q_list)
        for kk in range(nk):
            qa, qb = q_list[kk]
            S.op("pe", lambda e: e.matmul(PS_YO[:, 0:dv], qa, Sbf_list[kk][:, 0:dv], start=(kk == 0), stop=(kk == nk - 1)),
                 reads=[qb, Sbf_list[kk]], writes=[PS_YO])
        S.op("act", lambda e: e.activation(tm[:, 0:dv], PS_YD[:, 0:dv], AF.Copy), reads=[PS_YD], writes=[tm])
        S.op("dve", lambda e: e.scalar_tensor_tensor(ybuf[:, y0:y0 + dv], PS_YO[:, 0:dv], ecol_ap, tm[:, 0:dv], ALU.mult, ALU.add),
             reads=[PS_YO, ecol_buf, tm], writes=[ybuf])
        for kk in range(nk):
            ka, kb = kd_list[kk]
            S.op("pe", lambda e: e.matmul(PS_S[:, 0:dv], ka, v_ap, start=True, stop=True), reads=[kb, v_buf], writes=[PS_S])
            S.op("dve", lambda e: e.scalar_tensor_tensor(S_list[kk][:, 0:dv], S_list[kk][:, 0:dv], etot_ap, PS_S[:, 0:dv], ALU.mult, ALU.add),
                 reads=[S_list[kk], etot_buf, PS_S], writes=[S_list[kk]])
            S.op("pool", lambda e: e.tensor_copy(Sbf_list[kk][:, 0:dv], S_list[kk][:, 0:dv]), reads=[S_list[kk]], writes=[Sbf_list[kk]])

    TMPD = S.sb("tmpd", [128, 2 * NCH], F32)
    DTS = S.sb("dts", [128, 2 * NCH], F32)
    AALL = S.sb("aall", [128, 2 * NCH], F32)
    ECOL = S.sb("ecol", [128, 2 * NCH], F32)
    ETOT = S.sb("etot", [128, 2 * NCH], F32)
    NA = S.sb("na", [128, 2], F32)
    for h in range(2):
        sl = slice(h * NCH, (h + 1) * NCH)
        S.op("act", lambda e: e.activation(TMPD[:, sl], DT[:, sl], AF.Exp, bias=SMALL[:, 18 + h:19 + h]), reads=[DT, SMALL], writes=[TMPD])
        S.op("act", lambda e: e.activation(DTS[:, sl], TMPD[:, sl], AF.Ln, bias=1.0), reads=[TMPD], writes=[DTS])
    S.op("act", lambda e: e.activation(NA[:, :], SMALL[:, 20:22], AF.Exp), reads=[SMALL], writes=[NA])
    S.op("dve", lambda e: e.tensor_single_scalar(NA[:, :], NA[:, :], -1.0, ALU.mult), reads=[NA], writes=[NA])
    for h in range(2):
        sl = slice(h * NCH, (h + 1) * NCH)
        S.op("dve", lambda e: e.tensor_single_scalar(AALL[:, sl], DTS[:, sl], NA[:, h:h + 1], ALU.mult), reads=[DTS, NA], writes=[AALL])
    S.op("pe", lambda e: e.matmul(PS_X[:, 0:2 * NCH], CST[:, C_TRI, :], AALL[:, :], start=True, stop=True), reads=[CST, AALL], writes=[PS_X])
    S.op("act", lambda e: e.activation(ECOL[:, :], PS_X[:, 0:2 * NCH], AF.Exp), reads=[PS_X], writes=[ECOL])
    S.op("pe", lambda e: e.matmul(PS_X[:, 0:2 * NCH], CST[:, C_ONE, :], AALL[:, :], start=True, stop=True), reads=[CST, AALL], writes=[PS_X])
    S.op("act", lambda e: e.activation(ETOT[:, :], PS_X[:, 0:2 * NCH], AF.Exp), reads=[PS_X], writes=[ETOT])

    XR = [[S.sb(f"xr{b}_{i}", [128, 515], F32) for i in range(2)] for b in range(3)]
    ACS = S.sb("acs", [128, 512], F32)
    XS = [S.sb(f"xs{i}", [128, 512], F32) for i in range(2)]
    BTt = [S.sb(f"btt{i}", [128, 512], BF16) for i in range(2)]
    CTt = [S.sb(f"ctt{i}", [128, 512], BF16) for i in range(2)]
    XDT = [S.sb(f"xdt{i}", [128, 64], BF16) for i in range(2)]
    KD = [S.sb(f"kd{i}", [128, 128], BF16) for i in range(4)]
    TMP2 = S.sb("tmp2", [128, 128], F32)
    SST = [S.sb(f"sst{i}", [128, 64], F32) for i in range(2)]
    SSB = [S.sb(f"ssb{i}", [128, 64], BF16) for i in range(2)]
    for h in range(2):
        S.op("dve", lambda e: e.memset(SST[h][:, :], 0.0), writes=[SST[h]])
        S.op("dve", lambda e: e.memset(SSB[h][:, :], 0.0), writes=[SSB[h]])
    for ti in range(33):
        c0, n = tile_rng(ti)
        outs = [XS[ti % 2], BTt[ti % 2], CTt[ti % 2]]
        for b in range(3):
            xr = XR[b][ti % 2]
            if ti == 0:
                S.dma("sp", [(xr[:, 3:3 + n], scfm_d[3 + b, :, 0:n])], writes=[xr])
                S.op("dve", lambda e: e.memset(xr[:, 0:3 + 112], 0.0), writes=[xr])
            else:
                S.dma("sp", [(xr[:, 0:3 + n], scfm_d[3 + b, :, c0 - 3:c0 + n])], writes=[xr])
            w0 = 3 + 4 * b
            S.op("dve", lambda e: e.tensor_single_scalar(ACS[:, 0:n], xr[:, 0:n], SMALL[:, w0:w0 + 1], ALU.mult), reads=[xr, SMALL], writes=[ACS])
            for i in (1, 2, 3):
                S.op("dve", lambda e: e.scalar_tensor_tensor(ACS[:, 0:n], xr[:, i:i + n], SMALL[:, w0 + i:w0 + i + 1], ACS[:, 0:n], ALU.mult, ALU.add),
                     reads=[xr, SMALL, ACS], writes=[ACS])
            S.op("act", lambda e: e.activation(outs[b][:, 0:n], ACS[:, 0:n], AF.Silu, bias=SMALL[:, 15 + b:16 + b]), reads=[ACS, SMALL], writes=[outs[b]])
        xs, bt, ct = outs
        if ti == 0:
            S.op("dve", lambda e: e.memset(xs[:, 0:112], 0.0), writes=[xs])
        for sub in range(n // 128):
            c = ti * 4 + sub
            cs_ = slice(sub * 128, (sub + 1) * 128)
            S.op("pe", lambda e: e.matmul(PS_ST[:, 0:128], bt[:, cs_], ct[:, cs_], start=True, stop=True), reads=[bt, ct], writes=[PS_ST])
            S.op("pe", lambda e: e.transpose(PSH[:, 0:128], bt[:, cs_], CSTB[:, C_ID, :]), reads=[bt, CSTB], writes=[PSH])
            S.op("pe", lambda e: e.transpose(PS_X[:, 0:128], xs[:, cs_], CST[:, C_ID, :]), reads=[xs, CST], writes=[PS_X])
            yb_ = YB[c % 2]
            for h in range(2):
                col = h * NCH + c
                sg, sm = decay(AALL[:, col:col + 1], AALL)
                xdt = XDT[h]
                kd = KD[h]
                S.op("dve", lambda e: e.tensor_single_scalar(xdt[:, :], PS_X[:, 64 * h:64 * h + 64], DTS[:, col:col + 1], ALU.mult),
                     reads=[PS_X, DTS], writes=[xdt])
                S.op("act", lambda e: e.activation(kd[:, :], PSH[:, 0:128], AF.Copy, scale=sg[:, 127:128]), reads=[PSH, sg], writes=[kd])
                lin_head([(ct[:, cs_], ct)], [(kd[:, :], kd)], xdt[:, :], xdt, sm, ECOL[:, col:col + 1], ECOL,
                         ETOT[:, col:col + 1], ETOT, [SST[h]], [SSB[h]], yb_, 64 * h, 64)
            S.op("dve", lambda e: e.tensor_tensor(TMP2[:, :], PS_X[:, 0:128], DROW[:, :], ALU.mult), reads=[PS_X, DROW], writes=[TMP2])
            S.op("pool", lambda e: e.tensor_tensor(yb_[:, :], yb_[:, :], TMP2[:, :], ALU.add), reads=[yb_, TMP2], writes=[yb_])
            S.dma("pool", [(yb_d[c * 128:(c + 1) * 128, :], yb_[:, :])], reads=[yb_], writes=[OUT[1]])

    RIN = [[S.sb(f"rin{b}_{i}", [128, 512], F32) for i in range(2)] for b in range(6)]
    RT = [S.sb(f"rt{i}", [128, 512], F32) for i in range(4)]
    QR = [[S.sb(f"qr{k}_{i}", [128, 512], BF16) for i in range(2)] for k in range(2)]
    KR = [[S.sb(f"kr{k}_{i}", [128, 512], BF16) for i in range(2)] for k in range(2)]
    RVT = [S.sb(f"rvt{i}", [128, 128], BF16) for i in range(3)]
    RST = [S.sb(f"rst{i}", [128, 128], F32) for i in range(2)]
    RSB = [S.sb(f"rsb{i}", [128, 128], BF16) for i in range(2)]
    RC = S.sb("rc", [128, 4], F32)
    for k in range(2):
        S.op("dve", lambda e: e.memset(RST[k][:, :], 0.0), writes=[RST[k]])
        S.op("dve", lambda e: e.memset(RSB[k][:, :], 0.0), writes=[RSB[k]])
    sgR, smR = decay(SMALL[:, 22:23], SMALL, scale=1.0 / 16.0, slot=2)
    S.op("pe", lambda e: e.matmul(PS_X[:, 0:1], CST[:, C_TRI, :], SMALL[:, 22:23], start=True, stop=True), reads=[CST, SMALL], writes=[PS_X])
    S.op("act", lambda e: e.activation(RC[:, 0:1], PS_X[:, 0:1], AF.Exp), reads=[PS_X], writes=[RC])
    S.op("pe", lambda e: e.matmul(PS_X[:, 0:1], CST[:, C_ONE, :], SMALL[:, 22:23], start=True, stop=True), reads=[CST, SMALL], writes=[PS_X])
    S.op("act", lambda e: e.activation(RC[:, 1:2], PS_X[:, 0:1], AF.Exp), reads=[PS_X], writes=[RC])
    S.op("dve", lambda e: e.tensor_single_scalar(RC[:, 2:3], sgR[:, 127:128], 1.0 / 16.0, ALU.mult), reads=[sgR], writes=[RC])
    for ti in range(33):
        c0, n = tile_rng(ti)
        rin = [RIN[b][ti % 2] for b in range(6)]
        for b in range(4):
            S.dma("sp", [(rin[b][:, 0:n], scfm_d[6 + b, :, c0:c0 + n])], writes=[rin[b]])
        S.dma("sp", [(rin[4][:, 0:n], cos_d[:, c0:c0 + n])], writes=[rin[4]])
        S.dma("sp", [(rin[5][:, 0:n], sin_d[:, c0:c0 + n])], writes=[rin[5]])
        for (x0, x1, dst) in ((rin[0], rin[1], QR), (rin[2], rin[3], KR)):
            d0, d1 = dst[0][ti % 2], dst[1][ti % 2]
            S.op("dve", lambda e: e.tensor_tensor(RT[0][:, 0:n], x0[:, 0:n], rin[4][:, 0:n], ALU.mult), reads=[x0, rin[4]], writes=[RT[0]])
            S.op("pool", lambda e: e.tensor_tensor(RT[1][:, 0:n], x1[:, 0:n], rin[5][:, 0:n], ALU.mult), reads=[x1, rin[5]], writes=[RT[1]])
            S.op("dve", lambda e: e.tensor_tensor(d0[:, 0:n], RT[0][:, 0:n], RT[1][:, 0:n], ALU.subtract), reads=[RT[0], RT[1]], writes=[d0])
            S.op("pool", lambda e: e.tensor_tensor(RT[2][:, 0:n], x0[:, 0:n], rin[5][:, 0:n], ALU.mult), reads=[x0, rin[5]], writes=[RT[2]])
            S.op("dve", lambda e: e.tensor_tensor(RT[3][:, 0:n], x1[:, 0:n], rin[4][:, 0:n], ALU.mult), reads=[x1, rin[4]], writes=[RT[3]])
            S.op("dve", lambda e: e.tensor_tensor(d1[:, 0:n], RT[2][:, 0:n], RT[3][:, 0:n], ALU.add), reads=[RT[2], RT[3]], writes=[d1])
        qr = [QR[0][ti % 2], QR[1][ti % 2]]
        kr = [KR[0][ti % 2], KR[1][ti % 2]]
        for sub in range(n // 128):
            c = ti * 4 + sub
            cs_ = slice(sub * 128, (sub + 1) * 128)
            rv = RVT[c % 3]
            S.dma("sp", [(rv[:, :], rv_d[c * 128:(c + 1) * 128, :])], writes=[rv])
            for kk in range(2):
                S.op("pe", lambda e: e.matmul(PS_ST[:, 0:128], kr[kk][:, cs_], qr[kk][:, cs_], start=(kk == 0), stop=(kk == 1)),
                     reads=[kr[kk], qr[kk]], writes=[PS_ST])
            kds = []
            for kk in range(2):
                S.op("pe", lambda e: e.transpose(PSH[:, kk * 128:(kk + 1) * 128], kr[kk][:, cs_], CSTB[:, C_ID, :]), reads=[kr[kk], CSTB], writes=[PSH])
                kd = KD[2 * (c % 2) + kk]
                S.op("act", lambda e: e.activation(kd[:, :], PSH[:, kk * 128:(kk + 1) * 128], AF.Copy, scale=RC[:, 2:3]), reads=[PSH, RC], writes=[kd])
                kds.append((kd[:, :], kd))
            yb_ = YB[c % 2]
            lin_head([(qr[0][:, cs_], qr[0]), (qr[1][:, cs_], qr[1])], kds, rv[:, :], rv, smR, RC[:, 0:1], RC, RC[:, 1:2], RC,
                     RST, RSB, yb_, 0, 128)
            S.dma("pool", [(yc_d[c * 128:(c + 1) * 128, :], yb_[:, :])], reads=[yb_], writes=[OUT[2]])

    S.finish(OUT)
    return nc, S


IN_SIZES = (1024, 1024, 1024, 1024, 2048, 16, 1024, 1024, 1024, 1024, 1024, 1024, 1024, 4096)
IN_OFFS = np.concatenate([[0], np.cumsum(IN_SIZES)]).astype(int)
_CONST_CACHE = {}


def _consts():
    if not _CONST_CACHE:
        _CONST_CACHE["cst"] = host_consts()
        c, s = rope_tables()
        _CONST_CACHE["cos"] = c
        _CONST_CACHE["sin"] = s
    return _CONST_CACHE


def fm(a):
    t = np.ascontiguousarray(a.T)
    return t.reshape(t.shape[0] // 128, 128, t.shape[1])


def colvec(v):
    return np.ascontiguousarray(v.reshape(-1, 128).T)


def prep_H(h, w_in, conv_a, ssd_conv_w, ssd_conv_b, ssd_dt_bias, ssd_a_log, ssd_d, npre):
    cs = _consts()
    hT = fm(h)
    maps = []
    O = IN_OFFS
    for c in range(NCORE):
        g = c // 2
        hh = c // 2
        fmcols = np.concatenate([
            np.arange(O[0] + 128 * c, O[0] + 128 * c + 128),
            np.arange(O[1] + 128 * c, O[1] + 128 * c + 128),
            np.arange(O[2] + 128 * c, O[2] + 128 * c + 128),
            np.arange(O[4] + 128 * c, O[4] + 128 * c + 128),
            np.arange(O[4] + 1024 + 128 * g, O[4] + 1024 + 128 * g + 128),
            np.arange(O[4] + 1536 + 128 * g, O[4] + 1536 + 128 * g + 128),
            np.arange(O[6] + 256 * hh, O[6] + 256 * hh + 256),
            np.arange(O[7] + 256 * hh, O[7] + 256 * hh + 256),
            np.arange(O[10] + 128 * c, O[10] + 128 * c + 128),
            np.arange(O[11] + 128 * c, O[11] + 128 * c + 128),
        ])
        tmcols = np.concatenate([
            np.arange(O[8] + 128 * c, O[8] + 128 * c + 128),
            np.arange(O[12] + 128 * c, O[12] + 128 * c + 128),
            np.arange(O[5] + 2 * c, O[5] + 2 * c + 2),
        ])
        xcols = [np.arange(128 * c, 128 * c + 128), np.arange(1024 + 128 * g, 1024 + 128 * g + 128),
                 np.arange(1536 + 128 * g, 1536 + 128 * g + 128)]
        scw = np.concatenate([ssd_conv_w[:, xc].T for xc in xcols], axis=1)
        scb = np.stack([ssd_conv_b[xc] for xc in xcols], axis=1)
        drow = np.broadcast_to(np.repeat(ssd_d[2 * c:2 * c + 2], 64)[None, :], (128, 128))
        lg = math.log(1.0 - 2.0 ** (-5.0 - hh))
        maps.append({
            "hT": hT,
            "npre": colvec(npre),
            "wfm": np.ascontiguousarray(w_in[:, fmcols].reshape(KC, 128, NFM * 128)),
            "wtm": np.ascontiguousarray(w_in[:, tmcols].reshape(KC, 128, NTM)),
            "cva": np.ascontiguousarray(conv_a[:, 128 * c:128 * c + 128].T),
            "scw": np.ascontiguousarray(scw),
            "scb": np.ascontiguousarray(scb),
            "dtb": np.ascontiguousarray(np.broadcast_to(ssd_dt_bias[None, 2 * c:2 * c + 2], (128, 2))),
            "alog": np.ascontiguousarray(np.broadcast_to(ssd_a_log[None, 2 * c:2 * c + 2], (128, 2))),
            "drow": np.ascontiguousarray(drow),
            "lgam": np.full((128, 1), lg, np.float32),
            "cos": cs["cos"], "sin": cs["sin"], "cst": cs["cst"],
        })
    return maps


def gather_H(res):
    ya = np.concatenate([r["yaT"].T for r in res], axis=1)
    yb = np.concatenate([r["yb"] for r in res], axis=1)
    yc = np.concatenate([r["yc"] for r in res], axis=1)
    yd = np.concatenate([r["ydT"].T for r in res], axis=1)
    return ya, yb, yc, yd


NTH = TOK // 3
NTT = NTH // TT


def build_T():
    nc = bass.Bass("TRN2", target_bir_lowering=False)
    S = Sched(nc)
    ei = lambda n, s, dt=F32: nc.dram_tensor(n, list(s), dt, kind="ExternalInput")
    hT_d = ei("hT", [KC, 128, TOK])
    yin_d = [ei(nm, [KC, 128, TOK]) for nm in ("ya", "yb", "yc", "yd")]
    wzg_d = ei("wzg", [D, 6144])
    wbr_d = ei("wbr", [4, D, D])
    wout_d = ei("wout", [D, D])
    wfi_d = ei("wfi", [D, 2 * DFF])
    wfo_d = ei("wfo", [DFF, D])
    vec_d = ei("vecs", [128, 40])
    cst_d = ei("cst", [128, 7, 128])
    hout_d = nc.dram_tensor("hout", [KC, 128, TOK], F32, kind="ExternalOutput")
    HOUT = Buf(hout_d, "houtd")

    CST = S.sb("cst", [128, 7, 128], F32)
    VEC = S.sb("vec", [128, 40], F32)
    S.dma("sp", [(CST[:, :, :], cst_d[:, :, :])], writes=[CST])
    S.dma("sp", [(VEC[:, :], vec_d[:, :])], writes=[VEC])
    V_PRE, V_SSD, V_POST, V_FPRE, V_FPOST = 0, 8, 16, 24, 32
    H = [S.sb(f"h{k}", [128, NTH], F32) for k in range(KC)]
    HN = [S.sb(f"hn{k}", [128, NTH], BF16) for k in range(KC)]
    BIGF = [S.sb(f"bf{k}", [128, NTH], F32) for k in range(KC)]
    BRF = [S.sb(f"br{k}", [128, NTH], BF16) for k in range(32)]
    BR = [BRF[8 * n:8 * n + 8] for n in range(4)]
    HID = BRF[0:22]
    MG = [S.sb(f"mg{k}", [128, NTH], BF16) for k in range(KC)]
    WS = [S.sb(f"ws{i}", [128, 11, 128], F32) for i in range(2)]
    WB = [S.sb(f"wb{i}", [128, 22, 128], BF16) for i in range(2)]
    SQ = [S.sb(f"sq{i}", [128, TT], F32) for i in range(2)]
    R1 = S.sb("r1", [128, TT], F32)
    RS = [S.sb(f"rs{i}", [128, TT], F32) for i in range(2)]
    STG = [S.sb(f"stg{i}", [128, TT], F32) for i in range(4)]
    SIG = [S.sb(f"sig{i}", [128, TT], F32) for i in range(2)]
    TMPF = [S.sb(f"tmpf{i}", [128, TT], F32) for i in range(2)]
    ACC = [S.sb(f"acc{i}", [128, TT], F32) for i in range(NTT)]
    MEAN = S.sb("mean", [128, TT], F32)
    DD = [S.sb(f"dd{i}", [128, TT], F32) for i in range(2)]
    PSB = [S.ps(f"ps{i}", [128, 512]) for i in range(8)]
    PS_SS = PSB[0]
    ctr = {"w": 0, "ws": 0, "ps": 0, "rs": 0, "sq": 0, "stg": 0, "sig": 0, "tmp": 0}

    def nxt(key, lst):
        i = ctr[key]
        ctr[key] += 1
        return lst[i % len(lst)]

    def load_w(wd2, kc, cb):
        wb = nxt("w", WB)
        v = wd2.rearrange("(k p) n -> p k n", p=128)
        for k0 in range(0, kc, 11):
            k1 = min(kc, k0 + 11)
            ws = nxt("ws", WS)
            S.dma("sp", [(ws[:, 0:k1 - k0, :], v[:, k0:k1, cb * 128:(cb + 1) * 128])], writes=[ws])
            S.op("pool", lambda e: e.tensor_copy(wb[:, k0:k1, :], ws[:, 0:k1 - k0, :]), reads=[ws], writes=[wb])
        return wb

    def tsl(t):
        return slice(t * TT, (t + 1) * TT)

    def gemm(wb, kc, X, t):
        ps = PSB[1 + ctr["ps"] % 6]
        ctr["ps"] += 1
        for k in range(kc):
            S.op("pe", lambda e: e.matmul(ps[:, 0:TT], wb[:, k, :], X[k][:, tsl(t)], start=(k == 0), stop=(k == kc - 1)),
                 reads=[wb, X[k]], writes=[ps])
        return ps

    def rstd_tile(chunks, t, nfeat, sl_fn=None):
        n = len(chunks)
        for i, (b, ap) in enumerate(chunks):
            sq = nxt("sq", SQ)
            S.op("pool", lambda e: e.tensor_tensor(sq[:, :], ap, ap, ALU.mult), reads=[b], writes=[sq])
            S.op("pe", lambda e: e.matmul(PS_SS[:, 0:TT], CST[:, C_ONE, :], sq[:, :], start=(i == 0), stop=(i == n - 1)),
                 reads=[CST, sq], writes=[PS_SS])
        S.op("act", lambda e: e.activation(R1[:, :], PS_SS[:, 0:TT], AF.Sqrt, bias=EPS, scale=1.0 / nfeat), reads=[PS_SS], writes=[R1])
        rs = nxt("rs", RS)
        S.op("dve", lambda e: e.reciprocal(rs[:, :], R1[:, :]), reads=[R1], writes=[rs])
        return rs

    def norm_to_hn(voff):
        for t in range(NTT):
            rs = rstd_tile([(H[k], H[k][:, tsl(t)]) for k in range(KC)], t, D)
            for k in range(KC):
                S.op("dve", lambda e: e.scalar_tensor_tensor(HN[k][:, tsl(t)], H[k][:, tsl(t)], VEC[:, voff + k:voff + k + 1], rs[:, :], ALU.mult, ALU.mult),
                     reads=[H[k], VEC, rs], writes=[HN[k]])

    def add_normed(voff):
        for t in range(NTT):
            rs = rstd_tile([(BIGF[k], BIGF[k][:, tsl(t)]) for k in range(KC)], t, D)
            for k in range(KC):
                tm = nxt("tmp", TMPF)
                S.op("dve", lambda e: e.scalar_tensor_tensor(tm[:, :], BIGF[k][:, tsl(t)], VEC[:, voff + k:voff + k + 1], rs[:, :], ALU.mult, ALU.mult),
                     reads=[BIGF[k], VEC, rs], writes=[tm])
                S.op("pool", lambda e: e.tensor_tensor(H[k][:, tsl(t)], H[k][:, tsl(t)], tm[:, :], ALU.add), reads=[H[k], tm], writes=[H[k]])

    for hf in range(TOK // NTH):
        off = hf * NTH
        for k in range(KC):
            S.dma("sp", [(H[k][:, :], hT_d[k, :, off:off + NTH])], writes=[H[k]])
        norm_to_hn(V_PRE)
        for cb in range(KC):
            wb = load_w(wzg_d, KC, cb)
            for t in range(NTT):
                ps = gemm(wb, KC, HN, t)
                sg = nxt("sig", SIG)
                S.op("act", lambda e: e.activation(sg[:, :], ps[:, 0:TT], AF.Silu), reads=[ps], writes=[sg])
                st = nxt("stg", STG)
                S.dma("sp", [(st[:, :], yin_d[1][cb, :, off + t * TT:off + (t + 1) * TT])], writes=[st])
                S.op("dve", lambda e: e.tensor_tensor(BIGF[cb][:, tsl(t)], st[:, :], sg[:, :], ALU.mult), reads=[st, sg], writes=[BIGF[cb]])
        for t in range(NTT):
            for g in range(4):
                rs = rstd_tile([(BIGF[2 * g + i], BIGF[2 * g + i][:, tsl(t)]) for i in range(2)], t, 256)
                for i in range(2):
                    k = 2 * g + i
                    S.op("dve", lambda e: e.scalar_tensor_tensor(BR[1][k][:, tsl(t)], BIGF[k][:, tsl(t)], VEC[:, V_SSD + k:V_SSD + k + 1], rs[:, :], ALU.mult, ALU.mult),
                         reads=[BIGF[k], VEC, rs], writes=[BR[1][k]])
        for cb in range(KC):
            wb = load_w(wzg_d, KC, KC + cb)
            for t in range(NTT):
                ps = gemm(wb, KC, HN, t)
                S.op("act", lambda e: e.activation(BIGF[cb][:, tsl(t)], ps[:, 0:TT], AF.Silu), reads=[ps], writes=[BIGF[cb]])
        for t in range(NTT):
            for g in range(4):
                ycs = []
                for i in range(2):
                    st = nxt("stg", STG)
                    S.dma("sp", [(st[:, :], yin_d[2][2 * g + i, :, off + t * TT:off + (t + 1) * TT])], writes=[st])
                    ycs.append(st)
                for i in range(2):
                    S.op("pe", lambda e: e.matmul(PS_SS[:, 0:TT], CST[:, C_ONE, :], ycs[i][:, :], start=(i == 0), stop=(i == 1)),
                         reads=[CST, ycs[i]], writes=[PS_SS])
                S.op("act", lambda e: e.activation(MEAN[:, :], PS_SS[:, 0:TT], AF.Copy, scale=1.0 / 256.0), reads=[PS_SS], writes=[MEAN])
                for i in range(2):
                    S.op("dve", lambda e: e.tensor_tensor(DD[i][:, :], ycs[i][:, :], MEAN[:, :], ALU.subtract), reads=[ycs[i], MEAN], writes=[DD[i]])
                rs = rstd_tile([(DD[i], DD[i][:, :]) for i in range(2)], t, 256)
                for i in range(2):
                    k = 2 * g + i
                    tm = nxt("tmp", TMPF)
                    S.op("dve", lambda e: e.tensor_tensor(tm[:, :], DD[i][:, :], rs[:, :], ALU.mult), reads=[DD[i], rs], writes=[tm])
                    S.op("dve", lambda e: e.tensor_tensor(BR[2][k][:, tsl(t)], tm[:, :], BIGF[k][:, tsl(t)], ALU.mult), reads=[tm, BIGF[k]], writes=[BR[2][k]])
        for (bi, yi) in ((0, 0), (3, 3)):
            for cb in range(KC):
                for t in range(NTT):
                    st = nxt("stg", STG)
                    S.dma("sp", [(st[:, :], yin_d[yi][cb, :, off + t * TT:off + (t + 1) * TT])], writes=[st])
                    S.op("pool", lambda e: e.tensor_copy(BR[bi][cb][:, tsl(t)], st[:, :]), reads=[st], writes=[BR[bi][cb]])
        for m in range(KC):
            for n in range(4):
                wbg = load_w(wzg_d, KC, 2 * KC + n * KC + m)
                wbu = load_w(wbr_d[n], KC, m)
                for t in range(NTT):
                    psg = gemm(wbg, KC, HN, t)
                    psu = gemm(wbu, KC, BR[n], t)
                    sg = nxt("sig", SIG)
                    S.op("act", lambda e: e.activation(sg[:, :], psg[:, 0:TT], AF.Sigmoid), reads=[psg], writes=[sg])
                    if n == 0:
                        S.op("dve", lambda e: e.tensor_tensor(ACC[t][:, :], sg[:, :], psu[:, 0:TT], ALU.mult), reads=[sg, psu], writes=[ACC[t]])
                    else:
                        tm = nxt("tmp", TMPF)
                        S.op("dve", lambda e: e.tensor_tensor(tm[:, :], sg[:, :], psu[:, 0:TT], ALU.mult), reads=[sg, psu], writes=[tm])
                        S.op("pool", lambda e: e.tensor_tensor(ACC[t][:, :], ACC[t][:, :], tm[:, :], ALU.add), reads=[ACC[t], tm], writes=[ACC[t]])
            for t in range(NTT):
                S.op("pool", lambda e: e.tensor_copy(MG[m][:, tsl(t)], ACC[t][:, :]), reads=[ACC[t]], writes=[MG[m]])
        for cb in range(KC):
            wb = load_w(wout_d, KC, cb)
            for t in range(NTT):
                ps = gemm(wb, KC, MG, t)
                S.op("act", lambda e: e.activation(BIGF[cb][:, tsl(t)], ps[:, 0:TT], AF.Copy), reads=[ps], writes=[BIGF[cb]])
        add_normed(V_POST)
        norm_to_hn(V_FPRE)
        for j in range(22):
            wbg = load_w(wfi_d, KC, j)
            wbu = load_w(wfi_d, KC, 22 + j)
            for t in range(NTT):
                psg = gemm(wbg, KC, HN, t)
                psu = gemm(wbu, KC, HN, t)
                sg = nxt("sig", SIG)
                S.op("act", lambda e: e.activation(sg[:, :], psg[:, 0:TT], AF.Silu), reads=[psg], writes=[sg])
                S.op("dve", lambda e: e.tensor_tensor(HID[j][:, tsl(t)], sg[:, :], psu[:, 0:TT], ALU.mult), reads=[sg, psu], writes=[HID[j]])
        for cb in range(KC):
            wb = load_w(wfo_d, 22, cb)
            for t in range(NTT):
                ps = gemm(wb, 22, HID, t)
                S.op("act", lambda e: e.activation(BIGF[cb][:, tsl(t)], ps[:, 0:TT], AF.Copy), reads=[ps], writes=[BIGF[cb]])
        add_normed(V_FPOST)
        for k in range(KC):
            S.dma("pool", [(hout_d[k, :, off:off + NTH], H[k][:, :])], reads=[H[k]], writes=[HOUT])
    S.finish([HOUT])
    return nc, S


def prep_T(h, ya, yb, yc, yd, w_in, w_branch, w_out, w_ffn_in, w_ffn_out, npre, ssdn, npost, nfpre, nfpost):
    cs = _consts()
    O = IN_OFFS
    wzg = np.ascontiguousarray(np.concatenate([w_in[:, O[3]:O[3] + 1024], w_in[:, O[9]:O[9] + 1024], w_in[:, O[13]:O[13] + 4096]], axis=1))
    vecs = np.ascontiguousarray(np.concatenate([colvec(v) for v in (npre, ssdn, npost, nfpre, nfpost)], axis=1))
    maps = []
    for c in range(NCORE):
        sl = slice(c * TOK, (c + 1) * TOK)
        maps.append({
            "hT": fm(h[sl]), "ya": fm(ya[sl]), "yb": fm(yb[sl]), "yc": fm(yc[sl]), "yd": fm(yd[sl]),
            "wzg": wzg, "wbr": np.ascontiguousarray(w_branch), "wout": np.ascontiguousarray(w_out),
            "wfi": np.ascontiguousarray(w_ffn_in), "wfo": np.ascontiguousarray(w_ffn_out),
            "vecs": vecs, "cst": cs["cst"],
        })
    return maps


def gather_T(res):
    return np.concatenate([r["hout"].reshape(D, TOK).T for r in res], axis=0)


_PROG = {}


def kernel(x, meta, w_in, conv_a, ssd_conv_w, ssd_conv_b, ssd_dt_bias, ssd_a_log, ssd_d, ssd_norm,
           w_branch, w_out, w_ffn_in, w_ffn_out, norm_mix_pre, norm_mix_post, norm_ffn_pre, norm_ffn_post):
    f = lambda a: np.asarray(a, dtype=np.float32)
    x, meta = f(x), f(meta)
    h = np.concatenate([np.zeros((L - 16 - x.shape[1], D), np.float32), meta, x[0]], axis=0)
    if "H" not in _PROG:
        _PROG["H"] = build_H()[0]
        _PROG["T"] = build_T()[0]
    cores = list(range(NCORE))
    for l in range(2):
        mH = prep_H(h, f(w_in[l]), f(conv_a[l]), f(ssd_conv_w[l]), f(ssd_conv_b[l]), f(ssd_dt_bias[l]),
                    f(ssd_a_log[l]), f(ssd_d[l]), f(norm_mix_pre[l]))
        rH = run_bass_kernel_spmd(_PROG["H"], mH, core_ids=cores)
        ya, yb, yc, yd = gather_H(rH.results)
        mT = prep_T(h, ya, yb, yc, yd, f(w_in[l]), f(w_branch[l]), f(w_out[l]), f(w_ffn_in[l]), f(w_ffn_out[l]),
                    f(norm_mix_pre[l]), f(ssd_norm[l]), f(norm_mix_post[l]), f(norm_ffn_pre[l]), f(norm_ffn_post[l]))
        rT = run_bass_kernel_spmd(_PROG["T"], mT, core_ids=cores)
        h = gather_T(rT.results)
    return np.ascontiguousarray(h[128:][None]).astype(np.float32)
```

```python
import math
import numpy as np
import ml_dtypes
import concourse.bass as bass
import concourse.mybir as mybir
from concourse.bass_utils import run_bass_kernel_spmd

F32 = mybir.dt.float32
BF16 = mybir.dt.bfloat16
AF = mybir.ActivationFunctionType
ALU = mybir.AluOpType
EPOCH = 24000

L = 16512
NCH = 129
D = 1024
KC = 8
EPS = 1e-6
NCORE = 8
TOK = L // NCORE
HALF = TOK // 2
TT = 344
DFF = 2816


class Buf:
    __slots__ = ("t", "name", "w", "r", "dsem", "dcnt", "excl")

    def __init__(self, t, name, excl=False):
        self.t = t
        self.name = name
        self.w = None
        self.r = []
        self.dsem = None
        self.dcnt = 0
        self.excl = excl

    def __getitem__(self, idx):
        return self.t[idx]


class Sched:
    def __init__(self, nc):
        self.nc = nc
        self.engs = {"pe": nc.tensor, "act": nc.scalar, "dve": nc.vector,
                     "pool": nc.gpsimd, "sp": nc.sync}
        self.sems = {k: [] for k in self.engs}
        self.cnt = {k: 0 for k in self.engs}
        self.seen = {k: {} for k in self.engs}
        self.nsem = 0
        self.ninst = 0
        self.dsems = []
        self.sb_off = 16640
        self.nps = 0

    def new_sem(self, name):
        self.nsem += 1
        return self.nc.alloc_semaphore(name=f"{name}_{self.nsem}")

    def sb(self, name, shape, dt):
        nbytes = int(np.prod(shape[1:])) * (4 if dt == F32 else 2)
        nbytes = (nbytes + 63) // 64 * 64
        t = self.nc.alloc_sbuf_tensor_at(f"{name}_{self.ninst}_{self.sb_off}", list(shape), dt, offset=self.sb_off)
        self.sb_off += nbytes
        assert self.sb_off <= 229376, ("sbuf overflow", name, self.sb_off)
        return Buf(t, name)

    def ps(self, name, shape, dt=F32):
        self.nps += 1
        assert self.nps <= 8
        return Buf(self.nc.alloc_psum_tensor(name, list(shape), dt), name, excl=True)

    def _deps(self, reads, writes):
        deps = []
        w2 = list(writes)
        for b in reads:
            if b.excl:
                w2.append(b)
                continue
            if b.w is not None:
                deps.append(b.w)
        for b in w2:
            if b.w is not None:
                deps.append(b.w)
            deps.extend(b.r)
        return deps, w2

    def _wait(self, ek, deps):
        eng = self.engs[ek]
        seen = self.seen[ek]
        best = {}
        for (s, v) in deps:
            if seen.get(s, 0) >= v:
                continue
            if best.get(s, 0) < v:
                best[s] = v
        for s, v in best.items():
            eng.wait_ge(s, v)
            seen[s] = v

    def _record(self, dep, reads, writes):
        for b in writes:
            b.w = dep
            b.r = []
        for b in reads:
            b.r.append(dep)
            if len(b.r) > 8:
                m = {}
                for (s, v) in b.r:
                    if m.get(s, 0) < v:
                        m[s] = v
                b.r = list(m.items())

    def op(self, ek, fn, reads=(), writes=()):
        deps, w2 = self._deps(reads, writes)
        self._wait(ek, deps)
        n = self.cnt[ek]
        ep, off = divmod(n, EPOCH)
        while len(self.sems[ek]) <= ep:
            self.sems[ek].append(self.new_sem(ek))
        sem = self.sems[ek][ep]
        fn(self.engs[ek]).then_inc(sem, 1)
        self.cnt[ek] = n + 1
        dep = (sem, off + 1)
        self._record(dep, [b for b in reads if not b.excl], w2)
        self.ninst += 1
        return dep

    def dma(self, qk, pairs, reads=(), writes=(), sembuf=None):
        deps, w2 = self._deps(reads, writes)
        sb = sembuf if sembuf is not None else (list(reads) + list(writes))[0]
        if sb.dsem is None:
            sb.dsem = self.new_sem("d")
            self.dsems.append(sb)
        if sb.dcnt > 0:
            deps.append((sb.dsem, sb.dcnt))
        self._wait(qk, deps)
        eng = self.engs[qk]
        for (o, i) in pairs:
            eng.dma_start(out=o, in_=i).then_inc(sb.dsem, 16)
            sb.dcnt += 16
            self.ninst += 1
        dep = (sb.dsem, sb.dcnt)
        self._record(dep, [b for b in reads if not b.excl], w2)
        return dep

    def cc(self, kind, in_ap, out_ap, groups, reads=(), writes=()):
        deps, w2 = self._deps(reads, writes)
        sb = list(writes)[0]
        if sb.dsem is None:
            sb.dsem = self.new_sem("c")
            self.dsems.append(sb)
        if sb.dcnt > 0:
            deps.append((sb.dsem, sb.dcnt))
        self._wait("pool", deps)
        self.engs["pool"].collective_compute(kind, ALU.bypass, replica_groups=groups, ins=[in_ap], outs=[out_ap]).then_inc(sb.dsem, 16)
        sb.dcnt += 16
        self.ninst += 1
        dep = (sb.dsem, sb.dcnt)
        self._record(dep, [b for b in reads if not b.excl], w2)
        return dep

    def barrier(self):
        deps = []
        for k in self.engs:
            n = self.cnt[k]
            if n == 0:
                continue
            ep, off = divmod(n - 1, EPOCH)
            deps.append((self.sems[k][ep], off + 1))
        for b in self.dsems:
            deps.append((b.dsem, b.dcnt))
        for k in self.engs:
            self._wait(k, deps)

    def finish(self, bufs):
        deps = []
        for b in bufs:
            if b.w is not None:
                deps.append(b.w)
        self._wait("sp", deps)


C_ID, C_TRI, C_SLT, C_ONE, C_U, C_LM, C_MS = range(7)


def run_threads(gens):
    gens = list(gens)
    while gens:
        for g in list(gens):
            try:
                next(g)
            except StopIteration:
                gens.remove(g)


def host_consts():
    i = np.arange(128)
    c = np.zeros((128, 7, 128), np.float32)
    c[:, C_ID, :] = np.eye(128)
    c[:, C_TRI, :] = (i[:, None] <= i[None, :])
    c[:, C_SLT, :] = (i[:, None] > i[None, :])
    c[:, C_ONE, :] = 1.0
    c[:, C_U, :] = (i[:, None] >= i[None, :])
    c[:, C_LM, :] = (i[:, None] < i[None, :])
    c[:, C_MS, :] = (i[None, :] > i[:, None])
    return c


def rope_tables():
    half = 128
    inv = np.power(np.float32(10000.0), -(np.arange(half, dtype=np.float32) / np.float32(half))).astype(np.float32)
    pos = np.arange(L, dtype=np.float32)
    ang = (pos[None, :] * inv[:, None]).astype(np.float32)
    return np.cos(ang.astype(np.float64)).astype(np.float32), np.sin(ang.astype(np.float64)).astype(np.float32)


NFM = 12
NTM = 258


def build_H():
    nc = bass.Bass("TRN2", target_bir_lowering=False)
    S = Sched(nc)
    ei = lambda n, s, dt=F32: nc.dram_tensor(n, list(s), dt, kind="ExternalInput")
    eo = lambda n, s, dt=F32: nc.dram_tensor(n, list(s), dt, kind="ExternalOutput")
    hT_d = ei("hT", [KC, 128, L])
    npre_d = ei("npre", [128, KC])
    wfm_d = ei("wfm", [KC, 128, NFM * 128])
    wtm_d = ei("wtm", [KC, 128, NTM])
    cva_d = ei("cva", [128, 3])
    scw_d = ei("scw", [128, 12])
    scb_d = ei("scb", [128, 3])
    dtb_d = ei("dtb", [128, 2])
    alog_d = ei("alog", [128, 2])
    drow_d = ei("drow", [128, 128])
    lgam_d = ei("lgam", [128, 1])
    cos_d = ei("cos", [128, L])
    sin_d = ei("sin", [128, L])
    cst_d = ei("cst", [128, 7, 128])
    yaT_d = eo("yaT", [128, L])
    yb_d = eo("yb", [L, 128])
    yc_d = eo("yc", [L, 128])
    ydT_d = eo("ydT", [128, L])
    scfm_d = nc.dram_tensor("scfm", [10, 128, L], F32)
    rv_d = nc.dram_tensor("rvs", [L, 128], BF16)
    OUT = [Buf(yaT_d, "yaTd"), Buf(yb_d, "ybd"), Buf(yc_d, "ycd"), Buf(ydT_d, "ydTd")]
    SCFM = [[Buf(scfm_d, f"scfm{b}_{i}") for i in range(33)] for b in range(10)]
    RVD = [Buf(rv_d, f"rvd{c}") for c in range(NCH)]

    CST = S.sb("cst", [128, 7, 128], F32)
    CSTB = S.sb("cstb", [128, 7, 128], BF16)
    SMALL = S.sb("small", [128, 32], F32)
    DROW = S.sb("drow", [128, 128], F32)
    DT = S.sb("dt", [128, 2 * NCH], F32)
    off_persist = S.sb_off
    QT = [S.sb(f"qt{i}", [128, 512], BF16) for i in range(33)]
    KT = [S.sb(f"kt{i}", [128, 512], BF16) for i in range(33)]
    SV = [S.sb(f"sv{i}", [128, 4, 128], BF16) for i in range(33)]
    off_qkv = S.sb_off
    PSB = [S.ps(f"ps{i}", [128, 512]) for i in range(7)]
    PSH = S.ps("psh", [128, 1024], BF16)
    base_off = S.sb_off

    S.dma("sp", [(CST[:, :, :], cst_d[:, :, :])], writes=[CST])
    S.dma("sp", [(SMALL[:, 0:3], cva_d[:, :]), (SMALL[:, 3:15], scw_d[:, :]), (SMALL[:, 15:18], scb_d[:, :]),
                 (SMALL[:, 18:20], dtb_d[:, :]), (SMALL[:, 20:22], alog_d[:, :]), (SMALL[:, 22:23], lgam_d[:, :]),
                 (SMALL[:, 24:32], npre_d[:, :])], writes=[SMALL])
    S.dma("sp", [(DROW[:, :], drow_d[:, :])], writes=[DROW])
    S.op("dve", lambda e: e.tensor_copy(CSTB[:, :, :], CST[:, :, :]), reads=[CST], writes=[CSTB])

    def tile_rng(i):
        c0 = i * 512
        return c0, min(512, L - c0)

    WFM = S.sb("wfm", [128, KC, NFM * 128], BF16)
    WTM = S.sb("wtm", [128, KC, NTM], BF16)
    WST = [S.sb(f"wst{i}", [128, NFM * 128], F32) for i in range(1)]
    for k in range(KC):
        st = WST[0]
        S.dma("sp", [(st[:, :], wfm_d[k, :, :])], writes=[st])
        S.op("pool", lambda e: e.tensor_copy(WFM[:, k, :], st[:, :]), reads=[st], writes=[WFM])
    for k in range(KC):
        st = WST[0]
        S.dma("sp", [(st[:, 0:NTM], wtm_d[k, :, :])], writes=[st])
        S.op("pool", lambda e: e.tensor_copy(WTM[:, k, :], st[:, 0:NTM]), reads=[st], writes=[WTM])
    HT = [S.sb(f"ht{i}", [128, KC, 512], F32) for i in range(2)]
    HN = [S.sb(f"hn{i}", [128, KC, 512], BF16) for i in range(2)]
    SQ = [S.sb(f"sq{i}", [128, 512], F32) for i in range(2)]
    R1 = S.sb("r1", [128, 512], F32)
    RS = S.sb("rs", [128, 512], F32)
    STG = [S.sb(f"stg{i}", [128, 512], F32) for i in range(3)]
    STV = [S.sb(f"stv{i}", [128, 128], BF16) for i in range(2)]
    stg_i = 0
    SCALE_Q = 1.0 / math.sqrt(128.0)
    for ti in range(33):
        c0, n = tile_rng(ti)
        ht = HT[ti % 2]
        hn = HN[ti % 2]
        S.dma("sp", [(ht[:, k, 0:n], hT_d[k, :, c0:c0 + n]) for k in range(KC)], writes=[ht])
        pss = PSB[0]
        for k in range(KC):
            sq = SQ[k % 2]
            S.op("pool", lambda e: e.tensor_tensor(sq[:, 0:n], ht[:, k, 0:n], ht[:, k, 0:n], ALU.mult), reads=[ht], writes=[sq])
            S.op("pe", lambda e: e.matmul(pss[:, 0:n], CST[:, C_ONE, :], sq[:, 0:n], start=(k == 0), stop=(k == KC - 1)),
                 reads=[CST, sq], writes=[pss])
        S.op("act", lambda e: e.activation(R1[:, 0:n], pss[:, 0:n], AF.Sqrt, bias=EPS, scale=1.0 / D), reads=[pss], writes=[R1])
        S.op("dve", lambda e: e.reciprocal(RS[:, 0:n], R1[:, 0:n]), reads=[R1], writes=[RS])
        for k in range(KC):
            S.op("dve", lambda e: e.scalar_tensor_tensor(hn[:, k, 0:n], ht[:, k, 0:n], SMALL[:, 24 + k:25 + k], RS[:, 0:n], ALU.mult, ALU.mult),
                 reads=[ht, SMALL, RS], writes=[hn])
        for blk in range(NFM):
            ps = PSB[1 + blk % 2]
            for k in range(KC):
                S.op("pe", lambda e: e.matmul(ps[:, 0:n], WFM[:, k, blk * 128:(blk + 1) * 128], hn[:, k, 0:n], start=(k == 0), stop=(k == KC - 1)),
                     reads=[WFM, hn], writes=[ps])
            if blk < 10:
                st = STG[stg_i % 3]
                stg_i += 1
                S.op("act", lambda e: e.activation(st[:, 0:n], ps[:, 0:n], AF.Copy), reads=[ps], writes=[st])
                S.dma("pool", [(scfm_d[blk, :, c0:c0 + n], st[:, 0:n])], reads=[st], writes=[SCFM[blk][ti]])
            elif blk == 10:
                S.op("act", lambda e: e.activation(QT[ti][:, 0:n], ps[:, 0:n], AF.Copy, scale=SCALE_Q), reads=[ps], writes=[QT[ti]])
            else:
                S.op("dve", lambda e: e.tensor_copy(KT[ti][:, 0:n], ps[:, 0:n]), reads=[ps], writes=[KT[ti]])
        for sub in range(n // 128):
            ch = ti * 4 + sub
            ps = PSB[3 + sub % 2]
            for k in range(KC):
                S.op("pe", lambda e: e.matmul(ps[:, 0:NTM], hn[:, k, sub * 128:(sub + 1) * 128], WTM[:, k, :], start=(k == 0), stop=(k == KC - 1)),
                     reads=[hn, WTM], writes=[ps])
            stv = STV[ch % 2]
            S.op("dve", lambda e: e.tensor_copy(stv[:, :], ps[:, 0:128]), reads=[ps], writes=[stv])
            if ch == 0:
                S.op("dve", lambda e: e.memset(stv[0:112, :], 0.0), writes=[stv])
            S.dma("pool", [(rv_d[ch * 128:(ch + 1) * 128, :], stv[:, :])], reads=[stv], writes=[RVD[ch]])
            S.op("act", lambda e: e.activation(SV[ti][:, sub, :], ps[:, 128:256], AF.Copy), reads=[ps], writes=[SV[ti]])
            for hh in range(2):
                S.op("dve", lambda e: e.tensor_copy(DT[:, hh * NCH + ch:hh * NCH + ch + 1], ps[:, 256 + hh:257 + hh]), reads=[ps], writes=[DT])

    S.barrier()
    S.sb_off = off_qkv

    def sb_thread(tiles, z, acc, outp, tg):
        EB = [S.sb(f"eb{tg}{i}", [128, 512], F32) for i in range(2)]
        SPB = [S.sb(f"spb{tg}{i}", [128, 512], BF16) for i in range(2)]
        ECB = [S.sb(f"ecb{tg}{i}", [128, 512], F32) for i in range(2)]
        WB = [S.sb(f"wb{tg}{i}", [128, 512], BF16) for i in range(2)]
        OS = [S.sb(f"os{tg}{i}", [128, 512], F32) for i in range(2)]
        seq = []
        for ti in tiles:
            c0, n = tile_rng(ti)
            i0 = ti * 4
            nb = n // 128
            for j in range(i0 + nb - 1, -1, -1):
                seq.append((ti, j, c0, n, max(0, j - i0) * 128, j == i0 + nb - 1))

        def qk(idx):
            ti, j, c0, n, cs, first = seq[idx]
            kt = KT[j // 4]
            ko = (j % 4) * 128
            S.op("pe", lambda e: e.matmul(z[:, cs:n], kt[:, ko:ko + 128], QT[ti][:, cs:n], start=True, stop=True),
                 reads=[kt, QT[ti]], writes=[z])

        qk(0)
        yield
        nout = 0
        for idx, (ti, j, c0, n, cs, first) in enumerate(seq):
            i0 = ti * 4
            eb, spb, ecb, wb = EB[idx % 2], SPB[idx % 2], ECB[idx % 2], WB[idx % 2]
            S.op("act", lambda e: e.activation(eb[:, cs:n], z[:, cs:n], AF.Exp), reads=[z], writes=[eb])
            if j >= i0:
                S.op("dve", lambda e: e.tensor_tensor(eb[:, cs:cs + 128], eb[:, cs:cs + 128], CST[:, C_MS, :], ALU.mult),
                     reads=[eb, CST], writes=[eb])
            if j == 0:
                S.op("dve", lambda e: e.memset(eb[0:112, cs:n], 0.0), writes=[eb])
            yield
            if idx + 1 < len(seq):
                qk(idx + 1)
            S.op("act", lambda e: e.activation(spb[:, cs:n], eb[:, cs:n], AF.Ln, bias=1.0), reads=[eb], writes=[spb])
            yield
            S.op("pe", lambda e: e.matmul(acc[:, cs:n], CSTB[:, C_U, :], spb[:, cs:n], start=first, stop=False, skip_group_check=True),
                 reads=[CSTB, spb], writes=[acc])
            yield
            S.op("act", lambda e: e.activation(ecb[:, cs:n], acc[:, cs:n], AF.Exp, scale=-1.0), reads=[acc], writes=[ecb])
            yield
            S.op("pe", lambda e: e.matmul(acc[:, cs:n], CSTB[:, C_LM, :], spb[:, cs:n], start=False, stop=False, skip_group_check=True),
                 reads=[CSTB, spb], writes=[acc])
            S.op("dve", lambda e: e.tensor_tensor(wb[:, cs:n], eb[:, cs:n], ecb[:, cs:n], ALU.mult), reads=[eb, ecb], writes=[wb])
            yield
            svb = SV[j // 4]
            S.op("pe", lambda e: e.matmul(outp[:, cs:n], svb[:, j % 4, :], wb[:, cs:n], start=first, stop=(j == 0), skip_group_check=True),
                 reads=[svb, wb], writes=[outp])
            if j == 0:
                os_ = OS[nout % 2]
                nout += 1
                S.op("dve", lambda e: e.tensor_copy(os_[:, 0:n], outp[:, 0:n]), reads=[outp], writes=[os_])
                S.dma("pool", [(ydT_d[:, c0:c0 + n], os_[:, 0:n])], reads=[os_], writes=[OUT[3]])
            yield

    PL = PSB[6]

    def lin_thread():
        lin_base = S.sb_off
        CW = 512
        CIN = [[S.sb(f"cin{b}_{i}", [128, CW], F32) for i in range(2)] for b in range(3)]
        UB = [S.sb(f"ub{i}", [128, CW + 2], F32) for i in range(2)]
        ACCB = S.sb("accb", [128, CW], F32)
        YO = [S.sb(f"yo{i}", [128, CW], F32) for i in range(2)]
        ntile = (L + CW - 1) // CW
        for t in range(ntile):
            c0 = t * CW
            n = min(CW, L - c0)
            cb_, cc_, cx_ = CIN[0][t % 2], CIN[1][t % 2], CIN[2][t % 2]
            S.dma("sp", [(cb_[:, 0:n], scfm_d[0, :, c0:c0 + n])], writes=[cb_])
            S.dma("sp", [(cc_[:, 0:n], scfm_d[1, :, c0:c0 + n])], writes=[cc_])
            S.dma("sp", [(cx_[:, 0:n], scfm_d[2, :, c0:c0 + n])], writes=[cx_])
            yield
            u = UB[t % 2]
            S.op("dve", lambda e: e.tensor_tensor(u[:, 2:2 + n], cc_[:, 0:n], cx_[:, 0:n], ALU.mult), reads=[cc_, cx_], writes=[u])
            if t == 0:
                S.op("dve", lambda e: e.memset(u[:, 0:2 + 112], 0.0), writes=[u])
            else:
                up = UB[(t - 1) % 2]
                S.op("pool", lambda e: e.tensor_copy(u[:, 0:2], up[:, CW:CW + 2]), reads=[up], writes=[u])
            yield
            S.op("dve", lambda e: e.tensor_single_scalar(ACCB[:, 0:n], u[:, 0:n], SMALL[:, 0:1], ALU.mult), reads=[u, SMALL], writes=[ACCB])
            for i in (1, 2):
                S.op("dve", lambda e: e.scalar_tensor_tensor(ACCB[:, 0:n], u[:, i:i + n], SMALL[:, i:i + 1], ACCB[:, 0:n], ALU.mult, ALU.add),
                     reads=[u, SMALL, ACCB], writes=[ACCB])
            yield
            yo = YO[t % 2]
            S.op("pool", lambda e: e.tensor_tensor(yo[:, 0:n], ACCB[:, 0:n], cb_[:, 0:n], ALU.mult), reads=[ACCB, cb_], writes=[yo])
            S.dma("pool", [(yaT_d[:, c0:c0 + n], yo[:, 0:n])], reads=[yo], writes=[OUT[0]])
            yield

        S.barrier()
        S.sb_off = lin_base
        RA = [S.sb(f"ra{i}", [128, 128], F32) for i in range(2)]
        SEG = [S.sb(f"seg{i}", [128, 128], F32) for i in range(3)]
        SEGM = [S.sb(f"segm{i}", [128, 128], F32) for i in range(3)]
        PB = [S.sb(f"pb{i}", [128, 128], BF16) for i in range(2)]
        TMPY = [S.sb(f"tmpy{i}", [128, 128], F32) for i in range(2)]
        YB = [S.sb(f"yb{i}", [128, 128], F32) for i in range(2)]
        cnt = {"d": 0, "p": 0}

        def decay(a_ap, a_buf, dreg, scale=None, slot=None):
            i = cnt["d"]
            cnt["d"] += 1
            ra = RA[i % 2]
            sg = SEG[slot if slot is not None else i % 2]
            sm = SEGM[slot if slot is not None else i % 2]
            S.op("dve", lambda e: e.tensor_single_scalar(ra[:, :], CST[:, C_TRI, :], a_ap, ALU.mult), reads=[CST, a_buf], writes=[ra])
            S.op("pe", lambda e: e.matmul(PL[:, dreg], CST[:, C_SLT, :], ra[:, :], start=True, stop=True), reads=[CST, ra], writes=[PL])
            S.op("act", lambda e: e.activation(sg[:, :], PL[:, dreg], AF.Exp), reads=[PL], writes=[sg])
            if scale is None:
                S.op("pool", lambda e: e.tensor_tensor(sm[:, :], sg[:, :], CST[:, C_TRI, :], ALU.mult), reads=[sg, CST], writes=[sm])
            else:
                S.op("dve", lambda e: e.scalar_tensor_tensor(sm[:, :], sg[:, :], scale, CST[:, C_TRI, :], ALU.mult, ALU.mult), reads=[sg, CST], writes=[sm])
            return sg, sm

        def lin_head(q_list, kd_list, v_ap, v_buf, sm, ecol_ap, ecol_buf, etot_ap, etot_buf, S_list, Sbf_list, ybuf, y0, dv, rst, ryd, ryo, rs_):
            i = cnt["p"]
            cnt["p"] += 1
            pb = PB[i % 2]
            tm = TMPY[i % 2]
            S.op("dve", lambda e: e.tensor_tensor(pb[:, :], PL[:, rst], sm[:, :], ALU.mult), reads=[PL, sm], writes=[pb])
            S.op("pe", lambda e: e.matmul(PL[:, ryd], pb[:, :], v_ap, start=True, stop=True), reads=[pb, v_buf], writes=[PL])
            nk = len(q_list)
            for kk in range(nk):
                qa, qb = q_list[kk]
                S.op("pe", lambda e: e.matmul(PL[:, ryo], qa, Sbf_list[kk][:, 0:dv], start=(kk == 0), stop=(kk == nk - 1), skip_group_check=True),
                     reads=[qb, Sbf_list[kk]], writes=[PL])
            S.op("act", lambda e: e.activation(tm[:, 0:dv], PL[:, ryd], AF.Copy), reads=[PL], writes=[tm])
            S.op("dve", lambda e: e.scalar_tensor_tensor(ybuf[:, y0:y0 + dv], PL[:, ryo], ecol_ap, tm[:, 0:dv], ALU.mult, ALU.add),
                 reads=[PL, ecol_buf, tm], writes=[ybuf])
            for kk in range(nk):
                ka, kb = kd_list[kk]
                S.op("pe", lambda e: e.matmul(PL[:, rs_], ka, v_ap, start=True, stop=True), reads=[kb, v_buf], writes=[PL])
                S.op("dve", lambda e: e.scalar_tensor_tensor(S_list[kk][:, 0:dv], S_list[kk][:, 0:dv], etot_ap, PL[:, rs_], ALU.mult, ALU.add),
                     reads=[S_list[kk], etot_buf, PL], writes=[S_list[kk]])
                S.op("pool", lambda e: e.tensor_copy(Sbf_list[kk][:, 0:dv], S_list[kk][:, 0:dv]), reads=[S_list[kk]], writes=[Sbf_list[kk]])

        off_shared = S.sb_off
        R_ST, R_D, R_X, R_YD, R_YO, R_S = slice(0, 128), slice(128, 256), slice(256, 384), slice(384, 448), slice(448, 512), slice(128, 192)
        TMPD = S.sb("tmpd", [128, 2 * NCH], F32)
        DTS = S.sb("dts", [128, 2 * NCH], F32)
        AALL = S.sb("aall", [128, 2 * NCH], F32)
        ECOL = S.sb("ecol", [128, 2 * NCH], F32)
        ETOT = S.sb("etot", [128, 2 * NCH], F32)
        NA = S.sb("na", [128, 2], F32)
        for h in range(2):
            sl = slice(h * NCH, (h + 1) * NCH)
            S.op("act", lambda e: e.activation(TMPD[:, sl], DT[:, sl], AF.Exp, bias=SMALL[:, 18 + h:19 + h]), reads=[DT, SMALL], writes=[TMPD])
            S.op("act", lambda e: e.activation(DTS[:, sl], TMPD[:, sl], AF.Ln, bias=1.0), reads=[TMPD], writes=[DTS])
        S.op("act", lambda e: e.activation(NA[:, :], SMALL[:, 20:22], AF.Exp), reads=[SMALL], writes=[NA])
        S.op("dve", lambda e: e.tensor_single_scalar(NA[:, :], NA[:, :], -1.0, ALU.mult), reads=[NA], writes=[NA])
        yield
        for h in range(2):
            sl = slice(h * NCH, (h + 1) * NCH)
            S.op("dve", lambda e: e.tensor_single_scalar(AALL[:, sl], DTS[:, sl], NA[:, h:h + 1], ALU.mult), reads=[DTS, NA], writes=[AALL])
        S.op("pe", lambda e: e.matmul(PL[:, 0:2 * NCH], CST[:, C_TRI, :], AALL[:, :], start=True, stop=True), reads=[CST, AALL], writes=[PL])
        S.op("act", lambda e: e.activation(ECOL[:, :], PL[:, 0:2 * NCH], AF.Exp), reads=[PL], writes=[ECOL])
        S.op("pe", lambda e: e.matmul(PL[:, 0:2 * NCH], CST[:, C_ONE, :], AALL[:, :], start=True, stop=True), reads=[CST, AALL], writes=[PL])
        S.op("act", lambda e: e.activation(ETOT[:, :], PL[:, 0:2 * NCH], AF.Exp), reads=[PL], writes=[ETOT])
        yield

        XR = [[S.sb(f"xr{b}_{i}", [128, 515], F32) for i in range(2)] for b in range(3)]
        ACS = S.sb("acs", [128, 512], F32)
        XS = [S.sb(f"xs{i}", [128, 512], F32) for i in range(2)]
        BTt = [S.sb(f"btt{i}", [128, 512], BF16) for i in range(2)]
        CTt = [S.sb(f"ctt{i}", [128, 512], BF16) for i in range(2)]
        XDT = [S.sb(f"xdt{i}", [128, 64], BF16) for i in range(2)]
        KD = [S.sb(f"kd{i}", [128, 128], BF16) for i in range(4)]
        TMP2 = S.sb("tmp2", [128, 128], F32)
        SST = [S.sb(f"sst{i}", [128, 64], F32) for i in range(2)]
        SSB = [S.sb(f"ssb{i}", [128, 64], BF16) for i in range(2)]
        for h in range(2):
            S.op("dve", lambda e: e.memset(SST[h][:, :], 0.0), writes=[SST[h]])
            S.op("dve", lambda e: e.memset(SSB[h][:, :], 0.0), writes=[SSB[h]])
        for ti in range(33):
            c0, n = tile_rng(ti)
            outs = [XS[ti % 2], BTt[ti % 2], CTt[ti % 2]]
            for b in range(3):
                xr = XR[b][ti % 2]
                if ti == 0:
                    S.dma("sp", [(xr[:, 3:3 + n], scfm_d[3 + b, :, 0:n])], writes=[xr])
                    S.op("dve", lambda e: e.memset(xr[:, 0:3 + 112], 0.0), writes=[xr])
                else:
                    S.dma("sp", [(xr[:, 0:3 + n], scfm_d[3 + b, :, c0 - 3:c0 + n])], writes=[xr])
                w0 = 3 + 4 * b
                S.op("dve", lambda e: e.tensor_single_scalar(ACS[:, 0:n], xr[:, 0:n], SMALL[:, w0:w0 + 1], ALU.mult), reads=[xr, SMALL], writes=[ACS])
                yield
                for i in (1, 2, 3):
                    S.op("dve", lambda e: e.scalar_tensor_tensor(ACS[:, 0:n], xr[:, i:i + n], SMALL[:, w0 + i:w0 + i + 1], ACS[:, 0:n], ALU.mult, ALU.add),
                         reads=[xr, SMALL, ACS], writes=[ACS])
                S.op("act", lambda e: e.activation(outs[b][:, 0:n], ACS[:, 0:n], AF.Silu, bias=SMALL[:, 15 + b:16 + b]), reads=[ACS, SMALL], writes=[outs[b]])
                yield
            xs, bt, ct = outs
            if ti == 0:
                S.op("dve", lambda e: e.memset(xs[:, 0:112], 0.0), writes=[xs])
            for sub in range(n // 128):
                c = ti * 4 + sub
                cs_ = slice(sub * 128, (sub + 1) * 128)
                S.op("pe", lambda e: e.matmul(PL[:, R_ST], bt[:, cs_], ct[:, cs_], start=True, stop=True), reads=[bt, ct], writes=[PL])
                S.op("pe", lambda e: e.transpose(PSH[:, 0:128], bt[:, cs_], CSTB[:, C_ID, :]), reads=[bt, CSTB], writes=[PSH])
                S.op("pe", lambda e: e.transpose(PL[:, R_X], xs[:, cs_], CST[:, C_ID, :]), reads=[xs, CST], writes=[PL])
                yield
                yb_ = YB[c % 2]
                for h in range(2):
                    col = h * NCH + c
                    sg, sm = decay(AALL[:, col:col + 1], AALL, R_D)
                    yield
                    xdt = XDT[h]
                    kd = KD[h]
                    S.op("dve", lambda e: e.tensor_single_scalar(xdt[:, :], PL[:, 256 + 64 * h:256 + 64 * h + 64], DTS[:, col:col + 1], ALU.mult),
                         reads=[PL, DTS], writes=[xdt])
                    S.op("act", lambda e: e.activation(kd[:, :], PSH[:, 0:128], AF.Copy, scale=sg[:, 127:128]), reads=[PSH, sg], writes=[kd])
                    yield
                    lin_head([(ct[:, cs_], ct)], [(kd[:, :], kd)], xdt[:, :], xdt, sm, ECOL[:, col:col + 1], ECOL,
                             ETOT[:, col:col + 1], ETOT, [SST[h]], [SSB[h]], yb_, 64 * h, 64, R_ST, R_YD, R_YO, R_S)
                    yield
                S.op("dve", lambda e: e.tensor_tensor(TMP2[:, :], PL[:, R_X], DROW[:, :], ALU.mult), reads=[PL, DROW], writes=[TMP2])
                S.op("pool", lambda e: e.tensor_tensor(yb_[:, :], yb_[:, :], TMP2[:, :], ALU.add), reads=[yb_, TMP2], writes=[yb_])
                S.dma("pool", [(yb_d[c * 128:(c + 1) * 128, :], yb_[:, :])], reads=[yb_], writes=[OUT[1]])
                yield

        S.barrier()
        S.sb_off = off_shared
        Q_ST, Q_YD, Q_YO, Q_S = slice(0, 128), slice(128, 256), slice(256, 384), slice(384, 512)
        RIN = [[S.sb(f"rin{b}_{i}", [128, 512], F32) for i in range(2)] for b in range(6)]
        RT = [S.sb(f"rt{i}", [128, 512], F32) for i in range(4)]
        QR = [[S.sb(f"qr{k}_{i}", [128, 512], BF16) for i in range(2)] for k in range(2)]
        KR = [[S.sb(f"kr{k}_{i}", [128, 512], BF16) for i in range(2)] for k in range(2)]
        RVT = [S.sb(f"rvt{i}", [128, 128], BF16) for i in range(3)]
        RST = [S.sb(f"rst{i}", [128, 128], F32) for i in range(2)]
        RSB = [S.sb(f"rsb{i}", [128, 128], BF16) for i in range(2)]
        KD = [S.sb(f"rkd{i}", [128, 128], BF16) for i in range(4)]
        RC = S.sb("rc", [128, 4], F32)
        for k in range(2):
            S.op("dve", lambda e: e.memset(RST[k][:, :], 0.0), writes=[RST[k]])
            S.op("dve", lambda e: e.memset(RSB[k][:, :], 0.0), writes=[RSB[k]])
        sgR, smR = decay(SMALL[:, 22:23], SMALL, Q_YD, scale=1.0 / 16.0, slot=2)
        S.op("pe", lambda e: e.matmul(PL[:, 256:257], CST[:, C_TRI, :], SMALL[:, 22:23], start=True, stop=True), reads=[CST, SMALL], writes=[PL])
        S.op("act", lambda e: e.activation(RC[:, 0:1], PL[:, 256:257], AF.Exp), reads=[PL], writes=[RC])
        S.op("pe", lambda e: e.matmul(PL[:, 256:257], CST[:, C_ONE, :], SMALL[:, 22:23], start=True, stop=True), reads=[CST, SMALL], writes=[PL])
        S.op("act", lambda e: e.activation(RC[:, 1:2], PL[:, 256:257], AF.Exp), reads=[PL], writes=[RC])
        S.op("dve", lambda e: e.tensor_single_scalar(RC[:, 2:3], sgR[:, 127:128], 1.0 / 16.0, ALU.mult), reads=[sgR], writes=[RC])
        yield
        for ti in range(33):
            c0, n = tile_rng(ti)
            rin = [RIN[b][ti % 2] for b in range(6)]
            for b in range(4):
                S.dma("sp", [(rin[b][:, 0:n], scfm_d[6 + b, :, c0:c0 + n])], writes=[rin[b]])
            S.dma("sp", [(rin[4][:, 0:n], cos_d[:, c0:c0 + n])], writes=[rin[4]])
            S.dma("sp", [(rin[5][:, 0:n], sin_d[:, c0:c0 + n])], writes=[rin[5]])
            yield
            for (x0, x1, dst) in ((rin[0], rin[1], QR), (rin[2], rin[3], KR)):
                d0, d1 = dst[0][ti % 2], dst[1][ti % 2]
                S.op("dve", lambda e: e.tensor_tensor(RT[0][:, 0:n], x0[:, 0:n], rin[4][:, 0:n], ALU.mult), reads=[x0, rin[4]], writes=[RT[0]])
                S.op("pool", lambda e: e.tensor_tensor(RT[1][:, 0:n], x1[:, 0:n], rin[5][:, 0:n], ALU.mult), reads=[x1, rin[5]], writes=[RT[1]])
                yield
                S.op("dve", lambda e: e.tensor_tensor(d0[:, 0:n], RT[0][:, 0:n], RT[1][:, 0:n], ALU.subtract), reads=[RT[0], RT[1]], writes=[d0])
                S.op("pool", lambda e: e.tensor_tensor(RT[2][:, 0:n], x0[:, 0:n], rin[5][:, 0:n], ALU.mult), reads=[x0, rin[5]], writes=[RT[2]])
                yield
                S.op("dve", lambda e: e.tensor_tensor(RT[3][:, 0:n], x1[:, 0:n], rin[4][:, 0:n], ALU.mult), reads=[x1, rin[4]], writes=[RT[3]])
                S.op("dve", lambda e: e.tensor_tensor(d1[:, 0:n], RT[2][:, 0:n], RT[3][:, 0:n], ALU.add), reads=[RT[2], RT[3]], writes=[d1])
                yield
            qr = [QR[0][ti % 2], QR[1][ti % 2]]
            kr = [KR[0][ti % 2], KR[1][ti % 2]]
            for sub in range(n // 128):
                c = ti * 4 + sub
                cs_ = slice(sub * 128, (sub + 1) * 128)
                rv = RVT[c % 3]
                S.dma("sp", [(rv[:, :], rv_d[c * 128:(c + 1) * 128, :])], writes=[rv])
                for kk in range(2):
                    S.op("pe", lambda e: e.matmul(PL[:, Q_ST], kr[kk][:, cs_], qr[kk][:, cs_], start=(kk == 0), stop=(kk == 1), skip_group_check=True),
                         reads=[kr[kk], qr[kk]], writes=[PL])
                yield
                kds = []
                for kk in range(2):
                    S.op("pe", lambda e: e.transpose(PSH[:, kk * 128:(kk + 1) * 128], kr[kk][:, cs_], CSTB[:, C_ID, :]), reads=[kr[kk], CSTB], writes=[PSH])
                    kd = KD[2 * (c % 2) + kk]
                    S.op("act", lambda e: e.activation(kd[:, :], PSH[:, kk * 128:(kk + 1) * 128], AF.Copy, scale=RC[:, 2:3]), reads=[PSH, RC], writes=[kd])
                    kds.append((kd[:, :], kd))
                yield
                yb_ = YB[c % 2]
                lin_head([(qr[0][:, cs_], qr[0]), (qr[1][:, cs_], qr[1])], kds, rv[:, :], rv, smR, RC[:, 0:1], RC, RC[:, 1:2], RC,
                         RST, RSB, yb_, 0, 128, Q_ST, Q_YD, Q_YO, Q_S)
                S.dma("pool", [(yc_d[c * 128:(c + 1) * 128, :], yb_[:, :])], reads=[yb_], writes=[OUT[2]])
                yield

    run_threads([sb_thread(list(range(0, 33, 2)), PSB[0], PSB[2], PSB[3], "a"),
                 sb_thread(list(range(1, 33, 2)), PSB[1], PSB[4], PSB[5], "b"),
                 lin_thread()])
    S.finish(OUT)
    return nc, S


IN_SIZES = (1024, 1024, 1024, 1024, 2048, 16, 1024, 1024, 1024, 1024, 1024, 1024, 1024, 4096)
IN_OFFS = np.concatenate([[0], np.cumsum(IN_SIZES)]).astype(int)
_CONST_CACHE = {}


def _consts():
    if not _CONST_CACHE:
        _CONST_CACHE["cst"] = host_consts()
        c, s = rope_tables()
        _CONST_CACHE["cos"] = c
        _CONST_CACHE["sin"] = s
    return _CONST_CACHE


def fm(a):
    t = np.ascontiguousarray(a.T)
    return t.reshape(t.shape[0] // 128, 128, t.shape[1])


def colvec(v):
    return np.ascontiguousarray(v.reshape(-1, 128).T)


def prep_H(h, w_in, conv_a, ssd_conv_w, ssd_conv_b, ssd_dt_bias, ssd_a_log, ssd_d, npre):
    cs = _consts()
    hT = fm(h)
    maps = []
    O = IN_OFFS
    for c in range(NCORE):
        g = c // 2
        hh = c // 2
        fmcols = np.concatenate([
            np.arange(O[0] + 128 * c, O[0] + 128 * c + 128),
            np.arange(O[1] + 128 * c, O[1] + 128 * c + 128),
            np.arange(O[2] + 128 * c, O[2] + 128 * c + 128),
            np.arange(O[4] + 128 * c, O[4] + 128 * c + 128),
            np.arange(O[4] + 1024 + 128 * g, O[4] + 1024 + 128 * g + 128),
            np.arange(O[4] + 1536 + 128 * g, O[4] + 1536 + 128 * g + 128),
            np.arange(O[6] + 256 * hh, O[6] + 256 * hh + 256),
            np.arange(O[7] + 256 * hh, O[7] + 256 * hh + 256),
            np.arange(O[10] + 128 * c, O[10] + 128 * c + 128),
            np.arange(O[11] + 128 * c, O[11] + 128 * c + 128),
        ])
        tmcols = np.concatenate([
            np.arange(O[8] + 128 * c, O[8] + 128 * c + 128),
            np.arange(O[12] + 128 * c, O[12] + 128 * c + 128),
            np.arange(O[5] + 2 * c, O[5] + 2 * c + 2),
        ])
        xcols = [np.arange(128 * c, 128 * c + 128), np.arange(1024 + 128 * g, 1024 + 128 * g + 128),
                 np.arange(1536 + 128 * g, 1536 + 128 * g + 128)]
        scw = np.concatenate([ssd_conv_w[:, xc].T for xc in xcols], axis=1)
        scb = np.stack([ssd_conv_b[xc] for xc in xcols], axis=1)
        drow = np.broadcast_to(np.repeat(ssd_d[2 * c:2 * c + 2], 64)[None, :], (128, 128))
        lg = math.log(1.0 - 2.0 ** (-5.0 - hh))
        maps.append({
            "hT": hT,
            "npre": colvec(npre),
            "wfm": np.ascontiguousarray(w_in[:, fmcols].reshape(KC, 128, NFM * 128)),
            "wtm": np.ascontiguousarray(w_in[:, tmcols].reshape(KC, 128, NTM)),
            "cva": np.ascontiguousarray(conv_a[:, 128 * c:128 * c + 128].T),
            "scw": np.ascontiguousarray(scw),
            "scb": np.ascontiguousarray(scb),
            "dtb": np.ascontiguousarray(np.broadcast_to(ssd_dt_bias[None, 2 * c:2 * c + 2], (128, 2))),
            "alog": np.ascontiguousarray(np.broadcast_to(ssd_a_log[None, 2 * c:2 * c + 2], (128, 2))),
            "drow": np.ascontiguousarray(drow),
            "lgam": np.full((128, 1), lg, np.float32),
            "cos": cs["cos"], "sin": cs["sin"], "cst": cs["cst"],
        })
    return maps


def gather_H(res):
    ya = np.concatenate([r["yaT"].T for r in res], axis=1)
    yb = np.concatenate([r["yb"] for r in res], axis=1)
    yc = np.concatenate([r["yc"] for r in res], axis=1)
    yd = np.concatenate([r["ydT"].T for r in res], axis=1)
    return ya, yb, yc, yd


NTH = TOK // 3
NTT = NTH // TT


def build_T():
    nc = bass.Bass("TRN2", target_bir_lowering=False)
    S = Sched(nc)
    ei = lambda n, s, dt=F32: nc.dram_tensor(n, list(s), dt, kind="ExternalInput")
    hT_d = ei("hT", [KC, 128, TOK])
    yin_d = [ei(nm, [KC, 128, TOK]) for nm in ("ya", "yb", "yc", "yd")]
    wzg_d = ei("wzg", [D, 6144])
    wbr_d = ei("wbr", [4, D, D])
    wout_d = ei("wout", [D, D])
    wfi_d = ei("wfi", [D, 2 * DFF])
    wfo_d = ei("wfo", [DFF, D])
    vec_d = ei("vecs", [128, 40])
    cst_d = ei("cst", [128, 7, 128])
    hout_d = nc.dram_tensor("hout", [KC, 128, TOK], F32, kind="ExternalOutput")
    HOUT = Buf(hout_d, "houtd")

    CST = S.sb("cst", [128, 7, 128], F32)
    VEC = S.sb("vec", [128, 40], F32)
    S.dma("sp", [(CST[:, :, :], cst_d[:, :, :])], writes=[CST])
    S.dma("sp", [(VEC[:, :], vec_d[:, :])], writes=[VEC])
    V_PRE, V_SSD, V_POST, V_FPRE, V_FPOST = 0, 8, 16, 24, 32
    H = [S.sb(f"h{k}", [128, NTH], F32) for k in range(KC)]
    HN = [S.sb(f"hn{k}", [128, NTH], BF16) for k in range(KC)]
    BIGF = [S.sb(f"bf{k}", [128, NTH], F32) for k in range(KC)]
    BRF = [S.sb(f"br{k}", [128, NTH], BF16) for k in range(32)]
    BR = [BRF[8 * n:8 * n + 8] for n in range(4)]
    HID = BRF[0:22]
    MG = [S.sb(f"mg{k}", [128, NTH], BF16) for k in range(KC)]
    WS = [S.sb(f"ws{i}", [128, 11, 128], F32) for i in range(2)]
    WB = [S.sb(f"wb{i}", [128, 22, 128], BF16) for i in range(2)]
    SQ = [S.sb(f"sq{i}", [128, TT], F32) for i in range(2)]
    R1 = S.sb("r1", [128, TT], F32)
    RS = [S.sb(f"rs{i}", [128, TT], F32) for i in range(2)]
    STG = [S.sb(f"stg{i}", [128, TT], F32) for i in range(4)]
    SIG = [S.sb(f"sig{i}", [128, TT], F32) for i in range(2)]
    TMPF = [S.sb(f"tmpf{i}", [128, TT], F32) for i in range(2)]
    ACC = [S.sb(f"acc{i}", [128, TT], F32) for i in range(NTT)]
    MEAN = S.sb("mean", [128, TT], F32)
    DD = [S.sb(f"dd{i}", [128, TT], F32) for i in range(2)]
    PSB = [S.ps(f"ps{i}", [128, 512]) for i in range(8)]
    PS_SS = PSB[0]
    ctr = {"w": 0, "ws": 0, "ps": 0, "rs": 0, "sq": 0, "stg": 0, "sig": 0, "tmp": 0}

    def nxt(key, lst):
        i = ctr[key]
        ctr[key] += 1
        return lst[i % len(lst)]

    def load_w(wd2, kc, cb):
        wb = nxt("w", WB)
        v = wd2.rearrange("(k p) n -> p k n", p=128)
        for k0 in range(0, kc, 11):
            k1 = min(kc, k0 + 11)
            ws = nxt("ws", WS)
            S.dma("sp", [(ws[:, 0:k1 - k0, :], v[:, k0:k1, cb * 128:(cb + 1) * 128])], writes=[ws])
            S.op("pool", lambda e: e.tensor_copy(wb[:, k0:k1, :], ws[:, 0:k1 - k0, :]), reads=[ws], writes=[wb])
        return wb

    def tsl(t):
        return slice(t * TT, (t + 1) * TT)

    def gemm(wb, kc, X, t):
        ps = PSB[1 + ctr["ps"] % 6]
        ctr["ps"] += 1
        for k in range(kc):
            S.op("pe", lambda e: e.matmul(ps[:, 0:TT], wb[:, k, :], X[k][:, tsl(t)], start=(k == 0), stop=(k == kc - 1)),
                 reads=[wb, X[k]], writes=[ps])
        return ps

    def rstd_tile(chunks, t, nfeat, sl_fn=None):
        n = len(chunks)
        for i, (b, ap) in enumerate(chunks):
            sq = nxt("sq", SQ)
            S.op("pool", lambda e: e.tensor_tensor(sq[:, :], ap, ap, ALU.mult), reads=[b], writes=[sq])
            S.op("pe", lambda e: e.matmul(PS_SS[:, 0:TT], CST[:, C_ONE, :], sq[:, :], start=(i == 0), stop=(i == n - 1)),
                 reads=[CST, sq], writes=[PS_SS])
        S.op("act", lambda e: e.activation(R1[:, :], PS_SS[:, 0:TT], AF.Sqrt, bias=EPS, scale=1.0 / nfeat), reads=[PS_SS], writes=[R1])
        rs = nxt("rs", RS)
        S.op("dve", lambda e: e.reciprocal(rs[:, :], R1[:, :]), reads=[R1], writes=[rs])
        return rs

    def norm_to_hn(voff):
        for t in range(NTT):
            rs = rstd_tile([(H[k], H[k][:, tsl(t)]) for k in range(KC)], t, D)
            for k in range(KC):
                S.op("dve", lambda e: e.scalar_tensor_tensor(HN[k][:, tsl(t)], H[k][:, tsl(t)], VEC[:, voff + k:voff + k + 1], rs[:, :], ALU.mult, ALU.mult),
                     reads=[H[k], VEC, rs], writes=[HN[k]])

    def add_normed(voff):
        for t in range(NTT):
            rs = rstd_tile([(BIGF[k], BIGF[k][:, tsl(t)]) for k in range(KC)], t, D)
            for k in range(KC):
                tm = nxt("tmp", TMPF)
                S.op("dve", lambda e: e.scalar_tensor_tensor(tm[:, :], BIGF[k][:, tsl(t)], VEC[:, voff + k:voff + k + 1], rs[:, :], ALU.mult, ALU.mult),
                     reads=[BIGF[k], VEC, rs], writes=[tm])
                S.op("pool", lambda e: e.tensor_tensor(H[k][:, tsl(t)], H[k][:, tsl(t)], tm[:, :], ALU.add), reads=[H[k], tm], writes=[H[k]])

    for hf in range(TOK // NTH):
        off = hf * NTH
        for k in range(KC):
            S.dma("sp", [(H[k][:, :], hT_d[k, :, off:off + NTH])], writes=[H[k]])
        norm_to_hn(V_PRE)
        for cb in range(KC):
            wb = load_w(wzg_d, KC, cb)
            for t in range(NTT):
                ps = gemm(wb, KC, HN, t)
                sg = nxt("sig", SIG)
                S.op("act", lambda e: e.activation(sg[:, :], ps[:, 0:TT], AF.Silu), reads=[ps], writes=[sg])
                st = nxt("stg", STG)
                S.dma("sp", [(st[:, :], yin_d[1][cb, :, off + t * TT:off + (t + 1) * TT])], writes=[st])
                S.op("dve", lambda e: e.tensor_tensor(BIGF[cb][:, tsl(t)], st[:, :], sg[:, :], ALU.mult), reads=[st, sg], writes=[BIGF[cb]])
        for t in range(NTT):
            for g in range(4):
                rs = rstd_tile([(BIGF[2 * g + i], BIGF[2 * g + i][:, tsl(t)]) for i in range(2)], t, 256)
                for i in range(2):
                    k = 2 * g + i
                    S.op("dve", lambda e: e.scalar_tensor_tensor(BR[1][k][:, tsl(t)], BIGF[k][:, tsl(t)], VEC[:, V_SSD + k:V_SSD + k + 1], rs[:, :], ALU.mult, ALU.mult),
                         reads=[BIGF[k], VEC, rs], writes=[BR[1][k]])
        for cb in range(KC):
            wb = load_w(wzg_d, KC, KC + cb)
            for t in range(NTT):
                ps = gemm(wb, KC, HN, t)
                S.op("act", lambda e: e.activation(BIGF[cb][:, tsl(t)], ps[:, 0:TT], AF.Silu), reads=[ps], writes=[BIGF[cb]])
        for t in range(NTT):
            for g in range(4):
                ycs = []
                for i in range(2):
                    st = nxt("stg", STG)
                    S.dma("sp", [(st[:, :], yin_d[2][2 * g + i, :, off + t * TT:off + (t + 1) * TT])], writes=[st])
                    ycs.append(st)
                for i in range(2):
                    S.op("pe", lambda e: e.matmul(PS_SS[:, 0:TT], CST[:, C_ONE, :], ycs[i][:, :], start=(i == 0), stop=(i == 1)),
                         reads=[CST, ycs[i]], writes=[PS_SS])
                S.op("act", lambda e: e.activation(MEAN[:, :], PS_SS[:, 0:TT], AF.Copy, scale=1.0 / 256.0), reads=[PS_SS], writes=[MEAN])
                for i in range(2):
                    S.op("dve", lambda e: e.tensor_tensor(DD[i][:, :], ycs[i][:, :], MEAN[:, :], ALU.subtract), reads=[ycs[i], MEAN], writes=[DD[i]])
                rs = rstd_tile([(DD[i], DD[i][:, :]) for i in range(2)], t, 256)
                for i in range(2):
                    k = 2 * g + i
                    tm = nxt("tmp", TMPF)
                    S.op("dve", lambda e: e.tensor_tensor(tm[:, :], DD[i][:, :], rs[:, :], ALU.mult), reads=[DD[i], rs], writes=[tm])
                    S.op("dve", lambda e: e.tensor_tensor(BR[2][k][:, tsl(t)], tm[:, :], BIGF[k][:, tsl(t)], ALU.mult), reads=[tm, BIGF[k]], writes=[BR[2][k]])
        for (bi, yi) in ((0, 0), (3, 3)):
            for cb in range(KC):
                for t in range(NTT):
                    st = nxt("stg", STG)
                    S.dma("sp", [(st[:, :], yin_d[yi][cb, :, off + t * TT:off + (t + 1) * TT])], writes=[st])
                    S.op("pool", lambda e: e.tensor_copy(BR[bi][cb][:, tsl(t)], st[:, :]), reads=[st], writes=[BR[bi][cb]])
        for m in range(KC):
            for n in range(4):
                wbg = load_w(wzg_d, KC, 2 * KC + n * KC + m)
                wbu = load_w(wbr_d[n], KC, m)
                for t in range(NTT):
                    psg = gemm(wbg, KC, HN, t)
                    psu = gemm(wbu, KC, BR[n], t)
                    sg = nxt("sig", SIG)
                    S.op("act", lambda e: e.activation(sg[:, :], psg[:, 0:TT], AF.Sigmoid), reads=[psg], writes=[sg])
                    if n == 0:
                        S.op("dve", lambda e: e.tensor_tensor(ACC[t][:, :], sg[:, :], psu[:, 0:TT], ALU.mult), reads=[sg, psu], writes=[ACC[t]])
                    else:
                        tm = nxt("tmp", TMPF)
                        S.op("dve", lambda e: e.tensor_tensor(tm[:, :], sg[:, :], psu[:, 0:TT], ALU.mult), reads=[sg, psu], writes=[tm])
                        S.op("pool", lambda e: e.tensor_tensor(ACC[t][:, :], ACC[t][:, :], tm[:, :], ALU.add), reads=[ACC[t], tm], writes=[ACC[t]])
            for t in range(NTT):
                S.op("pool", lambda e: e.tensor_copy(MG[m][:, tsl(t)], ACC[t][:, :]), reads=[ACC[t]], writes=[MG[m]])
        for cb in range(KC):
            wb = load_w(wout_d, KC, cb)
            for t in range(NTT):
                ps = gemm(wb, KC, MG, t)
                S.op("act", lambda e: e.activation(BIGF[cb][:, tsl(t)], ps[:, 0:TT], AF.Copy), reads=[ps], writes=[BIGF[cb]])
        add_normed(V_POST)
        norm_to_hn(V_FPRE)
        for j in range(22):
            wbg = load_w(wfi_d, KC, j)
            wbu = load_w(wfi_d, KC, 22 + j)
            for t in range(NTT):
                psg = gemm(wbg, KC, HN, t)
                psu = gemm(wbu, KC, HN, t)
                sg = nxt("sig", SIG)
                S.op("act", lambda e: e.activation(sg[:, :], psg[:, 0:TT], AF.Silu), reads=[psg], writes=[sg])
                S.op("dve", lambda e: e.tensor_tensor(HID[j][:, tsl(t)], sg[:, :], psu[:, 0:TT], ALU.mult), reads=[sg, psu], writes=[HID[j]])
        for cb in range(KC):
            wb = load_w(wfo_d, 22, cb)
            for t in range(NTT):
                ps = gemm(wb, 22, HID, t)
                S.op("act", lambda e: e.activation(BIGF[cb][:, tsl(t)], ps[:, 0:TT], AF.Copy), reads=[ps], writes=[BIGF[cb]])
        add_normed(V_FPOST)
        for k in range(KC):
            S.dma("pool", [(hout_d[k, :, off:off + NTH], H[k][:, :])], reads=[H[k]], writes=[HOUT])
    S.finish([HOUT])
    return nc, S


def prep_T(h, ya, yb, yc, yd, w_in, w_branch, w_out, w_ffn_in, w_ffn_out, npre, ssdn, npost, nfpre, nfpost):
    cs = _consts()
    O = IN_OFFS
    wzg = np.ascontiguousarray(np.concatenate([w_in[:, O[3]:O[3] + 1024], w_in[:, O[9]:O[9] + 1024], w_in[:, O[13]:O[13] + 4096]], axis=1))
    vecs = np.ascontiguousarray(np.concatenate([colvec(v) for v in (npre, ssdn, npost, nfpre, nfpost)], axis=1))
    maps = []
    for c in range(NCORE):
        sl = slice(c * TOK, (c + 1) * TOK)
        maps.append({
            "hT": fm(h[sl]), "ya": fm(ya[sl]), "yb": fm(yb[sl]), "yc": fm(yc[sl]), "yd": fm(yd[sl]),
            "wzg": wzg, "wbr": np.ascontiguousarray(w_branch), "wout": np.ascontiguousarray(w_out),
            "wfi": np.ascontiguousarray(w_ffn_in), "wfo": np.ascontiguousarray(w_ffn_out),
            "vecs": vecs, "cst": cs["cst"],
        })
    return maps


def gather_T(res):
    return np.concatenate([r["hout"].reshape(D, TOK).T for r in res], axis=0)


_PROG = {}


def kernel(x, meta, w_in, conv_a, ssd_conv_w, ssd_conv_b, ssd_dt_bias, ssd_a_log, ssd_d, ssd_norm,
           w_branch, w_out, w_ffn_in, w_ffn_out, norm_mix_pre, norm_mix_post, norm_ffn_pre, norm_ffn_post):
    f = lambda a: np.asarray(a, dtype=np.float32)
    x, meta = f(x), f(meta)
    h = np.concatenate([np.zeros((L - 16 - x.shape[1], D), np.float32), meta, x[0]], axis=0)
    if "H" not in _PROG:
        _PROG["H"] = build_H()[0]
        _PROG["T"] = build_T()[0]
    cores = list(range(NCORE))
    for l in range(2):
        mH = prep_H(h, f(w_in[l]), f(conv_a[l]), f(ssd_conv_w[l]), f(ssd_conv_b[l]), f(ssd_dt_bias[l]),
                    f(ssd_a_log[l]), f(ssd_d[l]), f(norm_mix_pre[l]))
        rH = run_bass_kernel_spmd(_PROG["H"], mH, core_ids=cores)
        ya, yb, yc, yd = gather_H(rH.results)
        mT = prep_T(h, ya, yb, yc, yd, f(w_in[l]), f(w_branch[l]), f(w_out[l]), f(w_ffn_in[l]), f(w_ffn_out[l]),
                    f(norm_mix_pre[l]), f(ssd_norm[l]), f(norm_mix_post[l]), f(norm_ffn_pre[l]), f(norm_ffn_post[l]))
        rT = run_bass_kernel_spmd(_PROG["T"], mT, core_ids=cores)
        h = gather_T(rT.results)
    return np.ascontiguousarray(h[128:][None]).astype(np.float32)
```

```python
import math
import numpy as np
import ml_dtypes
import concourse.bass as bass
import concourse.mybir as mybir
from concourse.bass_utils import run_bass_kernel_spmd

F32 = mybir.dt.float32
BF16 = mybir.dt.bfloat16
AF = mybir.ActivationFunctionType
ALU = mybir.AluOpType
EPOCH = 24000

L = 16512
NCH = 129
D = 1024
KC = 8
EPS = 1e-6
NCORE = 8
TOK = L // NCORE
HALF = TOK // 2
TT = 344
DFF = 2816


class Buf:
    __slots__ = ("t", "name", "w", "r", "dsem", "dcnt", "excl")

    def __init__(self, t, name, excl=False):
        self.t = t
        self.name = name
        self.w = None
        self.r = []
        self.dsem = None
        self.dcnt = 0
        self.excl = excl

    def __getitem__(self, idx):
        return self.t[idx]


class Sched:
    def __init__(self, nc):
        self.nc = nc
        self.engs = {"pe": nc.tensor, "act": nc.scalar, "dve": nc.vector,
                     "pool": nc.gpsimd, "sp": nc.sync}
        self.sems = {k: [] for k in self.engs}
        self.cnt = {k: 0 for k in self.engs}
        self.seen = {k: {} for k in self.engs}
        self.nsem = 0
        self.ninst = 0
        self.dsems = []
        self.sb_off = 16640
        self.nps = 0

    def new_sem(self, name):
        self.nsem += 1
        return self.nc.alloc_semaphore(name=f"{name}_{self.nsem}")

    def sb(self, name, shape, dt):
        nbytes = int(np.prod(shape[1:])) * (4 if dt == F32 else 2)
        nbytes = (nbytes + 63) // 64 * 64
        t = self.nc.alloc_sbuf_tensor_at(f"{name}_{self.ninst}_{self.sb_off}", list(shape), dt, offset=self.sb_off)
        self.sb_off += nbytes
        assert self.sb_off <= 229376, ("sbuf overflow", name, self.sb_off)
        return Buf(t, name)

    def ps(self, name, shape, dt=F32):
        self.nps += 1
        assert self.nps <= 8
        return Buf(self.nc.alloc_psum_tensor(name, list(shape), dt), name, excl=True)

    def _deps(self, reads, writes):
        deps = []
        w2 = list(writes)
        for b in reads:
            if b.excl:
                w2.append(b)
                continue
            if b.w is not None:
                deps.append(b.w)
        for b in w2:
            if b.w is not None:
                deps.append(b.w)
            deps.extend(b.r)
        return deps, w2

    def _wait(self, ek, deps):
        eng = self.engs[ek]
        seen = self.seen[ek]
        best = {}
        for (s, v) in deps:
            if seen.get(s, 0) >= v:
                continue
            if best.get(s, 0) < v:
                best[s] = v
        for s, v in best.items():
            eng.wait_ge(s, v)
            seen[s] = v

    def _record(self, dep, reads, writes):
        for b in writes:
            b.w = dep
            b.r = []
        for b in reads:
            b.r.append(dep)
            if len(b.r) > 8:
                m = {}
                for (s, v) in b.r:
                    if m.get(s, 0) < v:
                        m[s] = v
                b.r = list(m.items())

    def op(self, ek, fn, reads=(), writes=()):
        deps, w2 = self._deps(reads, writes)
        self._wait(ek, deps)
        n = self.cnt[ek]
        ep, off = divmod(n, EPOCH)
        while len(self.sems[ek]) <= ep:
            self.sems[ek].append(self.new_sem(ek))
        sem = self.sems[ek][ep]
        fn(self.engs[ek]).then_inc(sem, 1)
        self.cnt[ek] = n + 1
        dep = (sem, off + 1)
        self._record(dep, [b for b in reads if not b.excl], w2)
        self.ninst += 1
        return dep

    def dma(self, qk, pairs, reads=(), writes=(), sembuf=None):
        deps, w2 = self._deps(reads, writes)
        sb = sembuf if sembuf is not None else (list(reads) + list(writes))[0]
        if sb.dsem is None:
            sb.dsem = self.new_sem("d")
            self.dsems.append(sb)
        if sb.dcnt > 0:
            deps.append((sb.dsem, sb.dcnt))
        self._wait(qk, deps)
        eng = self.engs[qk]
        for (o, i) in pairs:
            eng.dma_start(out=o, in_=i).then_inc(sb.dsem, 16)
            sb.dcnt += 16
            self.ninst += 1
        dep = (sb.dsem, sb.dcnt)
        self._record(dep, [b for b in reads if not b.excl], w2)
        return dep

    def cc(self, kind, in_ap, out_ap, groups, reads=(), writes=()):
        deps, w2 = self._deps(reads, writes)
        sb = list(writes)[0]
        if sb.dsem is None:
            sb.dsem = self.new_sem("c")
            self.dsems.append(sb)
        if sb.dcnt > 0:
            deps.append((sb.dsem, sb.dcnt))
        self._wait("pool", deps)
        self.engs["pool"].collective_compute(kind, ALU.bypass, replica_groups=groups, ins=[in_ap], outs=[out_ap]).then_inc(sb.dsem, 16)
        sb.dcnt += 16
        self.ninst += 1
        dep = (sb.dsem, sb.dcnt)
        self._record(dep, [b for b in reads if not b.excl], w2)
        return dep

    def barrier(self):
        deps = []
        for k in self.engs:
            n = self.cnt[k]
            if n == 0:
                continue
            ep, off = divmod(n - 1, EPOCH)
            deps.append((self.sems[k][ep], off + 1))
        for b in self.dsems:
            deps.append((b.dsem, b.dcnt))
        for k in self.engs:
            self._wait(k, deps)

    def finish(self, bufs):
        deps = []
        for b in bufs:
            if b.w is not None:
                deps.append(b.w)
        self._wait("sp", deps)


C_ID, C_TRI, C_SLT, C_ONE, C_U, C_LM, C_MS = range(7)


def run_threads(gens):
    gens = list(gens)
    while gens:
        for g in list(gens):
            try:
                next(g)
            except StopIteration:
                gens.remove(g)


def host_consts():
    i = np.arange(128)
    c = np.zeros((128, 7, 128), np.float32)
    c[:, C_ID, :] = np.eye(128)
    c[:, C_TRI, :] = (i[:, None] <= i[None, :])
    c[:, C_SLT, :] = (i[:, None] > i[None, :])
    c[:, C_ONE, :] = 1.0
    c[:, C_U, :] = (i[:, None] >= i[None, :])
    c[:, C_LM, :] = (i[:, None] < i[None, :])
    c[:, C_MS, :] = (i[None, :] > i[:, None])
    return c


def rope_tables():
    half = 128
    inv = np.power(np.float32(10000.0), -(np.arange(half, dtype=np.float32) / np.float32(half))).astype(np.float32)
    pos = np.arange(L, dtype=np.float32)
    ang = (pos[None, :] * inv[:, None]).astype(np.float32)
    return np.cos(ang.astype(np.float64)).astype(np.float32), np.sin(ang.astype(np.float64)).astype(np.float32)


NFM = 12
NTM = 258


def build_H():
    nc = bass.Bass("TRN2", target_bir_lowering=False)
    S = Sched(nc)
    ei = lambda n, s, dt=F32: nc.dram_tensor(n, list(s), dt, kind="ExternalInput")
    eo = lambda n, s, dt=F32: nc.dram_tensor(n, list(s), dt, kind="ExternalOutput")
    hT_d = ei("hT", [KC, 128, L])
    npre_d = ei("npre", [128, KC])
    wfm_d = ei("wfm", [KC, 128, NFM * 128])
    wtm_d = ei("wtm", [KC, 128, NTM])
    cva_d = ei("cva", [128, 3])
    scw_d = ei("scw", [128, 12])
    scb_d = ei("scb", [128, 3])
    dtb_d = ei("dtb", [128, 2])
    alog_d = ei("alog", [128, 2])
    drow_d = ei("drow", [128, 128])
    lgam_d = ei("lgam", [128, 1])
    cos_d = ei("cos", [128, L])
    sin_d = ei("sin", [128, L])
    cst_d = ei("cst", [128, 7, 128])
    yaT_d = eo("yaT", [128, L])
    yb_d = eo("yb", [L, 128])
    yc_d = eo("yc", [L, 128])
    ydT_d = eo("ydT", [128, L])
    scfm_d = nc.dram_tensor("scfm", [10, 128, L], F32)
    rv_d = nc.dram_tensor("rvs", [L, 128], BF16)
    OUT = [Buf(yaT_d, "yaTd"), Buf(yb_d, "ybd"), Buf(yc_d, "ycd"), Buf(ydT_d, "ydTd")]
    SCFM = [[Buf(scfm_d, f"scfm{b}_{i}") for i in range(33)] for b in range(10)]
    RVD = [Buf(rv_d, f"rvd{c}") for c in range(NCH)]

    CST = S.sb("cst", [128, 7, 128], F32)
    CSTB = S.sb("cstb", [128, 7, 128], BF16)
    SMALL = S.sb("small", [128, 32], F32)
    DROW = S.sb("drow", [128, 128], F32)
    DT = S.sb("dt", [128, 2 * NCH], F32)
    off_persist = S.sb_off
    QT = [S.sb(f"qt{i}", [128, 512], BF16) for i in range(33)]
    KT = [S.sb(f"kt{i}", [128, 512], BF16) for i in range(33)]
    SV = [S.sb(f"sv{i}", [128, 4, 128], BF16) for i in range(33)]
    off_qkv = S.sb_off
    PSB = [S.ps(f"ps{i}", [128, 512]) for i in range(7)]
    PSH = S.ps("psh", [128, 1024], BF16)
    base_off = S.sb_off

    S.dma("sp", [(CST[:, :, :], cst_d[:, :, :])], writes=[CST])
    S.dma("sp", [(SMALL[:, 0:3], cva_d[:, :]), (SMALL[:, 3:15], scw_d[:, :]), (SMALL[:, 15:18], scb_d[:, :]),
                 (SMALL[:, 18:20], dtb_d[:, :]), (SMALL[:, 20:22], alog_d[:, :]), (SMALL[:, 22:23], lgam_d[:, :]),
                 (SMALL[:, 24:32], npre_d[:, :])], writes=[SMALL])
    S.dma("sp", [(DROW[:, :], drow_d[:, :])], writes=[DROW])
    S.op("dve", lambda e: e.tensor_copy(CSTB[:, :, :], CST[:, :, :]), reads=[CST], writes=[CSTB])

    def tile_rng(i):
        c0 = i * 512
        return c0, min(512, L - c0)

    WFM = S.sb("wfm", [128, KC, NFM * 128], BF16)
    WTM = S.sb("wtm", [128, KC, NTM], BF16)
    WST = [S.sb(f"wst{i}", [128, NFM * 128], F32) for i in range(1)]
    for k in range(KC):
        st = WST[0]
        S.dma("sp", [(st[:, :], wfm_d[k, :, :])], writes=[st])
        S.op("pool", lambda e: e.tensor_copy(WFM[:, k, :], st[:, :]), reads=[st], writes=[WFM])
    for k in range(KC):
        st = WST[0]
        S.dma("sp", [(st[:, 0:NTM], wtm_d[k, :, :])], writes=[st])
        S.op("pool", lambda e: e.tensor_copy(WTM[:, k, :], st[:, 0:NTM]), reads=[st], writes=[WTM])
    HT = [S.sb(f"ht{i}", [128, KC, 512], F32) for i in range(2)]
    HN = [S.sb(f"hn{i}", [128, KC, 512], BF16) for i in range(2)]
    SQ = [S.sb(f"sq{i}", [128, 512], F32) for i in range(2)]
    R1 = S.sb("r1", [128, 512], F32)
    RS = S.sb("rs", [128, 512], F32)
    STG = [S.sb(f"stg{i}", [128, 512], F32) for i in range(3)]
    STV = [S.sb(f"stv{i}", [128, 128], BF16) for i in range(2)]
    stg_i = 0
    SCALE_Q = 1.0 / math.sqrt(128.0)
    for ti in range(33):
        c0, n = tile_rng(ti)
        ht = HT[ti % 2]
        hn = HN[ti % 2]
        S.dma("sp", [(ht[:, k, 0:n], hT_d[k, :, c0:c0 + n]) for k in range(KC)], writes=[ht])
        pss = PSB[0]
        for k in range(KC):
            sq = SQ[k % 2]
            S.op("pool", lambda e: e.tensor_tensor(sq[:, 0:n], ht[:, k, 0:n], ht[:, k, 0:n], ALU.mult), reads=[ht], writes=[sq])
            S.op("pe", lambda e: e.matmul(pss[:, 0:n], CST[:, C_ONE, :], sq[:, 0:n], start=(k == 0), stop=(k == KC - 1)),
                 reads=[CST, sq], writes=[pss])
        S.op("act", lambda e: e.activation(R1[:, 0:n], pss[:, 0:n], AF.Sqrt, bias=EPS, scale=1.0 / D), reads=[pss], writes=[R1])
        S.op("dve", lambda e: e.reciprocal(RS[:, 0:n], R1[:, 0:n]), reads=[R1], writes=[RS])
        for k in range(KC):
            S.op("dve", lambda e: e.scalar_tensor_tensor(hn[:, k, 0:n], ht[:, k, 0:n], SMALL[:, 24 + k:25 + k], RS[:, 0:n], ALU.mult, ALU.mult),
                 reads=[ht, SMALL, RS], writes=[hn])
        for blk in range(NFM):
            ps = PSB[1 + blk % 2]
            for k in range(KC):
                S.op("pe", lambda e: e.matmul(ps[:, 0:n], WFM[:, k, blk * 128:(blk + 1) * 128], hn[:, k, 0:n], start=(k == 0), stop=(k == KC - 1)),
                     reads=[WFM, hn], writes=[ps])
            if blk < 10:
                st = STG[stg_i % 3]
                stg_i += 1
                S.op("act", lambda e: e.activation(st[:, 0:n], ps[:, 0:n], AF.Copy), reads=[ps], writes=[st])
                S.dma("pool", [(scfm_d[blk, :, c0:c0 + n], st[:, 0:n])], reads=[st], writes=[SCFM[blk][ti]])
            elif blk == 10:
                S.op("act", lambda e: e.activation(QT[ti][:, 0:n], ps[:, 0:n], AF.Copy, scale=SCALE_Q), reads=[ps], writes=[QT[ti]])
            else:
                S.op("dve", lambda e: e.tensor_copy(KT[ti][:, 0:n], ps[:, 0:n]), reads=[ps], writes=[KT[ti]])
        for sub in range(n // 128):
            ch = ti * 4 + sub
            ps = PSB[3 + sub % 2]
            for k in range(KC):
                S.op("pe", lambda e: e.matmul(ps[:, 0:NTM], hn[:, k, sub * 128:(sub + 1) * 128], WTM[:, k, :], start=(k == 0), stop=(k == KC - 1)),
                     reads=[hn, WTM], writes=[ps])
            stv = STV[ch % 2]
            S.op("dve", lambda e: e.tensor_copy(stv[:, :], ps[:, 0:128]), reads=[ps], writes=[stv])
            if ch == 0:
                S.op("dve", lambda e: e.memset(stv[0:112, :], 0.0), writes=[stv])
            S.dma("pool", [(rv_d[ch * 128:(ch + 1) * 128, :], stv[:, :])], reads=[stv], writes=[RVD[ch]])
            S.op("act", lambda e: e.activation(SV[ti][:, sub, :], ps[:, 128:256], AF.Copy), reads=[ps], writes=[SV[ti]])
            for hh in range(2):
                S.op("dve", lambda e: e.tensor_copy(DT[:, hh * NCH + ch:hh * NCH + ch + 1], ps[:, 256 + hh:257 + hh]), reads=[ps], writes=[DT])

    S.barrier()
    S.sb_off = off_qkv

    def sb_thread(tiles, z, acc, outp, tg):
        EB = [S.sb(f"eb{tg}{i}", [128, 512], F32) for i in range(2)]
        SPB = [S.sb(f"spb{tg}{i}", [128, 512], BF16) for i in range(2)]
        ECB = [S.sb(f"ecb{tg}{i}", [128, 512], F32) for i in range(2)]
        WB = [S.sb(f"wb{tg}{i}", [128, 512], BF16) for i in range(2)]
        OS = [S.sb(f"os{tg}{i}", [128, 512], F32) for i in range(2)]
        seq = []
        for ti in tiles:
            c0, n = tile_rng(ti)
            i0 = ti * 4
            nb = n // 128
            for j in range(i0 + nb - 1, -1, -1):
                seq.append((ti, j, c0, n, max(0, j - i0) * 128, j == i0 + nb - 1))

        def qk(idx):
            ti, j, c0, n, cs, first = seq[idx]
            kt = KT[j // 4]
            ko = (j % 4) * 128
            S.op("pe", lambda e: e.matmul(z[:, cs:n], kt[:, ko:ko + 128], QT[ti][:, cs:n], start=True, stop=True),
                 reads=[kt, QT[ti]], writes=[z])

        qk(0)
        yield
        nout = 0
        for idx, (ti, j, c0, n, cs, first) in enumerate(seq):
            i0 = ti * 4
            eb, spb, ecb, wb = EB[idx % 2], SPB[idx % 2], ECB[idx % 2], WB[idx % 2]
            S.op("act", lambda e: e.activation(eb[:, cs:n], z[:, cs:n], AF.Exp), reads=[z], writes=[eb])
            if j >= i0:
                S.op("dve", lambda e: e.tensor_tensor(eb[:, cs:cs + 128], eb[:, cs:cs + 128], CST[:, C_MS, :], ALU.mult),
                     reads=[eb, CST], writes=[eb])
            if j == 0:
                S.op("dve", lambda e: e.memset(eb[0:112, cs:n], 0.0), writes=[eb])
            yield
            if idx + 1 < len(seq):
                qk(idx + 1)
            S.op("act", lambda e: e.activation(spb[:, cs:n], eb[:, cs:n], AF.Ln, bias=1.0), reads=[eb], writes=[spb])
            yield
            S.op("pe", lambda e: e.matmul(acc[:, cs:n], CSTB[:, C_U, :], spb[:, cs:n], start=first, stop=False, skip_group_check=True),
                 reads=[CSTB, spb], writes=[acc])
            yield
            S.op("act", lambda e: e.activation(ecb[:, cs:n], acc[:, cs:n], AF.Exp, scale=-1.0), reads=[acc], writes=[ecb])
            yield
            S.op("pe", lambda e: e.matmul(acc[:, cs:n], CSTB[:, C_LM, :], spb[:, cs:n], start=False, stop=False, skip_group_check=True),
                 reads=[CSTB, spb], writes=[acc])
            S.op("dve", lambda e: e.tensor_tensor(wb[:, cs:n], eb[:, cs:n], ecb[:, cs:n], ALU.mult), reads=[eb, ecb], writes=[wb])
            yield
            svb = SV[j // 4]
            S.op("pe", lambda e: e.matmul(outp[:, cs:n], svb[:, j % 4, :], wb[:, cs:n], start=first, stop=(j == 0), skip_group_check=True),
                 reads=[svb, wb], writes=[outp])
            if j == 0:
                os_ = OS[nout % 2]
                nout += 1
                S.op("dve", lambda e: e.tensor_copy(os_[:, 0:n], outp[:, 0:n]), reads=[outp], writes=[os_])
                S.dma("pool", [(ydT_d[:, c0:c0 + n], os_[:, 0:n])], reads=[os_], writes=[OUT[3]])
            yield

    PL = PSB[6]

    def lin_thread():
        lin_base = S.sb_off
        CW = 512
        CIN = [[S.sb(f"cin{b}_{i}", [128, CW], F32) for i in range(2)] for b in range(3)]
        UB = [S.sb(f"ub{i}", [128, CW + 2], F32) for i in range(2)]
        ACCB = S.sb("accb", [128, CW], F32)
        YO = [S.sb(f"yo{i}", [128, CW], F32) for i in range(2)]
        ntile = (L + CW - 1) // CW
        for t in range(ntile):
            c0 = t * CW
            n = min(CW, L - c0)
            cb_, cc_, cx_ = CIN[0][t % 2], CIN[1][t % 2], CIN[2][t % 2]
            S.dma("sp", [(cb_[:, 0:n], scfm_d[0, :, c0:c0 + n])], writes=[cb_])
            S.dma("sp", [(cc_[:, 0:n], scfm_d[1, :, c0:c0 + n])], writes=[cc_])
            S.dma("sp", [(cx_[:, 0:n], scfm_d[2, :, c0:c0 + n])], writes=[cx_])
            yield
            u = UB[t % 2]
            S.op("dve", lambda e: e.tensor_tensor(u[:, 2:2 + n], cc_[:, 0:n], cx_[:, 0:n], ALU.mult), reads=[cc_, cx_], writes=[u])
            if t == 0:
                S.op("dve", lambda e: e.memset(u[:, 0:2 + 112], 0.0), writes=[u])
            else:
                up = UB[(t - 1) % 2]
                S.op("pool", lambda e: e.tensor_copy(u[:, 0:2], up[:, CW:CW + 2]), reads=[up], writes=[u])
            yield
            S.op("dve", lambda e: e.tensor_single_scalar(ACCB[:, 0:n], u[:, 0:n], SMALL[:, 0:1], ALU.mult), reads=[u, SMALL], writes=[ACCB])
            for i in (1, 2):
                S.op("dve", lambda e: e.scalar_tensor_tensor(ACCB[:, 0:n], u[:, i:i + n], SMALL[:, i:i + 1], ACCB[:, 0:n], ALU.mult, ALU.add),
                     reads=[u, SMALL, ACCB], writes=[ACCB])
            yield
            yo = YO[t % 2]
            S.op("pool", lambda e: e.tensor_tensor(yo[:, 0:n], ACCB[:, 0:n], cb_[:, 0:n], ALU.mult), reads=[ACCB, cb_], writes=[yo])
            S.dma("pool", [(yaT_d[:, c0:c0 + n], yo[:, 0:n])], reads=[yo], writes=[OUT[0]])
            yield

        S.barrier()
        S.sb_off = lin_base
        RA = [S.sb(f"ra{i}", [128, 128], F32) for i in range(2)]
        SEG = [S.sb(f"seg{i}", [128, 128], F32) for i in range(3)]
        SEGM = [S.sb(f"segm{i}", [128, 128], F32) for i in range(3)]
        PB = [S.sb(f"pb{i}", [128, 128], BF16) for i in range(2)]
        TMPY = [S.sb(f"tmpy{i}", [128, 128], F32) for i in range(2)]
        YB = [S.sb(f"yb{i}", [128, 128], F32) for i in range(2)]
        cnt = {"d": 0, "p": 0}

        def decay(a_ap, a_buf, dreg, scale=None, slot=None):
            i = cnt["d"]
            cnt["d"] += 1
            ra = RA[i % 2]
            sg = SEG[slot if slot is not None else i % 2]
            sm = SEGM[slot if slot is not None else i % 2]
            S.op("dve", lambda e: e.tensor_single_scalar(ra[:, :], CST[:, C_TRI, :], a_ap, ALU.mult), reads=[CST, a_buf], writes=[ra])
            S.op("pe", lambda e: e.matmul(PL[:, dreg], CST[:, C_SLT, :], ra[:, :], start=True, stop=True), reads=[CST, ra], writes=[PL])
            S.op("act", lambda e: e.activation(sg[:, :], PL[:, dreg], AF.Exp), reads=[PL], writes=[sg])
            if scale is None:
                S.op("pool", lambda e: e.tensor_tensor(sm[:, :], sg[:, :], CST[:, C_TRI, :], ALU.mult), reads=[sg, CST], writes=[sm])
            else:
                S.op("dve", lambda e: e.scalar_tensor_tensor(sm[:, :], sg[:, :], scale, CST[:, C_TRI, :], ALU.mult, ALU.mult), reads=[sg, CST], writes=[sm])
            return sg, sm

        def lin_head(q_list, kd_list, v_ap, v_buf, sm, ecol_ap, ecol_buf, etot_ap, etot_buf, S_list, Sbf_list, ybuf, y0, dv, rst, ryd, ryo, rs_):
            i = cnt["p"]
            cnt["p"] += 1
            pb = PB[i % 2]
            tm = TMPY[i % 2]
            S.op("dve", lambda e: e.tensor_tensor(pb[:, :], PL[:, rst], sm[:, :], ALU.mult), reads=[PL, sm], writes=[pb])
            S.op("pe", lambda e: e.matmul(PL[:, ryd], pb[:, :], v_ap, start=True, stop=True), reads=[pb, v_buf], writes=[PL])
            nk = len(q_list)
            for kk in range(nk):
                qa, qb = q_list[kk]
                S.op("pe", lambda e: e.matmul(PL[:, ryo], qa, Sbf_list[kk][:, 0:dv], start=(kk == 0), stop=(kk == nk - 1), skip_group_check=True),
                     reads=[qb, Sbf_list[kk]], writes=[PL])
            S.op("act", lambda e: e.activation(tm[:, 0:dv], PL[:, ryd], AF.Copy), reads=[PL], writes=[tm])
            S.op("dve", lambda e: e.scalar_tensor_tensor(ybuf[:, y0:y0 + dv], PL[:, ryo], ecol_ap, tm[:, 0:dv], ALU.mult, ALU.add),
                 reads=[PL, ecol_buf, tm], writes=[ybuf])
            for kk in range(nk):
                ka, kb = kd_list[kk]
                S.op("pe", lambda e: e.matmul(PL[:, rs_], ka, v_ap, start=True, stop=True), reads=[kb, v_buf], writes=[PL])
                S.op("dve", lambda e: e.scalar_tensor_tensor(S_list[kk][:, 0:dv], S_list[kk][:, 0:dv], etot_ap, PL[:, rs_], ALU.mult, ALU.add),
                     reads=[S_list[kk], etot_buf, PL], writes=[S_list[kk]])
                S.op("pool", lambda e: e.tensor_copy(Sbf_list[kk][:, 0:dv], S_list[kk][:, 0:dv]), reads=[S_list[kk]], writes=[Sbf_list[kk]])

        off_shared = S.sb_off
        R_ST, R_D, R_X, R_YD, R_YO, R_S = slice(0, 128), slice(128, 256), slice(256, 384), slice(384, 448), slice(448, 512), slice(128, 192)
        TMPD = S.sb("tmpd", [128, 2 * NCH], F32)
        DTS = S.sb("dts", [128, 2 * NCH], F32)
        AALL = S.sb("aall", [128, 2 * NCH], F32)
        ECOL = S.sb("ecol", [128, 2 * NCH], F32)
        ETOT = S.sb("etot", [128, 2 * NCH], F32)
        NA = S.sb("na", [128, 2], F32)
        for h in range(2):
            sl = slice(h * NCH, (h + 1) * NCH)
            S.op("act", lambda e: e.activation(TMPD[:, sl], DT[:, sl], AF.Exp, bias=SMALL[:, 18 + h:19 + h]), reads=[DT, SMALL], writes=[TMPD])
            S.op("act", lambda e: e.activation(DTS[:, sl], TMPD[:, sl], AF.Ln, bias=1.0), reads=[TMPD], writes=[DTS])
        S.op("act", lambda e: e.activation(NA[:, :], SMALL[:, 20:22], AF.Exp), reads=[SMALL], writes=[NA])
        S.op("dve", lambda e: e.tensor_single_scalar(NA[:, :], NA[:, :], -1.0, ALU.mult), reads=[NA], writes=[NA])
        yield
        for h in range(2):
            sl = slice(h * NCH, (h + 1) * NCH)
            S.op("dve", lambda e: e.tensor_single_scalar(AALL[:, sl], DTS[:, sl], NA[:, h:h + 1], ALU.mult), reads=[DTS, NA], writes=[AALL])
        S.op("pe", lambda e: e.matmul(PL[:, 0:2 * NCH], CST[:, C_TRI, :], AALL[:, :], start=True, stop=True), reads=[CST, AALL], writes=[PL])
        S.op("act", lambda e: e.activation(ECOL[:, :], PL[:, 0:2 * NCH], AF.Exp), reads=[PL], writes=[ECOL])
        S.op("pe", lambda e: e.matmul(PL[:, 0:2 * NCH], CST[:, C_ONE, :], AALL[:, :], start=True, stop=True), reads=[CST, AALL], writes=[PL])
        S.op("act", lambda e: e.activation(ETOT[:, :], PL[:, 0:2 * NCH], AF.Exp), reads=[PL], writes=[ETOT])
        yield

        XR = [[S.sb(f"xr{b}_{i}", [128, 515], F32) for i in range(2)] for b in range(3)]
        ACS = S.sb("acs", [128, 512], F32)
        XS = [S.sb(f"xs{i}", [128, 512], F32) for i in range(2)]
        BTt = [S.sb(f"btt{i}", [128, 512], BF16) for i in range(2)]
        CTt = [S.sb(f"ctt{i}", [128, 512], BF16) for i in range(2)]
        XDT = [S.sb(f"xdt{i}", [128, 64], BF16) for i in range(2)]
        KD = [S.sb(f"kd{i}", [128, 128], BF16) for i in range(4)]
        TMP2 = S.sb("tmp2", [128, 128], F32)
        SST = [S.sb(f"sst{i}", [128, 64], F32) for i in range(2)]
        SSB = [S.sb(f"ssb{i}", [128, 64], BF16) for i in range(2)]
        for h in range(2):
            S.op("dve", lambda e: e.memset(SST[h][:, :], 0.0), writes=[SST[h]])
            S.op("dve", lambda e: e.memset(SSB[h][:, :], 0.0), writes=[SSB[h]])
        for ti in range(33):
            c0, n = tile_rng(ti)
            outs = [XS[ti % 2], BTt[ti % 2], CTt[ti % 2]]
            for b in range(3):
                xr = XR[b][ti % 2]
                if ti == 0:
                    S.dma("sp", [(xr[:, 3:3 + n], scfm_d[3 + b, :, 0:n])], writes=[xr])
                    S.op("dve", lambda e: e.memset(xr[:, 0:3 + 112], 0.0), writes=[xr])
                else:
                    S.dma("sp", [(xr[:, 0:3 + n], scfm_d[3 + b, :, c0 - 3:c0 + n])], writes=[xr])
                w0 = 3 + 4 * b
                S.op("dve", lambda e: e.tensor_single_scalar(ACS[:, 0:n], xr[:, 0:n], SMALL[:, w0:w0 + 1], ALU.mult), reads=[xr, SMALL], writes=[ACS])
                yield
                for i in (1, 2, 3):
                    S.op("dve", lambda e: e.scalar_tensor_tensor(ACS[:, 0:n], xr[:, i:i + n], SMALL[:, w0 + i:w0 + i + 1], ACS[:, 0:n], ALU.mult, ALU.add),
                         reads=[xr, SMALL, ACS], writes=[ACS])
                S.op("act", lambda e: e.activation(outs[b][:, 0:n], ACS[:, 0:n], AF.Silu, bias=SMALL[:, 15 + b:16 + b]), reads=[ACS, SMALL], writes=[outs[b]])
                yield
            xs, bt, ct = outs
            if ti == 0:
                S.op("dve", lambda e: e.memset(xs[:, 0:112], 0.0), writes=[xs])
            for sub in range(n // 128):
                c = ti * 4 + sub
                cs_ = slice(sub * 128, (sub + 1) * 128)
                S.op("pe", lambda e: e.matmul(PL[:, R_ST], bt[:, cs_], ct[:, cs_], start=True, stop=True), reads=[bt, ct], writes=[PL])
                S.op("pe", lambda e: e.transpose(PSH[:, 0:128], bt[:, cs_], CSTB[:, C_ID, :]), reads=[bt, CSTB], writes=[PSH])
                S.op("pe", lambda e: e.transpose(PL[:, R_X], xs[:, cs_], CST[:, C_ID, :]), reads=[xs, CST], writes=[PL])
                yield
                yb_ = YB[c % 2]
                for h in range(2):
                    col = h * NCH + c
                    sg, sm = decay(AALL[:, col:col + 1], AALL, R_D)
                    yield
                    xdt = XDT[h]
                    kd = KD[h]
                    S.op("dve", lambda e: e.tensor_single_scalar(xdt[:, :], PL[:, 256 + 64 * h:256 + 64 * h + 64], DTS[:, col:col + 1], ALU.mult),
                         reads=[PL, DTS], writes=[xdt])
                    S.op("act", lambda e: e.activation(kd[:, :], PSH[:, 0:128], AF.Copy, scale=sg[:, 127:128]), reads=[PSH, sg], writes=[kd])
                    yield
                    lin_head([(ct[:, cs_], ct)], [(kd[:, :], kd)], xdt[:, :], xdt, sm, ECOL[:, col:col + 1], ECOL,
                             ETOT[:, col:col + 1], ETOT, [SST[h]], [SSB[h]], yb_, 64 * h, 64, R_ST, R_YD, R_YO, R_S)
                    yield
                S.op("dve", lambda e: e.tensor_tensor(TMP2[:, :], PL[:, R_X], DROW[:, :], ALU.mult), reads=[PL, DROW], writes=[TMP2])
                S.op("pool", lambda e: e.tensor_tensor(yb_[:, :], yb_[:, :], TMP2[:, :], ALU.add), reads=[yb_, TMP2], writes=[yb_])
                S.dma("pool", [(yb_d[c * 128:(c + 1) * 128, :], yb_[:, :])], reads=[yb_], writes=[OUT[1]])
                yield

        S.barrier()
        S.sb_off = off_shared
        Q_ST, Q_YD, Q_YO, Q_S = slice(0, 128), slice(128, 256), slice(256, 384), slice(384, 512)
        RIN = [[S.sb(f"rin{b}_{i}", [128, 512], F32) for i in range(2)] for b in range(6)]
        RT = [S.sb(f"rt{i}", [128, 512], F32) for i in range(4)]
        QR = [[S.sb(f"qr{k}_{i}", [128, 512], BF16) for i in range(2)] for k in range(2)]
        KR = [[S.sb(f"kr{k}_{i}", [128, 512], BF16) for i in range(2)] for k in range(2)]
        RVT = [S.sb(f"rvt{i}", [128, 128], BF16) for i in range(3)]
        RST = [S.sb(f"rst{i}", [128, 128], F32) for i in range(2)]
        RSB = [S.sb(f"rsb{i}", [128, 128], BF16) for i in range(2)]
        KD = [S.sb(f"rkd{i}", [128, 128], BF16) for i in range(4)]
        RC = S.sb("rc", [128, 4], F32)
        for k in range(2):
            S.op("dve", lambda e: e.memset(RST[k][:, :], 0.0), writes=[RST[k]])
            S.op("dve", lambda e: e.memset(RSB[k][:, :], 0.0), writes=[RSB[k]])
        sgR, smR = decay(SMALL[:, 22:23], SMALL, Q_YD, scale=1.0 / 16.0, slot=2)
        S.op("pe", lambda e: e.matmul(PL[:, 256:257], CST[:, C_TRI, :], SMALL[:, 22:23], start=True, stop=True), reads=[CST, SMALL], writes=[PL])
        S.op("act", lambda e: e.activation(RC[:, 0:1], PL[:, 256:257], AF.Exp), reads=[PL], writes=[RC])
        S.op("pe", lambda e: e.matmul(PL[:, 256:257], CST[:, C_ONE, :], SMALL[:, 22:23], start=True, stop=True), reads=[CST, SMALL], writes=[PL])
        S.op("act", lambda e: e.activation(RC[:, 1:2], PL[:, 256:257], AF.Exp), reads=[PL], writes=[RC])
        S.op("dve", lambda e: e.tensor_single_scalar(RC[:, 2:3], sgR[:, 127:128], 1.0 / 16.0, ALU.mult), reads=[sgR], writes=[RC])
        yield
        for ti in range(33):
            c0, n = tile_rng(ti)
            rin = [RIN[b][ti % 2] for b in range(6)]
            for b in range(4):
                S.dma("sp", [(rin[b][:, 0:n], scfm_d[6 + b, :, c0:c0 + n])], writes=[rin[b]])
            S.dma("sp", [(rin[4][:, 0:n], cos_d[:, c0:c0 + n])], writes=[rin[4]])
            S.dma("sp", [(rin[5][:, 0:n], sin_d[:, c0:c0 + n])], writes=[rin[5]])
            yield
            for (x0, x1, dst) in ((rin[0], rin[1], QR), (rin[2], rin[3], KR)):
                d0, d1 = dst[0][ti % 2], dst[1][ti % 2]
                S.op("dve", lambda e: e.tensor_tensor(RT[0][:, 0:n], x0[:, 0:n], rin[4][:, 0:n], ALU.mult), reads=[x0, rin[4]], writes=[RT[0]])
                S.op("pool", lambda e: e.tensor_tensor(RT[1][:, 0:n], x1[:, 0:n], rin[5][:, 0:n], ALU.mult), reads=[x1, rin[5]], writes=[RT[1]])
                yield
                S.op("dve", lambda e: e.tensor_tensor(d0[:, 0:n], RT[0][:, 0:n], RT[1][:, 0:n], ALU.subtract), reads=[RT[0], RT[1]], writes=[d0])
                S.op("pool", lambda e: e.tensor_tensor(RT[2][:, 0:n], x0[:, 0:n], rin[5][:, 0:n], ALU.mult), reads=[x0, rin[5]], writes=[RT[2]])
                yield
                S.op("dve", lambda e: e.tensor_tensor(RT[3][:, 0:n], x1[:, 0:n], rin[4][:, 0:n], ALU.mult), reads=[x1, rin[4]], writes=[RT[3]])
                S.op("dve", lambda e: e.tensor_tensor(d1[:, 0:n], RT[2][:, 0:n], RT[3][:, 0:n], ALU.add), reads=[RT[2], RT[3]], writes=[d1])
                yield
            qr = [QR[0][ti % 2], QR[1][ti % 2]]
            kr = [KR[0][ti % 2], KR[1][ti % 2]]
            for sub in range(n // 128):
                c = ti * 4 + sub
                cs_ = slice(sub * 128, (sub + 1) * 128)
                rv = RVT[c % 3]
                S.dma("sp", [(rv[:, :], rv_d[c * 128:(c + 1) * 128, :])], writes=[rv])
                for kk in range(2):
                    S.op("pe", lambda e: e.matmul(PL[:, Q_ST], kr[kk][:, cs_], qr[kk][:, cs_], start=(kk == 0), stop=(kk == 1), skip_group_check=True),
                         reads=[kr[kk], qr[kk]], writes=[PL])
                yield
                kds = []
                for kk in range(2):
                    S.op("pe", lambda e: e.transpose(PSH[:, kk * 128:(kk + 1) * 128], kr[kk][:, cs_], CSTB[:, C_ID, :]), reads=[kr[kk], CSTB], writes=[PSH])
                    kd = KD[2 * (c % 2) + kk]
                    S.op("act", lambda e: e.activation(kd[:, :], PSH[:, kk * 128:(kk + 1) * 128], AF.Copy, scale=RC[:, 2:3]), reads=[PSH, RC], writes=[kd])
                    kds.append((kd[:, :], kd))
                yield
                yb_ = YB[c % 2]
                lin_head([(qr[0][:, cs_], qr[0]), (qr[1][:, cs_], qr[1])], kds, rv[:, :], rv, smR, RC[:, 0:1], RC, RC[:, 1:2], RC,
                         RST, RSB, yb_, 0, 128, Q_ST, Q_YD, Q_YO, Q_S)
                S.dma("pool", [(yc_d[c * 128:(c + 1) * 128, :], yb_[:, :])], reads=[yb_], writes=[OUT[2]])
                yield

    run_threads([sb_thread(list(range(0, 33, 2)), PSB[0], PSB[2], PSB[3], "a"),
                 sb_thread(list(range(1, 33, 2)), PSB[1], PSB[4], PSB[5], "b"),
                 lin_thread()])
    S.finish(OUT)
    return nc, S


IN_SIZES = (1024, 1024, 1024, 1024, 2048, 16, 1024, 1024, 1024, 1024, 1024, 1024, 1024, 4096)
IN_OFFS = np.concatenate([[0], np.cumsum(IN_SIZES)]).astype(int)
_CONST_CACHE = {}


def _consts():
    if not _CONST_CACHE:
        _CONST_CACHE["cst"] = host_consts()
        c, s = rope_tables()
        _CONST_CACHE["cos"] = c
        _CONST_CACHE["sin"] = s
    return _CONST_CACHE


def fm(a):
    t = np.ascontiguousarray(a.T)
    return t.reshape(t.shape[0] // 128, 128, t.shape[1])


def colvec(v):
    return np.ascontiguousarray(v.reshape(-1, 128).T)


def prep_H(h, w_in, conv_a, ssd_conv_w, ssd_conv_b, ssd_dt_bias, ssd_a_log, ssd_d, npre):
    cs = _consts()
    hT = fm(h)
    maps = []
    O = IN_OFFS
    for c in range(NCORE):
        g = c // 2
        hh = c // 2
        fmcols = np.concatenate([
            np.arange(O[0] + 128 * c, O[0] + 128 * c + 128),
            np.arange(O[1] + 128 * c, O[1] + 128 * c + 128),
            np.arange(O[2] + 128 * c, O[2] + 128 * c + 128),
            np.arange(O[4] + 128 * c, O[4] + 128 * c + 128),
            np.arange(O[4] + 1024 + 128 * g, O[4] + 1024 + 128 * g + 128),
            np.arange(O[4] + 1536 + 128 * g, O[4] + 1536 + 128 * g + 128),
            np.arange(O[6] + 256 * hh, O[6] + 256 * hh + 256),
            np.arange(O[7] + 256 * hh, O[7] + 256 * hh + 256),
            np.arange(O[10] + 128 * c, O[10] + 128 * c + 128),
            np.arange(O[11] + 128 * c, O[11] + 128 * c + 128),
        ])
        tmcols = np.concatenate([
            np.arange(O[8] + 128 * c, O[8] + 128 * c + 128),
            np.arange(O[12] + 128 * c, O[12] + 128 * c + 128),
            np.arange(O[5] + 2 * c, O[5] + 2 * c + 2),
        ])
        xcols = [np.arange(128 * c, 128 * c + 128), np.arange(1024 + 128 * g, 1024 + 128 * g + 128),
                 np.arange(1536 + 128 * g, 1536 + 128 * g + 128)]
        scw = np.concatenate([ssd_conv_w[:, xc].T for xc in xcols], axis=1)
        scb = np.stack([ssd_conv_b[xc] for xc in xcols], axis=1)
        drow = np.broadcast_to(np.repeat(ssd_d[2 * c:2 * c + 2], 64)[None, :], (128, 128))
        lg = math.log(1.0 - 2.0 ** (-5.0 - hh))
        maps.append({
            "hT": hT,
            "npre": colvec(npre),
            "wfm": np.ascontiguousarray(w_in[:, fmcols].reshape(KC, 128, NFM * 128)),
            "wtm": np.ascontiguousarray(w_in[:, tmcols].reshape(KC, 128, NTM)),
            "cva": np.ascontiguousarray(conv_a[:, 128 * c:128 * c + 128].T),
            "scw": np.ascontiguousarray(scw),
            "scb": np.ascontiguousarray(scb),
            "dtb": np.ascontiguousarray(np.broadcast_to(ssd_dt_bias[None, 2 * c:2 * c + 2], (128, 2))),
            "alog": np.ascontiguousarray(np.broadcast_to(ssd_a_log[None, 2 * c:2 * c + 2], (128, 2))),
            "drow": np.ascontiguousarray(drow),
            "lgam": np.full((128, 1), lg, np.float32),
            "cos": cs["cos"], "sin": cs["sin"], "cst": cs["cst"],
        })
    return maps


def gather_H(res):
    ya = np.concatenate([r["yaT"].T for r in res], axis=1)
    yb = np.concatenate([r["yb"] for r in res], axis=1)
    yc = np.concatenate([r["yc"] for r in res], axis=1)
    yd = np.concatenate([r["ydT"].T for r in res], axis=1)
    return ya, yb, yc, yd


NTH = TOK // 3
NTT = NTH // TT


def build_T():
    nc = bass.Bass("TRN2", target_bir_lowering=False)
    S = Sched(nc)
    ei = lambda n, s, dt=F32: nc.dram_tensor(n, list(s), dt, kind="ExternalInput")
    hT_d = ei("hT", [KC, 128, TOK])
    yin_d = [ei(nm, [KC, 128, TOK]) for nm in ("ya", "yb", "yc", "yd")]
    wzg_d = ei("wzg", [D, 6144])
    wbr_d = ei("wbr", [4, D, D])
    wout_d = ei("wout", [D, D])
    wfi_d = ei("wfi", [D, 2 * DFF])
    wfo_d = ei("wfo", [DFF, D])
    vec_d = ei("vecs", [128, 40])
    cst_d = ei("cst", [128, 7, 128])
    hout_d = nc.dram_tensor("hout", [KC, 128, TOK], F32, kind="ExternalOutput")
    HOUT = Buf(hout_d, "houtd")

    CST = S.sb("cst", [128, 7, 128], F32)
    VEC = S.sb("vec", [128, 40], F32)
    S.dma("sp", [(CST[:, :, :], cst_d[:, :, :])], writes=[CST])
    S.dma("sp", [(VEC[:, :], vec_d[:, :])], writes=[VEC])
    V_PRE, V_SSD, V_POST, V_FPRE, V_FPOST = 0, 8, 16, 24, 32
    H = [S.sb(f"h{k}", [128, NTH], F32) for k in range(KC)]
    HN = [S.sb(f"hn{k}", [128, NTH], BF16) for k in range(KC)]
    BIGF = [S.sb(f"bf{k}", [128, NTH], F32) for k in range(KC)]
    BRF = [S.sb(f"br{k}", [128, NTH], BF16) for k in range(32)]
    BR = [BRF[8 * n:8 * n + 8] for n in range(4)]
    HID = BRF[0:22]
    MG = [S.sb(f"mg{k}", [128, NTH], BF16) for k in range(KC)]
    WS = [S.sb(f"ws{i}", [128, 11, 128], F32) for i in range(3)]
    WB = [S.sb(f"wb{i}", [128, 22, 128], BF16) for i in range(4)]
    SQ = [S.sb(f"sq{i}", [128, TT], F32) for i in range(2)]
    R1 = S.sb("r1", [128, TT], F32)
    RS = [S.sb(f"rs{i}", [128, TT], F32) for i in range(2)]
    STG = [S.sb(f"stg{i}", [128, TT], F32) for i in range(4)]
    SIG = [S.sb(f"sig{i}", [128, TT], F32) for i in range(2)]
    TMPF = [S.sb(f"tmpf{i}", [128, TT], F32) for i in range(2)]
    ACC = [S.sb(f"acc{i}", [128, TT], F32) for i in range(NTT)]
    MEAN = S.sb("mean", [128, TT], F32)
    DD = [S.sb(f"dd{i}", [128, TT], F32) for i in range(2)]
    PSB = [S.ps(f"ps{i}", [128, 512]) for i in range(8)]
    PS_SS = PSB[0]
    ctr = {"w": 0, "ws": 0, "ps": 0, "rs": 0, "sq": 0, "stg": 0, "sig": 0, "tmp": 0}

    def nxt(key, lst):
        i = ctr[key]
        ctr[key] += 1
        return lst[i % len(lst)]

    def load_w(wd2, kc, cb):
        wb = nxt("w", WB)
        v = wd2.rearrange("(k p) n -> p k n", p=128)
        for k0 in range(0, kc, 11):
            k1 = min(kc, k0 + 11)
            ws = nxt("ws", WS)
            S.dma("sp", [(ws[:, 0:k1 - k0, :], v[:, k0:k1, cb * 128:(cb + 1) * 128])], writes=[ws])
            ctr["cast"] = ctr.get("cast", 0) + 1
            if ctr["cast"] % 2 == 0:
                S.op("dve", lambda e: e.tensor_copy(wb[:, k0:k1, :], ws[:, 0:k1 - k0, :]), reads=[ws], writes=[wb])
            else:
                S.op("act", lambda e: e.activation(wb[:, k0:k1, :], ws[:, 0:k1 - k0, :], AF.Copy), reads=[ws], writes=[wb])
        return wb

    def tsl(t):
        return slice(t * TT, (t + 1) * TT)

    def gemm(wb, kc, X, t):
        ps = PSB[1 + ctr["ps"] % 6]
        ctr["ps"] += 1
        for k in range(kc):
            S.op("pe", lambda e: e.matmul(ps[:, 0:TT], wb[:, k, :], X[k][:, tsl(t)], start=(k == 0), stop=(k == kc - 1)),
                 reads=[wb, X[k]], writes=[ps])
        return ps

    def rstd_tile(chunks, t, nfeat, sl_fn=None):
        n = len(chunks)
        for i, (b, ap) in enumerate(chunks):
            sq = nxt("sq", SQ)
            S.op("pool", lambda e: e.tensor_tensor(sq[:, :], ap, ap, ALU.mult), reads=[b], writes=[sq])
            S.op("pe", lambda e: e.matmul(PS_SS[:, 0:TT], CST[:, C_ONE, :], sq[:, :], start=(i == 0), stop=(i == n - 1)),
                 reads=[CST, sq], writes=[PS_SS])
        S.op("act", lambda e: e.activation(R1[:, :], PS_SS[:, 0:TT], AF.Sqrt, bias=EPS, scale=1.0 / nfeat), reads=[PS_SS], writes=[R1])
        rs = nxt("rs", RS)
        S.op("dve", lambda e: e.reciprocal(rs[:, :], R1[:, :]), reads=[R1], writes=[rs])
        return rs

    def norm_to_hn(voff):
        for t in range(NTT):
            rs = rstd_tile([(H[k], H[k][:, tsl(t)]) for k in range(KC)], t, D)
            for k in range(KC):
                S.op("dve", lambda e: e.scalar_tensor_tensor(HN[k][:, tsl(t)], H[k][:, tsl(t)], VEC[:, voff + k:voff + k + 1], rs[:, :], ALU.mult, ALU.mult),
                     reads=[H[k], VEC, rs], writes=[HN[k]])

    def add_normed(voff):
        for t in range(NTT):
            rs = rstd_tile([(BIGF[k], BIGF[k][:, tsl(t)]) for k in range(KC)], t, D)
            for k in range(KC):
                tm = nxt("tmp", TMPF)
                S.op("dve", lambda e: e.scalar_tensor_tensor(tm[:, :], BIGF[k][:, tsl(t)], VEC[:, voff + k:voff + k + 1], rs[:, :], ALU.mult, ALU.mult),
                     reads=[BIGF[k], VEC, rs], writes=[tm])
                S.op("pool", lambda e: e.tensor_tensor(H[k][:, tsl(t)], H[k][:, tsl(t)], tm[:, :], ALU.add), reads=[H[k], tm], writes=[H[k]])

    for hf in range(TOK // NTH):
        off = hf * NTH
        for k in range(KC):
            S.dma("sp", [(H[k][:, :], hT_d[k, :, off:off + NTH])], writes=[H[k]])
        norm_to_hn(V_PRE)
        for cb in range(KC):
            wb = load_w(wzg_d, KC, cb)
            for t in range(NTT):
                ps = gemm(wb, KC, HN, t)
                sg = nxt("sig", SIG)
                S.op("act", lambda e: e.activation(sg[:, :], ps[:, 0:TT], AF.Silu), reads=[ps], writes=[sg])
                st = nxt("stg", STG)
                S.dma("sp", [(st[:, :], yin_d[1][cb, :, off + t * TT:off + (t + 1) * TT])], writes=[st])
                S.op("dve", lambda e: e.tensor_tensor(BIGF[cb][:, tsl(t)], st[:, :], sg[:, :], ALU.mult), reads=[st, sg], writes=[BIGF[cb]])
        for t in range(NTT):
            for g in range(4):
                rs = rstd_tile([(BIGF[2 * g + i], BIGF[2 * g + i][:, tsl(t)]) for i in range(2)], t, 256)
                for i in range(2):
                    k = 2 * g + i
                    S.op("dve", lambda e: e.scalar_tensor_tensor(BR[1][k][:, tsl(t)], BIGF[k][:, tsl(t)], VEC[:, V_SSD + k:V_SSD + k + 1], rs[:, :], ALU.mult, ALU.mult),
                         reads=[BIGF[k], VEC, rs], writes=[BR[1][k]])
        for cb in range(KC):
            wb = load_w(wzg_d, KC, KC + cb)
            for t in range(NTT):
                ps = gemm(wb, KC, HN, t)
                S.op("act", lambda e: e.activation(BIGF[cb][:, tsl(t)], ps[:, 0:TT], AF.Silu), reads=[ps], writes=[BIGF[cb]])
        for t in range(NTT):
            for g in range(4):
                ycs = []
                for i in range(2):
                    st = nxt("stg", STG)
                    S.dma("sp", [(st[:, :], yin_d[2][2 * g + i, :, off + t * TT:off + (t + 1) * TT])], writes=[st])
                    ycs.append(st)
                for i in range(2):
                    S.op("pe", lambda e: e.matmul(PS_SS[:, 0:TT], CST[:, C_ONE, :], ycs[i][:, :], start=(i == 0), stop=(i == 1)),
                         reads=[CST, ycs[i]], writes=[PS_SS])
                S.op("act", lambda e: e.activation(MEAN[:, :], PS_SS[:, 0:TT], AF.Copy, scale=1.0 / 256.0), reads=[PS_SS], writes=[MEAN])
                for i in range(2):
                    S.op("dve", lambda e: e.tensor_tensor(DD[i][:, :], ycs[i][:, :], MEAN[:, :], ALU.subtract), reads=[ycs[i], MEAN], writes=[DD[i]])
                rs = rstd_tile([(DD[i], DD[i][:, :]) for i in range(2)], t, 256)
                for i in range(2):
                    k = 2 * g + i
                    tm = nxt("tmp", TMPF)
                    S.op("dve", lambda e: e.tensor_tensor(tm[:, :], DD[i][:, :], rs[:, :], ALU.mult), reads=[DD[i], rs], writes=[tm])
                    S.op("dve", lambda e: e.tensor_tensor(BR[2][k][:, tsl(t)], tm[:, :], BIGF[k][:, tsl(t)], ALU.mult), reads=[tm, BIGF[k]], writes=[BR[2][k]])
        for (bi, yi) in ((0, 0), (3, 3)):
            for cb in range(KC):
                for t in range(NTT):
                    st = nxt("stg", STG)
                    S.dma("sp", [(st[:, :], yin_d[yi][cb, :, off + t * TT:off + (t + 1) * TT])], writes=[st])
                    S.op("pool", lambda e: e.tensor_copy(BR[bi][cb][:, tsl(t)], st[:, :]), reads=[st], writes=[BR[bi][cb]])
        for m in range(KC):
            for n in range(4):
                wbg = load_w(wzg_d, KC, 2 * KC + n * KC + m)
                wbu = load_w(wbr_d[n], KC, m)
                for t in range(NTT):
                    psg = gemm(wbg, KC, HN, t)
                    psu = gemm(wbu, KC, BR[n], t)
                    sg = nxt("sig", SIG)
                    S.op("act", lambda e: e.activation(sg[:, :], psg[:, 0:TT], AF.Sigmoid), reads=[psg], writes=[sg])
                    if n == 0:
                        S.op("dve", lambda e: e.tensor_tensor(ACC[t][:, :], sg[:, :], psu[:, 0:TT], ALU.mult), reads=[sg, psu], writes=[ACC[t]])
                    else:
                        tm = nxt("tmp", TMPF)
                        S.op("dve", lambda e: e.tensor_tensor(tm[:, :], sg[:, :], psu[:, 0:TT], ALU.mult), reads=[sg, psu], writes=[tm])
                        S.op("pool", lambda e: e.tensor_tensor(ACC[t][:, :], ACC[t][:, :], tm[:, :], ALU.add), reads=[ACC[t], tm], writes=[ACC[t]])
            for t in range(NTT):
                S.op("pool", lambda e: e.tensor_copy(MG[m][:, tsl(t)], ACC[t][:, :]), reads=[ACC[t]], writes=[MG[m]])
        for cb in range(KC):
            wb = load_w(wout_d, KC, cb)
            for t in range(NTT):
                ps = gemm(wb, KC, MG, t)
                S.op("act", lambda e: e.activation(BIGF[cb][:, tsl(t)], ps[:, 0:TT], AF.Copy), reads=[ps], writes=[BIGF[cb]])
        add_normed(V_POST)
        norm_to_hn(V_FPRE)
        for j in range(22):
            wbg = load_w(wfi_d, KC, j)
            wbu = load_w(wfi_d, KC, 22 + j)
            for t in range(NTT):
                psg = gemm(wbg, KC, HN, t)
                psu = gemm(wbu, KC, HN, t)
                sg = nxt("sig", SIG)
                S.op("act", lambda e: e.activation(sg[:, :], psg[:, 0:TT], AF.Silu), reads=[psg], writes=[sg])
                S.op("dve", lambda e: e.tensor_tensor(HID[j][:, tsl(t)], sg[:, :], psu[:, 0:TT], ALU.mult), reads=[sg, psu], writes=[HID[j]])
        for cb in range(KC):
            wb = load_w(wfo_d, 22, cb)
            for t in range(NTT):
                ps = gemm(wb, 22, HID, t)
                S.op("act", lambda e: e.activation(BIGF[cb][:, tsl(t)], ps[:, 0:TT], AF.Copy), reads=[ps], writes=[BIGF[cb]])
        add_normed(V_FPOST)
        for k in range(KC):
            S.dma("pool", [(hout_d[k, :, off:off + NTH], H[k][:, :])], reads=[H[k]], writes=[HOUT])
    S.finish([HOUT])
    return nc, S


def prep_T(h, ya, yb, yc, yd, w_in, w_branch, w_out, w_ffn_in, w_ffn_out, npre, ssdn, npost, nfpre, nfpost):
    cs = _consts()
    O = IN_OFFS
    wzg = np.ascontiguousarray(np.concatenate([w_in[:, O[3]:O[3] + 1024], w_in[:, O[9]:O[9] + 1024], w_in[:, O[13]:O[13] + 4096]], axis=1))
    vecs = np.ascontiguousarray(np.concatenate([colvec(v) for v in (npre, ssdn, npost, nfpre, nfpost)], axis=1))
    maps = []
    for c in range(NCORE):
        sl = slice(c * TOK, (c + 1) * TOK)
        maps.append({
            "hT": fm(h[sl]), "ya": fm(ya[sl]), "yb": fm(yb[sl]), "yc": fm(yc[sl]), "yd": fm(yd[sl]),
            "wzg": wzg, "wbr": np.ascontiguousarray(w_branch), "wout": np.ascontiguousarray(w_out),
            "wfi": np.ascontiguousarray(w_ffn_in), "wfo": np.ascontiguousarray(w_ffn_out),
            "vecs": vecs, "cst": cs["cst"],
        })
    return maps


def gather_T(res):
    return np.concatenate([r["hout"].reshape(D, TOK).T for r in res], axis=0)


_PROG = {}


def kernel(x, meta, w_in, conv_a, ssd_conv_w, ssd_conv_b, ssd_dt_bias, ssd_a_log, ssd_d, ssd_norm,
           w_branch, w_out, w_ffn_in, w_ffn_out, norm_mix_pre, norm_mix_post, norm_ffn_pre, norm_ffn_post):
    f = lambda a: np.asarray(a, dtype=np.float32)
    x, meta = f(x), f(meta)
    h = np.concatenate([np.zeros((L - 16 - x.shape[1], D), np.float32), meta, x[0]], axis=0)
    if "H" not in _PROG:
        _PROG["H"] = build_H()[0]
        _PROG["T"] = build_T()[0]
    cores = list(range(NCORE))
    for l in range(2):
        mH = prep_H(h, f(w_in[l]), f(conv_a[l]), f(ssd_conv_w[l]), f(ssd_conv_b[l]), f(ssd_dt_bias[l]),
                    f(ssd_a_log[l]), f(ssd_d[l]), f(norm_mix_pre[l]))
        rH = run_bass_kernel_spmd(_PROG["H"], mH, core_ids=cores)
        ya, yb, yc, yd = gather_H(rH.results)
        mT = prep_T(h, ya, yb, yc, yd, f(w_in[l]), f(w_branch[l]), f(w_out[l]), f(w_ffn_in[l]), f(w_ffn_out[l]),
                    f(norm_mix_pre[l]), f(ssd_norm[l]), f(norm_mix_post[l]), f(norm_ffn_pre[l]), f(norm_ffn_post[l]))
        rT = run_bass_kernel_spmd(_PROG["T"], mT, core_ids=cores)
        h = gather_T(rT.results)
    return np.ascontiguousarray(h[128:][None]).astype(np.float32)
```

```python
import math
import numpy as np
import ml_dtypes
import concourse.bass as bass
import concourse.mybir as mybir
from concourse.bass_utils import run_bass_kernel_spmd

F32 = mybir.dt.float32
BF16 = mybir.dt.bfloat16
AF = mybir.ActivationFunctionType
ALU = mybir.AluOpType
EPOCH = 24000

L = 16512
NCH = 129
D = 1024
KC = 8
EPS = 1e-6
NCORE = 8
TOK = L // NCORE
HALF = TOK // 2
TT = 344
DFF = 2816


class Buf:
    __slots__ = ("t", "name", "w", "r", "dsem", "dcnt", "excl")

    def __init__(self, t, name, excl=False):
        self.t = t
        self.name = name
        self.w = None
        self.r = []
        self.dsem = None
        self.dcnt = 0
        self.excl = excl

    def __getitem__(self, idx):
        return self.t[idx]


class Sched:
    def __init__(self, nc):
        self.nc = nc
        self.engs = {"pe": nc.tensor, "act": nc.scalar, "dve": nc.vector,
                     "pool": nc.gpsimd, "sp": nc.sync}
        self.sems = {k: [] for k in self.engs}
        self.cnt = {k: 0 for k in self.engs}
        self.seen = {k: {} for k in self.engs}
        self.nsem = 0
        self.ninst = 0
        self.dsems = []
        self.sb_off = 16640
        self.nps = 0

    def new_sem(self, name):
        self.nsem += 1
        return self.nc.alloc_semaphore(name=f"{name}_{self.nsem}")

    def sb(self, name, shape, dt):
        nbytes = int(np.prod(shape[1:])) * (4 if dt == F32 else 2)
        nbytes = (nbytes + 63) // 64 * 64
        t = self.nc.alloc_sbuf_tensor_at(f"{name}_{self.ninst}_{self.sb_off}", list(shape), dt, offset=self.sb_off)
        self.sb_off += nbytes
        assert self.sb_off <= 229376, ("sbuf overflow", name, self.sb_off)
        return Buf(t, name)

    def ps(self, name, shape, dt=F32):
        self.nps += 1
        assert self.nps <= 8
        return Buf(self.nc.alloc_psum_tensor(name, list(shape), dt), name, excl=True)

    def _deps(self, reads, writes):
        deps = []
        w2 = list(writes)
        for b in reads:
            if b.excl:
                w2.append(b)
                continue
            if b.w is not None:
                deps.append(b.w)
        for b in w2:
            if b.w is not None:
                deps.append(b.w)
            deps.extend(b.r)
        return deps, w2

    def _wait(self, ek, deps):
        eng = self.engs[ek]
        seen = self.seen[ek]
        best = {}
        for (s, v) in deps:
            if seen.get(s, 0) >= v:
                continue
            if best.get(s, 0) < v:
                best[s] = v
        for s, v in best.items():
            eng.wait_ge(s, v)
            seen[s] = v

    def _record(self, dep, reads, writes):
        for b in writes:
            b.w = dep
            b.r = []
        for b in reads:
            b.r.append(dep)
            if len(b.r) > 8:
                m = {}
                for (s, v) in b.r:
                    if m.get(s, 0) < v:
                        m[s] = v
                b.r = list(m.items())

    def op(self, ek, fn, reads=(), writes=()):
        deps, w2 = self._deps(reads, writes)
        self._wait(ek, deps)
        n = self.cnt[ek]
        ep, off = divmod(n, EPOCH)
        while len(self.sems[ek]) <= ep:
            self.sems[ek].append(self.new_sem(ek))
        sem = self.sems[ek][ep]
        fn(self.engs[ek]).then_inc(sem, 1)
        self.cnt[ek] = n + 1
        dep = (sem, off + 1)
        self._record(dep, [b for b in reads if not b.excl], w2)
        self.ninst += 1
        return dep

    def dma(self, qk, pairs, reads=(), writes=(), sembuf=None):
        deps, w2 = self._deps(reads, writes)
        sb = sembuf if sembuf is not None else (list(reads) + list(writes))[0]
        if sb.dsem is None:
            sb.dsem = self.new_sem("d")
            self.dsems.append(sb)
        if sb.dcnt > 0:
            deps.append((sb.dsem, sb.dcnt))
        self._wait(qk, deps)
        eng = self.engs[qk]
        for (o, i) in pairs:
            eng.dma_start(out=o, in_=i).then_inc(sb.dsem, 16)
            sb.dcnt += 16
            self.ninst += 1
        dep = (sb.dsem, sb.dcnt)
        self._record(dep, [b for b in reads if not b.excl], w2)
        return dep

    def cc(self, kind, in_ap, out_ap, groups, reads=(), writes=()):
        deps, w2 = self._deps(reads, writes)
        sb = list(writes)[0]
        if sb.dsem is None:
            sb.dsem = self.new_sem("c")
            self.dsems.append(sb)
        if sb.dcnt > 0:
            deps.append((sb.dsem, sb.dcnt))
        self._wait("pool", deps)
        self.engs["pool"].collective_compute(kind, ALU.bypass, replica_groups=groups, ins=[in_ap], outs=[out_ap]).then_inc(sb.dsem, 16)
        sb.dcnt += 16
        self.ninst += 1
        dep = (sb.dsem, sb.dcnt)
        self._record(dep, [b for b in reads if not b.excl], w2)
        return dep

    def barrier(self):
        deps = []
        for k in self.engs:
            n = self.cnt[k]
            if n == 0:
                continue
            ep, off = divmod(n - 1, EPOCH)
            deps.append((self.sems[k][ep], off + 1))
        for b in self.dsems:
            deps.append((b.dsem, b.dcnt))
        for k in self.engs:
            self._wait(k, deps)

    def finish(self, bufs):
        deps = []
        for b in bufs:
            if b.w is not None:
                deps.append(b.w)
        self._wait("sp", deps)


C_ID, C_TRI, C_SLT, C_ONE, C_U, C_LM, C_MS = range(7)


def run_threads(gens):
    gens = list(gens)
    while gens:
        for g in list(gens):
            try:
                next(g)
            except StopIteration:
                gens.remove(g)


def host_consts():
    i = np.arange(128)
    c = np.zeros((128, 7, 128), np.float32)
    c[:, C_ID, :] = np.eye(128)
    c[:, C_TRI, :] = (i[:, None] <= i[None, :])
    c[:, C_SLT, :] = (i[:, None] > i[None, :])
    c[:, C_ONE, :] = 1.0
    c[:, C_U, :] = (i[:, None] >= i[None, :])
    c[:, C_LM, :] = (i[:, None] < i[None, :])
    c[:, C_MS, :] = (i[None, :] > i[:, None])
    return c


def rope_tables():
    half = 128
    inv = np.power(np.float32(10000.0), -(np.arange(half, dtype=np.float32) / np.float32(half))).astype(np.float32)
    pos = np.arange(L, dtype=np.float32)
    ang = (pos[None, :] * inv[:, None]).astype(np.float32)
    return np.cos(ang.astype(np.float64)).astype(np.float32), np.sin(ang.astype(np.float64)).astype(np.float32)


NFM = 12
NTM = 258


def build_H():
    nc = bass.Bass("TRN2", target_bir_lowering=False)
    S = Sched(nc)
    ei = lambda n, s, dt=F32: nc.dram_tensor(n, list(s), dt, kind="ExternalInput")
    eo = lambda n, s, dt=F32: nc.dram_tensor(n, list(s), dt, kind="ExternalOutput")
    hT_d = ei("hT", [KC, 128, L])
    npre_d = ei("npre", [128, KC])
    wfm_d = ei("wfm", [KC, 128, NFM * 128])
    wtm_d = ei("wtm", [KC, 128, NTM])
    cva_d = ei("cva", [128, 3])
    scw_d = ei("scw", [128, 12])
    scb_d = ei("scb", [128, 3])
    dtb_d = ei("dtb", [128, 2])
    alog_d = ei("alog", [128, 2])
    drow_d = ei("drow", [128, 128])
    lgam_d = ei("lgam", [128, 1])
    cos_d = ei("cos", [128, L])
    sin_d = ei("sin", [128, L])
    cst_d = ei("cst", [128, 7, 128])
    yaT_d = eo("yaT", [128, L])
    yb_d = eo("yb", [L, 128])
    yc_d = eo("yc", [L, 128])
    ydT_d = eo("ydT", [128, L])
    scfm_d = nc.dram_tensor("scfm", [10, 128, L], F32)
    rv_d = nc.dram_tensor("rvs", [L, 128], BF16)
    OUT = [Buf(yaT_d, "yaTd"), Buf(yb_d, "ybd"), Buf(yc_d, "ycd"), Buf(ydT_d, "ydTd")]
    SCFM = [[Buf(scfm_d, f"scfm{b}_{i}") for i in range(33)] for b in range(10)]
    RVD = [Buf(rv_d, f"rvd{c}") for c in range(NCH)]

    CST = S.sb("cst", [128, 7, 128], F32)
    CSTB = S.sb("cstb", [128, 7, 128], BF16)
    SMALL = S.sb("small", [128, 32], F32)
    DROW = S.sb("drow", [128, 128], F32)
    DT = S.sb("dt", [128, 2 * NCH], F32)
    off_persist = S.sb_off
    QT = [S.sb(f"qt{i}", [128, 512], BF16) for i in range(33)]
    KT = [S.sb(f"kt{i}", [128, 512], BF16) for i in range(33)]
    SV = [S.sb(f"sv{i}", [128, 4, 128], BF16) for i in range(33)]
    off_qkv = S.sb_off
    PSB = [S.ps(f"ps{i}", [128, 512]) for i in range(7)]
    PSH = S.ps("psh", [128, 1024], BF16)
    base_off = S.sb_off

    S.dma("sp", [(CST[:, :, :], cst_d[:, :, :])], writes=[CST])
    S.dma("sp", [(SMALL[:, 0:3], cva_d[:, :]), (SMALL[:, 3:15], scw_d[:, :]), (SMALL[:, 15:18], scb_d[:, :]),
                 (SMALL[:, 18:20], dtb_d[:, :]), (SMALL[:, 20:22], alog_d[:, :]), (SMALL[:, 22:23], lgam_d[:, :]),
                 (SMALL[:, 24:32], npre_d[:, :])], writes=[SMALL])
    S.dma("sp", [(DROW[:, :], drow_d[:, :])], writes=[DROW])
    S.op("dve", lambda e: e.tensor_copy(CSTB[:, :, :], CST[:, :, :]), reads=[CST], writes=[CSTB])

    def tile_rng(i):
        c0 = i * 512
        return c0, min(512, L - c0)

    WFM = S.sb("wfm", [128, KC, NFM * 128], BF16)
    WTM = S.sb("wtm", [128, KC, NTM], BF16)
    WST = [S.sb(f"wst{i}", [128, NFM * 128], F32) for i in range(1)]
    for k in range(KC):
        st = WST[0]
        S.dma("sp", [(st[:, :], wfm_d[k, :, :])], writes=[st])
        S.op("pool", lambda e: e.tensor_copy(WFM[:, k, :], st[:, :]), reads=[st], writes=[WFM])
    for k in range(KC):
        st = WST[0]
        S.dma("sp", [(st[:, 0:NTM], wtm_d[k, :, :])], writes=[st])
        S.op("pool", lambda e: e.tensor_copy(WTM[:, k, :], st[:, 0:NTM]), reads=[st], writes=[WTM])
    HT = [S.sb(f"ht{i}", [128, KC, 512], F32) for i in range(2)]
    HN = [S.sb(f"hn{i}", [128, KC, 512], BF16) for i in range(2)]
    SQ = [S.sb(f"sq{i}", [128, 512], F32) for i in range(2)]
    R1 = S.sb("r1", [128, 512], F32)
    RS = S.sb("rs", [128, 512], F32)
    STG = [S.sb(f"stg{i}", [128, 512], F32) for i in range(3)]
    STV = [S.sb(f"stv{i}", [128, 128], BF16) for i in range(2)]
    stg_i = 0
    SCALE_Q = 1.0 / math.sqrt(128.0)
    for ti in range(33):
        c0, n = tile_rng(ti)
        ht = HT[ti % 2]
        hn = HN[ti % 2]
        S.dma("sp", [(ht[:, k, 0:n], hT_d[k, :, c0:c0 + n]) for k in range(KC)], writes=[ht])
        pss = PSB[0]
        for k in range(KC):
            sq = SQ[k % 2]
            S.op("pool", lambda e: e.tensor_tensor(sq[:, 0:n], ht[:, k, 0:n], ht[:, k, 0:n], ALU.mult), reads=[ht], writes=[sq])
            S.op("pe", lambda e: e.matmul(pss[:, 0:n], CST[:, C_ONE, :], sq[:, 0:n], start=(k == 0), stop=(k == KC - 1)),
                 reads=[CST, sq], writes=[pss])
        S.op("act", lambda e: e.activation(R1[:, 0:n], pss[:, 0:n], AF.Sqrt, bias=EPS, scale=1.0 / D), reads=[pss], writes=[R1])
        S.op("dve", lambda e: e.reciprocal(RS[:, 0:n], R1[:, 0:n]), reads=[R1], writes=[RS])
        for k in range(KC):
            S.op("dve", lambda e: e.scalar_tensor_tensor(hn[:, k, 0:n], ht[:, k, 0:n], SMALL[:, 24 + k:25 + k], RS[:, 0:n], ALU.mult, ALU.mult),
                 reads=[ht, SMALL, RS], writes=[hn])
        for blk in range(NFM):
            ps = PSB[1 + blk % 2]
            for k in range(KC):
                S.op("pe", lambda e: e.matmul(ps[:, 0:n], WFM[:, k, blk * 128:(blk + 1) * 128], hn[:, k, 0:n], start=(k == 0), stop=(k == KC - 1)),
                     reads=[WFM, hn], writes=[ps])
            if blk < 10:
                st = STG[stg_i % 3]
                stg_i += 1
                S.op("act", lambda e: e.activation(st[:, 0:n], ps[:, 0:n], AF.Copy), reads=[ps], writes=[st])
                S.dma("pool", [(scfm_d[blk, :, c0:c0 + n], st[:, 0:n])], reads=[st], writes=[SCFM[blk][ti]])
            elif blk == 10:
                S.op("act", lambda e: e.activation(QT[ti][:, 0:n], ps[:, 0:n], AF.Copy, scale=SCALE_Q), reads=[ps], writes=[QT[ti]])
            else:
                S.op("dve", lambda e: e.tensor_copy(KT[ti][:, 0:n], ps[:, 0:n]), reads=[ps], writes=[KT[ti]])
        for sub in range(n // 128):
            ch = ti * 4 + sub
            ps = PSB[3 + sub % 2]
            for k in range(KC):
                S.op("pe", lambda e: e.matmul(ps[:, 0:NTM], hn[:, k, sub * 128:(sub + 1) * 128], WTM[:, k, :], start=(k == 0), stop=(k == KC - 1)),
                     reads=[hn, WTM], writes=[ps])
            stv = STV[ch % 2]
            S.op("dve", lambda e: e.tensor_copy(stv[:, :], ps[:, 0:128]), reads=[ps], writes=[stv])
            if ch == 0:
                S.op("dve", lambda e: e.memset(stv[0:112, :], 0.0), writes=[stv])
            S.dma("pool", [(rv_d[ch * 128:(ch + 1) * 128, :], stv[:, :])], reads=[stv], writes=[RVD[ch]])
            S.op("act", lambda e: e.activation(SV[ti][:, sub, :], ps[:, 128:256], AF.Copy), reads=[ps], writes=[SV[ti]])
            for hh in range(2):
                S.op("dve", lambda e: e.tensor_copy(DT[:, hh * NCH + ch:hh * NCH + ch + 1], ps[:, 256 + hh:257 + hh]), reads=[ps], writes=[DT])

    S.barrier()
    S.sb_off = off_qkv

    def sb_thread(tiles, z, acc, outp, tg):
        EB = [S.sb(f"eb{tg}{i}", [128, 512], F32) for i in range(2)]
        SPB = [S.sb(f"spb{tg}{i}", [128, 512], BF16) for i in range(2)]
        ECB = [S.sb(f"ecb{tg}{i}", [128, 512], F32) for i in range(2)]
        WB = [S.sb(f"wb{tg}{i}", [128, 512], BF16) for i in range(2)]
        OS = [S.sb(f"os{tg}{i}", [128, 512], F32) for i in range(2)]
        seq = []
        for ti in tiles:
            c0, n = tile_rng(ti)
            i0 = ti * 4
            nb = n // 128
            for j in range(i0 + nb - 1, -1, -1):
                seq.append((ti, j, c0, n, max(0, j - i0) * 128, j == i0 + nb - 1))

        def qk(idx):
            ti, j, c0, n, cs, first = seq[idx]
            kt = KT[j // 4]
            ko = (j % 4) * 128
            S.op("pe", lambda e: e.matmul(z[:, cs:n], kt[:, ko:ko + 128], QT[ti][:, cs:n], start=True, stop=True),
                 reads=[kt, QT[ti]], writes=[z])

        qk(0)
        yield
        nout = 0
        for idx, (ti, j, c0, n, cs, first) in enumerate(seq):
            i0 = ti * 4
            eb, spb, ecb, wb = EB[idx % 2], SPB[idx % 2], ECB[idx % 2], WB[idx % 2]
            S.op("act", lambda e: e.activation(eb[:, cs:n], z[:, cs:n], AF.Exp), reads=[z], writes=[eb])
            if j >= i0:
                S.op("dve", lambda e: e.tensor_tensor(eb[:, cs:cs + 128], eb[:, cs:cs + 128], CST[:, C_MS, :], ALU.mult),
                     reads=[eb, CST], writes=[eb])
            if j == 0:
                S.op("dve", lambda e: e.memset(eb[0:112, cs:n], 0.0), writes=[eb])
            yield
            if idx + 1 < len(seq):
                qk(idx + 1)
            S.op("act", lambda e: e.activation(spb[:, cs:n], eb[:, cs:n], AF.Ln, bias=1.0), reads=[eb], writes=[spb])
            yield
            S.op("pe", lambda e: e.matmul(acc[:, cs:n], CSTB[:, C_U, :], spb[:, cs:n], start=first, stop=False, skip_group_check=True),
                 reads=[CSTB, spb], writes=[acc])
            yield
            S.op("act", lambda e: e.activation(ecb[:, cs:n], acc[:, cs:n], AF.Exp, scale=-1.0), reads=[acc], writes=[ecb])
            yield
            S.op("pe", lambda e: e.matmul(acc[:, cs:n], CSTB[:, C_LM, :], spb[:, cs:n], start=False, stop=False, skip_group_check=True),
                 reads=[CSTB, spb], writes=[acc])
            S.op("dve", lambda e: e.tensor_tensor(wb[:, cs:n], eb[:, cs:n], ecb[:, cs:n], ALU.mult), reads=[eb, ecb], writes=[wb])
            yield
            svb = SV[j // 4]
            S.op("pe", lambda e: e.matmul(outp[:, cs:n], svb[:, j % 4, :], wb[:, cs:n], start=first, stop=(j == 0), skip_group_check=True),
                 reads=[svb, wb], writes=[outp])
            if j == 0:
                os_ = OS[nout % 2]
                nout += 1
                S.op("dve", lambda e: e.tensor_copy(os_[:, 0:n], outp[:, 0:n]), reads=[outp], writes=[os_])
                S.dma("pool", [(ydT_d[:, c0:c0 + n], os_[:, 0:n])], reads=[os_], writes=[OUT[3]])
            yield

    PL = PSB[6]

    def lin_thread():
        lin_base = S.sb_off
        CW = 512
        CIN = [[S.sb(f"cin{b}_{i}", [128, CW], F32) for i in range(2)] for b in range(3)]
        UB = [S.sb(f"ub{i}", [128, CW + 2], F32) for i in range(2)]
        ACCB = S.sb("accb", [128, CW], F32)
        YO = [S.sb(f"yo{i}", [128, CW], F32) for i in range(2)]
        ntile = (L + CW - 1) // CW
        for t in range(ntile):
            c0 = t * CW
            n = min(CW, L - c0)
            cb_, cc_, cx_ = CIN[0][t % 2], CIN[1][t % 2], CIN[2][t % 2]
            S.dma("sp", [(cb_[:, 0:n], scfm_d[0, :, c0:c0 + n])], writes=[cb_])
            S.dma("sp", [(cc_[:, 0:n], scfm_d[1, :, c0:c0 + n])], writes=[cc_])
            S.dma("sp", [(cx_[:, 0:n], scfm_d[2, :, c0:c0 + n])], writes=[cx_])
            yield
            u = UB[t % 2]
            S.op("dve", lambda e: e.tensor_tensor(u[:, 2:2 + n], cc_[:, 0:n], cx_[:, 0:n], ALU.mult), reads=[cc_, cx_], writes=[u])
            if t == 0:
                S.op("dve", lambda e: e.memset(u[:, 0:2 + 112], 0.0), writes=[u])
            else:
                up = UB[(t - 1) % 2]
                S.op("pool", lambda e: e.tensor_copy(u[:, 0:2], up[:, CW:CW + 2]), reads=[up], writes=[u])
            yield
            S.op("dve", lambda e: e.tensor_single_scalar(ACCB[:, 0:n], u[:, 0:n], SMALL[:, 0:1], ALU.mult), reads=[u, SMALL], writes=[ACCB])
            for i in (1, 2):
                S.op("dve", lambda e: e.scalar_tensor_tensor(ACCB[:, 0:n], u[:, i:i + n], SMALL[:, i:i + 1], ACCB[:, 0:n], ALU.mult, ALU.add),
                     reads=[u, SMALL, ACCB], writes=[ACCB])
            yield
            yo = YO[t % 2]
            S.op("pool", lambda e: e.tensor_tensor(yo[:, 0:n], ACCB[:, 0:n], cb_[:, 0:n], ALU.mult), reads=[ACCB, cb_], writes=[yo])
            S.dma("pool", [(yaT_d[:, c0:c0 + n], yo[:, 0:n])], reads=[yo], writes=[OUT[0]])
            yield

        S.barrier()
        S.sb_off = lin_base
        RA = [S.sb(f"ra{i}", [128, 128], F32) for i in range(2)]
        SEG = [S.sb(f"seg{i}", [128, 128], F32) for i in range(3)]
        SEGM = [S.sb(f"segm{i}", [128, 128], F32) for i in range(3)]
        PB = [S.sb(f"pb{i}", [128, 128], BF16) for i in range(2)]
        TMPY = [S.sb(f"tmpy{i}", [128, 128], F32) for i in range(2)]
        YB = [S.sb(f"yb{i}", [128, 128], F32) for i in range(2)]
        cnt = {"d": 0, "p": 0}

        def decay(a_ap, a_buf, dreg, scale=None, slot=None):
            i = cnt["d"]
            cnt["d"] += 1
            ra = RA[i % 2]
            sg = SEG[slot if slot is not None else i % 2]
            sm = SEGM[slot if slot is not None else i % 2]
            S.op("dve", lambda e: e.tensor_single_scalar(ra[:, :], CST[:, C_TRI, :], a_ap, ALU.mult), reads=[CST, a_buf], writes=[ra])
            S.op("pe", lambda e: e.matmul(PL[:, dreg], CST[:, C_SLT, :], ra[:, :], start=True, stop=True), reads=[CST, ra], writes=[PL])
            S.op("act", lambda e: e.activation(sg[:, :], PL[:, dreg], AF.Exp), reads=[PL], writes=[sg])
            if scale is None:
                S.op("pool", lambda e: e.tensor_tensor(sm[:, :], sg[:, :], CST[:, C_TRI, :], ALU.mult), reads=[sg, CST], writes=[sm])
            else:
                S.op("dve", lambda e: e.scalar_tensor_tensor(sm[:, :], sg[:, :], scale, CST[:, C_TRI, :], ALU.mult, ALU.mult), reads=[sg, CST], writes=[sm])
            return sg, sm

        def lin_head(q_list, kd_list, v_ap, v_buf, sm, ecol_ap, ecol_buf, etot_ap, etot_buf, S_list, Sbf_list, ybuf, y0, dv, rst, ryd, ryo, rs_):
            i = cnt["p"]
            cnt["p"] += 1
            pb = PB[i % 2]
            tm = TMPY[i % 2]
            S.op("dve", lambda e: e.tensor_tensor(pb[:, :], PL[:, rst], sm[:, :], ALU.mult), reads=[PL, sm], writes=[pb])
            S.op("pe", lambda e: e.matmul(PL[:, ryd], pb[:, :], v_ap, start=True, stop=True), reads=[pb, v_buf], writes=[PL])
            nk = len(q_list)
            for kk in range(nk):
                qa, qb = q_list[kk]
                S.op("pe", lambda e: e.matmul(PL[:, ryo], qa, Sbf_list[kk][:, 0:dv], start=(kk == 0), stop=(kk == nk - 1), skip_group_check=True),
                     reads=[qb, Sbf_list[kk]], writes=[PL])
            S.op("act", lambda e: e.activation(tm[:, 0:dv], PL[:, ryd], AF.Copy), reads=[PL], writes=[tm])
            S.op("dve", lambda e: e.scalar_tensor_tensor(ybuf[:, y0:y0 + dv], PL[:, ryo], ecol_ap, tm[:, 0:dv], ALU.mult, ALU.add),
                 reads=[PL, ecol_buf, tm], writes=[ybuf])
            for kk in range(nk):
                ka, kb = kd_list[kk]
                S.op("pe", lambda e: e.matmul(PL[:, rs_], ka, v_ap, start=True, stop=True), reads=[kb, v_buf], writes=[PL])
                S.op("dve", lambda e: e.scalar_tensor_tensor(S_list[kk][:, 0:dv], S_list[kk][:, 0:dv], etot_ap, PL[:, rs_], ALU.mult, ALU.add),
                     reads=[S_list[kk], etot_buf, PL], writes=[S_list[kk]])
                S.op("pool", lambda e: e.tensor_copy(Sbf_list[kk][:, 0:dv], S_list[kk][:, 0:dv]), reads=[S_list[kk]], writes=[Sbf_list[kk]])

        off_shared = S.sb_off
        R_ST, R_D, R_X, R_YD, R_YO, R_S = slice(0, 128), slice(128, 256), slice(256, 384), slice(384, 448), slice(448, 512), slice(128, 192)
        TMPD = S.sb("tmpd", [128, 2 * NCH], F32)
        DTS = S.sb("dts", [128, 2 * NCH], F32)
        AALL = S.sb("aall", [128, 2 * NCH], F32)
        ECOL = S.sb("ecol", [128, 2 * NCH], F32)
        ETOT = S.sb("etot", [128, 2 * NCH], F32)
        NA = S.sb("na", [128, 2], F32)
        for h in range(2):
            sl = slice(h * NCH, (h + 1) * NCH)
            S.op("act", lambda e: e.activation(TMPD[:, sl], DT[:, sl], AF.Exp, bias=SMALL[:, 18 + h:19 + h]), reads=[DT, SMALL], writes=[TMPD])
            S.op("act", lambda e: e.activation(DTS[:, sl], TMPD[:, sl], AF.Ln, bias=1.0), reads=[TMPD], writes=[DTS])
        S.op("act", lambda e: e.activation(NA[:, :], SMALL[:, 20:22], AF.Exp), reads=[SMALL], writes=[NA])
        S.op("dve", lambda e: e.tensor_single_scalar(NA[:, :], NA[:, :], -1.0, ALU.mult), reads=[NA], writes=[NA])
        yield
        for h in range(2):
            sl = slice(h * NCH, (h + 1) * NCH)
            S.op("dve", lambda e: e.tensor_single_scalar(AALL[:, sl], DTS[:, sl], NA[:, h:h + 1], ALU.mult), reads=[DTS, NA], writes=[AALL])
        S.op("pe", lambda e: e.matmul(PL[:, 0:2 * NCH], CST[:, C_TRI, :], AALL[:, :], start=True, stop=True), reads=[CST, AALL], writes=[PL])
        S.op("act", lambda e: e.activation(ECOL[:, :], PL[:, 0:2 * NCH], AF.Exp), reads=[PL], writes=[ECOL])
        S.op("pe", lambda e: e.matmul(PL[:, 0:2 * NCH], CST[:, C_ONE, :], AALL[:, :], start=True, stop=True), reads=[CST, AALL], writes=[PL])
        S.op("act", lambda e: e.activation(ETOT[:, :], PL[:, 0:2 * NCH], AF.Exp), reads=[PL], writes=[ETOT])
        yield

        XR = [[S.sb(f"xr{b}_{i}", [128, 515], F32) for i in range(2)] for b in range(3)]
        ACS = S.sb("acs", [128, 512], F32)
        XS = [S.sb(f"xs{i}", [128, 512], F32) for i in range(2)]
        BTt = [S.sb(f"btt{i}", [128, 512], BF16) for i in range(2)]
        CTt = [S.sb(f"ctt{i}", [128, 512], BF16) for i in range(2)]
        XDT = [S.sb(f"xdt{i}", [128, 64], BF16) for i in range(2)]
        KD = [S.sb(f"kd{i}", [128, 128], BF16) for i in range(4)]
        TMP2 = S.sb("tmp2", [128, 128], F32)
        SST = [S.sb(f"sst{i}", [128, 64], F32) for i in range(2)]
        SSB = [S.sb(f"ssb{i}", [128, 64], BF16) for i in range(2)]
        for h in range(2):
            S.op("dve", lambda e: e.memset(SST[h][:, :], 0.0), writes=[SST[h]])
            S.op("dve", lambda e: e.memset(SSB[h][:, :], 0.0), writes=[SSB[h]])
        for ti in range(33):
            c0, n = tile_rng(ti)
            outs = [XS[ti % 2], BTt[ti % 2], CTt[ti % 2]]
            for b in range(3):
                xr = XR[b][ti % 2]
                if ti == 0:
                    S.dma("sp", [(xr[:, 3:3 + n], scfm_d[3 + b, :, 0:n])], writes=[xr])
                    S.op("dve", lambda e: e.memset(xr[:, 0:3 + 112], 0.0), writes=[xr])
                else:
                    S.dma("sp", [(xr[:, 0:3 + n], scfm_d[3 + b, :, c0 - 3:c0 + n])], writes=[xr])
                w0 = 3 + 4 * b
                S.op("dve", lambda e: e.tensor_single_scalar(ACS[:, 0:n], xr[:, 0:n], SMALL[:, w0:w0 + 1], ALU.mult), reads=[xr, SMALL], writes=[ACS])
                yield
                for i in (1, 2, 3):
                    S.op("dve", lambda e: e.scalar_tensor_tensor(ACS[:, 0:n], xr[:, i:i + n], SMALL[:, w0 + i:w0 + i + 1], ACS[:, 0:n], ALU.mult, ALU.add),
                         reads=[xr, SMALL, ACS], writes=[ACS])
                S.op("act", lambda e: e.activation(outs[b][:, 0:n], ACS[:, 0:n], AF.Silu, bias=SMALL[:, 15 + b:16 + b]), reads=[ACS, SMALL], writes=[outs[b]])
                yield
            xs, bt, ct = outs
            if ti == 0:
                S.op("dve", lambda e: e.memset(xs[:, 0:112], 0.0), writes=[xs])
            for sub in range(n // 128):
                c = ti * 4 + sub
                cs_ = slice(sub * 128, (sub + 1) * 128)
                S.op("pe", lambda e: e.matmul(PL[:, R_ST], bt[:, cs_], ct[:, cs_], start=True, stop=True), reads=[bt, ct], writes=[PL])
                S.op("pe", lambda e: e.transpose(PSH[:, 0:128], bt[:, cs_], CSTB[:, C_ID, :]), reads=[bt, CSTB], writes=[PSH])
                S.op("pe", lambda e: e.transpose(PL[:, R_X], xs[:, cs_], CST[:, C_ID, :]), reads=[xs, CST], writes=[PL])
                yield
                yb_ = YB[c % 2]
                for h in range(2):
                    col = h * NCH + c
                    sg, sm = decay(AALL[:, col:col + 1], AALL, R_D)
                    yield
                    xdt = XDT[h]
                    kd = KD[h]
                    S.op("dve", lambda e: e.tensor_single_scalar(xdt[:, :], PL[:, 256 + 64 * h:256 + 64 * h + 64], DTS[:, col:col + 1], ALU.mult),
                         reads=[PL, DTS], writes=[xdt])
                    S.op("act", lambda e: e.activation(kd[:, :], PSH[:, 0:128], AF.Copy, scale=sg[:, 127:128]), reads=[PSH, sg], writes=[kd])
                    yield
                    lin_head([(ct[:, cs_], ct)], [(kd[:, :], kd)], xdt[:, :], xdt, sm, ECOL[:, col:col + 1], ECOL,
                             ETOT[:, col:col + 1], ETOT, [SST[h]], [SSB[h]], yb_, 64 * h, 64, R_ST, R_YD, R_YO, R_S)
                    yield
                S.op("dve", lambda e: e.tensor_tensor(TMP2[:, :], PL[:, R_X], DROW[:, :], ALU.mult), reads=[PL, DROW], writes=[TMP2])
                S.op("pool", lambda e: e.tensor_tensor(yb_[:, :], yb_[:, :], TMP2[:, :], ALU.add), reads=[yb_, TMP2], writes=[yb_])
                S.dma("pool", [(yb_d[c * 128:(c + 1) * 128, :], yb_[:, :])], reads=[yb_], writes=[OUT[1]])
                yield

        S.barrier()
        S.sb_off = off_shared
        Q_ST, Q_YD, Q_YO, Q_S = slice(0, 128), slice(128, 256), slice(256, 384), slice(384, 512)
        RIN = [[S.sb(f"rin{b}_{i}", [128, 512], F32) for i in range(2)] for b in range(6)]
        RT = [S.sb(f"rt{i}", [128, 512], F32) for i in range(4)]
        QR = [[S.sb(f"qr{k}_{i}", [128, 512], BF16) for i in range(2)] for k in range(2)]
        KR = [[S.sb(f"kr{k}_{i}", [128, 512], BF16) for i in range(2)] for k in range(2)]
        RVT = [S.sb(f"rvt{i}", [128, 128], BF16) for i in range(3)]
        RST = [S.sb(f"rst{i}", [128, 128], F32) for i in range(2)]
        RSB = [S.sb(f"rsb{i}", [128, 128], BF16) for i in range(2)]
        KD = [S.sb(f"rkd{i}", [128, 128], BF16) for i in range(4)]
        RC = S.sb("rc", [128, 4], F32)
        for k in range(2):
            S.op("dve", lambda e: e.memset(RST[k][:, :], 0.0), writes=[RST[k]])
            S.op("dve", lambda e: e.memset(RSB[k][:, :], 0.0), writes=[RSB[k]])
        sgR, smR = decay(SMALL[:, 22:23], SMALL, Q_YD, scale=1.0 / 16.0, slot=2)
        S.op("pe", lambda e: e.matmul(PL[:, 256:257], CST[:, C_TRI, :], SMALL[:, 22:23], start=True, stop=True), reads=[CST, SMALL], writes=[PL])
        S.op("act", lambda e: e.activation(RC[:, 0:1], PL[:, 256:257], AF.Exp), reads=[PL], writes=[RC])
        S.op("pe", lambda e: e.matmul(PL[:, 256:257], CST[:, C_ONE, :], SMALL[:, 22:23], start=True, stop=True), reads=[CST, SMALL], writes=[PL])
        S.op("act", lambda e: e.activation(RC[:, 1:2], PL[:, 256:257], AF.Exp), reads=[PL], writes=[RC])
        S.op("dve", lambda e: e.tensor_single_scalar(RC[:, 2:3], sgR[:, 127:128], 1.0 / 16.0, ALU.mult), reads=[sgR], writes=[RC])
        yield
        for ti in range(33):
            c0, n = tile_rng(ti)
            rin = [RIN[b][ti % 2] for b in range(6)]
            for b in range(4):
                S.dma("sp", [(rin[b][:, 0:n], scfm_d[6 + b, :, c0:c0 + n])], writes=[rin[b]])
            S.dma("sp", [(rin[4][:, 0:n], cos_d[:, c0:c0 + n])], writes=[rin[4]])
            S.dma("sp", [(rin[5][:, 0:n], sin_d[:, c0:c0 + n])], writes=[rin[5]])
            yield
            for (x0, x1, dst) in ((rin[0], rin[1], QR), (rin[2], rin[3], KR)):
                d0, d1 = dst[0][ti % 2], dst[1][ti % 2]
                S.op("dve", lambda e: e.tensor_tensor(RT[0][:, 0:n], x0[:, 0:n], rin[4][:, 0:n], ALU.mult), reads=[x0, rin[4]], writes=[RT[0]])
                S.op("pool", lambda e: e.tensor_tensor(RT[1][:, 0:n], x1[:, 0:n], rin[5][:, 0:n], ALU.mult), reads=[x1, rin[5]], writes=[RT[1]])
                yield
                S.op("dve", lambda e: e.tensor_tensor(d0[:, 0:n], RT[0][:, 0:n], RT[1][:, 0:n], ALU.subtract), reads=[RT[0], RT[1]], writes=[d0])
                S.op("pool", lambda e: e.tensor_tensor(RT[2][:, 0:n], x0[:, 0:n], rin[5][:, 0:n], ALU.mult), reads=[x0, rin[5]], writes=[RT[2]])
                yield
                S.op("dve", lambda e: e.tensor_tensor(RT[3][:, 0:n], x1[:, 0:n], rin[4][:, 0:n], ALU.mult), reads=[x1, rin[4]], writes=[RT[3]])
                S.op("dve", lambda e: e.tensor_tensor(d1[:, 0:n], RT[2][:, 0:n], RT[3][:, 0:n], ALU.add), reads=[RT[2], RT[3]], writes=[d1])
                yield
            qr = [QR[0][ti % 2], QR[1][ti % 2]]
            kr = [KR[0][ti % 2], KR[1][ti % 2]]
            for sub in range(n // 128):
                c = ti * 4 + sub
                cs_ = slice(sub * 128, (sub + 1) * 128)
                rv = RVT[c % 3]
                S.dma("sp", [(rv[:, :], rv_d[c * 128:(c + 1) * 128, :])], writes=[rv])
                for kk in range(2):
                    S.op("pe", lambda e: e.matmul(PL[:, Q_ST], kr[kk][:, cs_], qr[kk][:, cs_], start=(kk == 0), stop=(kk == 1), skip_group_check=True),
                         reads=[kr[kk], qr[kk]], writes=[PL])
                yield
                kds = []
                for kk in range(2):
                    S.op("pe", lambda e: e.transpose(PSH[:, kk * 128:(kk + 1) * 128], kr[kk][:, cs_], CSTB[:, C_ID, :]), reads=[kr[kk], CSTB], writes=[PSH])
                    kd = KD[2 * (c % 2) + kk]
                    S.op("act", lambda e: e.activation(kd[:, :], PSH[:, kk * 128:(kk + 1) * 128], AF.Copy, scale=RC[:, 2:3]), reads=[PSH, RC], writes=[kd])
                    kds.append((kd[:, :], kd))
                yield
                yb_ = YB[c % 2]
                lin_head([(qr[0][:, cs_], qr[0]), (qr[1][:, cs_], qr[1])], kds, rv[:, :], rv, smR, RC[:, 0:1], RC, RC[:, 1:2], RC,
                         RST, RSB, yb_, 0, 128, Q_ST, Q_YD, Q_YO, Q_S)
                S.dma("pool", [(yc_d[c * 128:(c + 1) * 128, :], yb_[:, :])], reads=[yb_], writes=[OUT[2]])
                yield

    run_threads([sb_thread(list(range(0, 33, 2)), PSB[0], PSB[2], PSB[3], "a"),
                 sb_thread(list(range(1, 33, 2)), PSB[1], PSB[4], PSB[5], "b"),
                 lin_thread()])
    S.finish(OUT)
    return nc, S


IN_SIZES = (1024, 1024, 1024, 1024, 2048, 16, 1024, 1024, 1024, 1024, 1024, 1024, 1024, 4096)
IN_OFFS = np.concatenate([[0], np.cumsum(IN_SIZES)]).astype(int)
_CONST_CACHE = {}


def _consts():
    if not _CONST_CACHE:
        _CONST_CACHE["cst"] = host_consts()
        c, s = rope_tables()
        _CONST_CACHE["cos"] = c
        _CONST_CACHE["sin"] = s
    return _CONST_CACHE


def fm(a):
    t = np.ascontiguousarray(a.T)
    return t.reshape(t.shape[0] // 128, 128, t.shape[1])


def colvec(v):
    return np.ascontiguousarray(v.reshape(-1, 128).T)


def prep_H(h, w_in, conv_a, ssd_conv_w, ssd_conv_b, ssd_dt_bias, ssd_a_log, ssd_d, npre):
    cs = _consts()
    hT = fm(h)
    maps = []
    O = IN_OFFS
    for c in range(NCORE):
        g = c // 2
        hh = c // 2
        fmcols = np.concatenate([
            np.arange(O[0] + 128 * c, O[0] + 128 * c + 128),
            np.arange(O[1] + 128 * c, O[1] + 128 * c + 128),
            np.arange(O[2] + 128 * c, O[2] + 128 * c + 128),
            np.arange(O[4] + 128 * c, O[4] + 128 * c + 128),
            np.arange(O[4] + 1024 + 128 * g, O[4] + 1024 + 128 * g + 128),
            np.arange(O[4] + 1536 + 128 * g, O[4] + 1536 + 128 * g + 128),
            np.arange(O[6] + 256 * hh, O[6] + 256 * hh + 256),
            np.arange(O[7] + 256 * hh, O[7] + 256 * hh + 256),
            np.arange(O[10] + 128 * c, O[10] + 128 * c + 128),
            np.arange(O[11] + 128 * c, O[11] + 128 * c + 128),
        ])
        tmcols = np.concatenate([
            np.arange(O[8] + 128 * c, O[8] + 128 * c + 128),
            np.arange(O[12] + 128 * c, O[12] + 128 * c + 128),
            np.arange(O[5] + 2 * c, O[5] + 2 * c + 2),
        ])
        xcols = [np.arange(128 * c, 128 * c + 128), np.arange(1024 + 128 * g, 1024 + 128 * g + 128),
                 np.arange(1536 + 128 * g, 1536 + 128 * g + 128)]
        scw = np.concatenate([ssd_conv_w[:, xc].T for xc in xcols], axis=1)
        scb = np.stack([ssd_conv_b[xc] for xc in xcols], axis=1)
        drow = np.broadcast_to(np.repeat(ssd_d[2 * c:2 * c + 2], 64)[None, :], (128, 128))
        lg = math.log(1.0 - 2.0 ** (-5.0 - hh))
        maps.append({
            "hT": hT,
            "npre": colvec(npre),
            "wfm": np.ascontiguousarray(w_in[:, fmcols].reshape(KC, 128, NFM * 128)),
            "wtm": np.ascontiguousarray(w_in[:, tmcols].reshape(KC, 128, NTM)),
            "cva": np.ascontiguousarray(conv_a[:, 128 * c:128 * c + 128].T),
            "scw": np.ascontiguousarray(scw),
            "scb": np.ascontiguousarray(scb),
            "dtb": np.ascontiguousarray(np.broadcast_to(ssd_dt_bias[None, 2 * c:2 * c + 2], (128, 2))),
            "alog": np.ascontiguousarray(np.broadcast_to(ssd_a_log[None, 2 * c:2 * c + 2], (128, 2))),
            "drow": np.ascontiguousarray(drow),
            "lgam": np.full((128, 1), lg, np.float32),
            "cos": cs["cos"], "sin": cs["sin"], "cst": cs["cst"],
        })
    return maps


def gather_H(res):
    ya = np.concatenate([r["yaT"].T for r in res], axis=1)
    yb = np.concatenate([r["yb"] for r in res], axis=1)
    yc = np.concatenate([r["yc"] for r in res], axis=1)
    yd = np.concatenate([r["ydT"].T for r in res], axis=1)
    return ya, yb, yc, yd


NTH = TOK // 3
NTT = NTH // TT


def build_T():
    nc = bass.Bass("TRN2", target_bir_lowering=False)
    S = Sched(nc)
    ei = lambda n, s, dt=F32: nc.dram_tensor(n, list(s), dt, kind="ExternalInput")
    hT_d = ei("hT", [KC, 128, TOK])
    yin_d = [ei(nm, [KC, 128, TOK]) for nm in ("ya", "yb", "yc", "yd")]
    wzg_d = ei("wzg", [D, 6144])
    wbr_d = ei("wbr", [4, D, D])
    wout_d = ei("wout", [D, D])
    wfi_d = ei("wfi", [D, 2 * DFF])
    wfo_d = ei("wfo", [DFF, D])
    vec_d = ei("vecs", [128, 40])
    cst_d = ei("cst", [128, 7, 128])
    hout_d = nc.dram_tensor("hout", [KC, 128, TOK], F32, kind="ExternalOutput")
    HOUT = Buf(hout_d, "houtd")

    CST = S.sb("cst", [128, 7, 128], F32)
    VEC = S.sb("vec", [128, 40], F32)
    S.dma("sp", [(CST[:, :, :], cst_d[:, :, :])], writes=[CST])
    S.dma("sp", [(VEC[:, :], vec_d[:, :])], writes=[VEC])
    V_PRE, V_SSD, V_POST, V_FPRE, V_FPOST = 0, 8, 16, 24, 32
    H = [S.sb(f"h{k}", [128, NTH], F32) for k in range(KC)]
    HN = [S.sb(f"hn{k}", [128, NTH], BF16) for k in range(KC)]
    BIGF = [S.sb(f"bf{k}", [128, NTH], F32) for k in range(KC)]
    BRF = [S.sb(f"br{k}", [128, NTH], BF16) for k in range(32)]
    BR = [BRF[8 * n:8 * n + 8] for n in range(4)]
    HID = BRF[0:22]
    MG = [S.sb(f"mg{k}", [128, NTH], BF16) for k in range(KC)]
    WS = [S.sb(f"ws{i}", [128, 11, 128], F32) for i in range(3)]
    WB = [S.sb(f"wb{i}", [128, 22, 128], BF16) for i in range(4)]
    SQ = [S.sb(f"sq{i}", [128, TT], F32) for i in range(2)]
    R1 = S.sb("r1", [128, TT], F32)
    RS = [S.sb(f"rs{i}", [128, TT], F32) for i in range(2)]
    STG = [S.sb(f"stg{i}", [128, TT], F32) for i in range(4)]
    SIG = [S.sb(f"sig{i}", [128, TT], F32) for i in range(2)]
    TMPF = [S.sb(f"tmpf{i}", [128, TT], F32) for i in range(2)]
    ACC = [S.sb(f"acc{i}", [128, TT], F32) for i in range(NTT)]
    MEAN = S.sb("mean", [128, TT], F32)
    DD = [S.sb(f"dd{i}", [128, TT], F32) for i in range(2)]
    PSB = [S.ps(f"ps{i}", [128, 512]) for i in range(8)]
    PS_SS = PSB[0]
    ctr = {"w": 0, "ws": 0, "ps": 0, "rs": 0, "sq": 0, "stg": 0, "sig": 0, "tmp": 0}

    def nxt(key, lst):
        i = ctr[key]
        ctr[key] += 1
        return lst[i % len(lst)]

    def load_w(wd2, kc, cb):
        wb = nxt("w", WB)
        v = wd2.rearrange("(k p) n -> p k n", p=128)
        for k0 in range(0, kc, 11):
            k1 = min(kc, k0 + 11)
            ws = nxt("ws", WS)
            S.dma("sp", [(ws[:, 0:k1 - k0, :], v[:, k0:k1, cb * 128:(cb + 1) * 128])], writes=[ws])
            ctr["cast"] = ctr.get("cast", 0) + 1
            if ctr["cast"] % 2 == 0:
                S.op("dve", lambda e: e.tensor_copy(wb[:, k0:k1, :], ws[:, 0:k1 - k0, :]), reads=[ws], writes=[wb])
            else:
                S.op("act", lambda e: e.activation(wb[:, k0:k1, :], ws[:, 0:k1 - k0, :], AF.Copy), reads=[ws], writes=[wb])
        return wb

    def tsl(t):
        return slice(t * TT, (t + 1) * TT)

    def gemm(wb, kc, X, t):
        ps = PSB[1 + ctr["ps"] % 6]
        ctr["ps"] += 1
        for k in range(kc):
            S.op("pe", lambda e: e.matmul(ps[:, 0:TT], wb[:, k, :], X[k][:, tsl(t)], start=(k == 0), stop=(k == kc - 1)),
                 reads=[wb, X[k]], writes=[ps])
        return ps

    def gemm2(wb, kc, X):
        pss = []
        for t in range(NTT):
            pss.append(PSB[1 + ctr["ps"] % 6])
            ctr["ps"] += 1
        for k in range(kc):
            for t in range(NTT):
                S.op("pe", lambda e: e.matmul(pss[t][:, 0:TT], wb[:, k, :], X[k][:, tsl(t)], start=(k == 0), stop=(k == kc - 1)),
                     reads=[wb, X[k]], writes=[pss[t]])
        return pss

    def rstd_tile(chunks, t, nfeat, sl_fn=None):
        n = len(chunks)
        for i, (b, ap) in enumerate(chunks):
            sq = nxt("sq", SQ)
            S.op("pool", lambda e: e.tensor_tensor(sq[:, :], ap, ap, ALU.mult), reads=[b], writes=[sq])
            S.op("pe", lambda e: e.matmul(PS_SS[:, 0:TT], CST[:, C_ONE, :], sq[:, :], start=(i == 0), stop=(i == n - 1)),
                 reads=[CST, sq], writes=[PS_SS])
        S.op("act", lambda e: e.activation(R1[:, :], PS_SS[:, 0:TT], AF.Sqrt, bias=EPS, scale=1.0 / nfeat), reads=[PS_SS], writes=[R1])
        rs = nxt("rs", RS)
        S.op("dve", lambda e: e.reciprocal(rs[:, :], R1[:, :]), reads=[R1], writes=[rs])
        return rs

    def norm_to_hn(voff):
        for t in range(NTT):
            rs = rstd_tile([(H[k], H[k][:, tsl(t)]) for k in range(KC)], t, D)
            for k in range(KC):
                S.op("dve", lambda e: e.scalar_tensor_tensor(HN[k][:, tsl(t)], H[k][:, tsl(t)], VEC[:, voff + k:voff + k + 1], rs[:, :], ALU.mult, ALU.mult),
                     reads=[H[k], VEC, rs], writes=[HN[k]])

    def add_normed(voff):
        for t in range(NTT):
            rs = rstd_tile([(BIGF[k], BIGF[k][:, tsl(t)]) for k in range(KC)], t, D)
            for k in range(KC):
                tm = nxt("tmp", TMPF)
                S.op("dve", lambda e: e.scalar_tensor_tensor(tm[:, :], BIGF[k][:, tsl(t)], VEC[:, voff + k:voff + k + 1], rs[:, :], ALU.mult, ALU.mult),
                     reads=[BIGF[k], VEC, rs], writes=[tm])
                S.op("pool", lambda e: e.tensor_tensor(H[k][:, tsl(t)], H[k][:, tsl(t)], tm[:, :], ALU.add), reads=[H[k], tm], writes=[H[k]])

    for hf in range(TOK // NTH):
        off = hf * NTH
        for k in range(KC):
            S.dma("sp", [(H[k][:, :], hT_d[k, :, off:off + NTH])], writes=[H[k]])
        norm_to_hn(V_PRE)
        for cb in range(KC):
            wb = load_w(wzg_d, KC, cb)
            for t in range(NTT):
                ps = gemm(wb, KC, HN, t)
                sg = nxt("sig", SIG)
                S.op("act", lambda e: e.activation(sg[:, :], ps[:, 0:TT], AF.Silu), reads=[ps], writes=[sg])
                st = nxt("stg", STG)
                S.dma("sp", [(st[:, :], yin_d[1][cb, :, off + t * TT:off + (t + 1) * TT])], writes=[st])
                S.op("dve", lambda e: e.tensor_tensor(BIGF[cb][:, tsl(t)], st[:, :], sg[:, :], ALU.mult), reads=[st, sg], writes=[BIGF[cb]])
        for t in range(NTT):
            for g in range(4):
                rs = rstd_tile([(BIGF[2 * g + i], BIGF[2 * g + i][:, tsl(t)]) for i in range(2)], t, 256)
                for i in range(2):
                    k = 2 * g + i
                    S.op("dve", lambda e: e.scalar_tensor_tensor(BR[1][k][:, tsl(t)], BIGF[k][:, tsl(t)], VEC[:, V_SSD + k:V_SSD + k + 1], rs[:, :], ALU.mult, ALU.mult),
                         reads=[BIGF[k], VEC, rs], writes=[BR[1][k]])
        for cb in range(KC):
            wb = load_w(wzg_d, KC, KC + cb)
            for t in range(NTT):
                ps = gemm(wb, KC, HN, t)
                S.op("act", lambda e: e.activation(BIGF[cb][:, tsl(t)], ps[:, 0:TT], AF.Silu), reads=[ps], writes=[BIGF[cb]])
        for t in range(NTT):
            for g in range(4):
                ycs = []
                for i in range(2):
                    st = nxt("stg", STG)
                    S.dma("sp", [(st[:, :], yin_d[2][2 * g + i, :, off + t * TT:off + (t + 1) * TT])], writes=[st])
                    ycs.append(st)
                for i in range(2):
                    S.op("pe", lambda e: e.matmul(PS_SS[:, 0:TT], CST[:, C_ONE, :], ycs[i][:, :], start=(i == 0), stop=(i == 1)),
                         reads=[CST, ycs[i]], writes=[PS_SS])
                S.op("act", lambda e: e.activation(MEAN[:, :], PS_SS[:, 0:TT], AF.Copy, scale=1.0 / 256.0), reads=[PS_SS], writes=[MEAN])
                for i in range(2):
                    S.op("dve", lambda e: e.tensor_tensor(DD[i][:, :], ycs[i][:, :], MEAN[:, :], ALU.subtract), reads=[ycs[i], MEAN], writes=[DD[i]])
                rs = rstd_tile([(DD[i], DD[i][:, :]) for i in range(2)], t, 256)
                for i in range(2):
                    k = 2 * g + i
                    tm = nxt("tmp", TMPF)
                    S.op("dve", lambda e: e.tensor_tensor(tm[:, :], DD[i][:, :], rs[:, :], ALU.mult), reads=[DD[i], rs], writes=[tm])
                    S.op("dve", lambda e: e.tensor_tensor(BR[2][k][:, tsl(t)], tm[:, :], BIGF[k][:, tsl(t)], ALU.mult), reads=[tm, BIGF[k]], writes=[BR[2][k]])
        for (bi, yi) in ((0, 0), (3, 3)):
            for cb in range(KC):
                for t in range(NTT):
                    st = nxt("stg", STG)
                    S.dma("sp", [(st[:, :], yin_d[yi][cb, :, off + t * TT:off + (t + 1) * TT])], writes=[st])
                    S.op("pool", lambda e: e.tensor_copy(BR[bi][cb][:, tsl(t)], st[:, :]), reads=[st], writes=[BR[bi][cb]])
        for m in range(KC):
            for n in range(4):
                wbg = load_w(wzg_d, KC, 2 * KC + n * KC + m)
                wbu = load_w(wbr_d[n], KC, m)
                for t in range(NTT):
                    psg = gemm(wbg, KC, HN, t)
                    psu = gemm(wbu, KC, BR[n], t)
                    sg = nxt("sig", SIG)
                    S.op("act", lambda e: e.activation(sg[:, :], psg[:, 0:TT], AF.Sigmoid), reads=[psg], writes=[sg])
                    if n == 0:
                        S.op("dve", lambda e: e.tensor_tensor(ACC[t][:, :], sg[:, :], psu[:, 0:TT], ALU.mult), reads=[sg, psu], writes=[ACC[t]])
                    else:
                        tm = nxt("tmp", TMPF)
                        S.op("dve", lambda e: e.tensor_tensor(tm[:, :], sg[:, :], psu[:, 0:TT], ALU.mult), reads=[sg, psu], writes=[tm])
                        S.op("pool", lambda e: e.tensor_tensor(ACC[t][:, :], ACC[t][:, :], tm[:, :], ALU.add), reads=[ACC[t], tm], writes=[ACC[t]])
            for t in range(NTT):
                S.op("pool", lambda e: e.tensor_copy(MG[m][:, tsl(t)], ACC[t][:, :]), reads=[ACC[t]], writes=[MG[m]])
        for cb in range(KC):
            wb = load_w(wout_d, KC, cb)
            pss = gemm2(wb, KC, MG)
            for t in range(NTT):
                ps = pss[t]
                S.op("act", lambda e: e.activation(BIGF[cb][:, tsl(t)], ps[:, 0:TT], AF.Copy), reads=[ps], writes=[BIGF[cb]])
        add_normed(V_POST)
        norm_to_hn(V_FPRE)
        for j in range(22):
            wbg = load_w(wfi_d, KC, j)
            wbu = load_w(wfi_d, KC, 22 + j)
            for t in range(NTT):
                psg = gemm(wbg, KC, HN, t)
                psu = gemm(wbu, KC, HN, t)
                sg = nxt("sig", SIG)
                S.op("act", lambda e: e.activation(sg[:, :], psg[:, 0:TT], AF.Silu), reads=[psg], writes=[sg])
                S.op("dve", lambda e: e.tensor_tensor(HID[j][:, tsl(t)], sg[:, :], psu[:, 0:TT], ALU.mult), reads=[sg, psu], writes=[HID[j]])
        for cb in range(KC):
            wb = load_w(wfo_d, 22, cb)
            pss = gemm2(wb, 22, HID)
            for t in range(NTT):
                ps = pss[t]
                S.op("act", lambda e: e.activation(BIGF[cb][:, tsl(t)], ps[:, 0:TT], AF.Copy), reads=[ps], writes=[BIGF[cb]])
        add_normed(V_FPOST)
        for k in range(KC):
            S.dma("pool", [(hout_d[k, :, off:off + NTH], H[k][:, :])], reads=[H[k]], writes=[HOUT])
    S.finish([HOUT])
    return nc, S


def prep_T(h, ya, yb, yc, yd, w_in, w_branch, w_out, w_ffn_in, w_ffn_out, npre, ssdn, npost, nfpre, nfpost):
    cs = _consts()
    O = IN_OFFS
    wzg = np.ascontiguousarray(np.concatenate([w_in[:, O[3]:O[3] + 1024], w_in[:, O[9]:O[9] + 1024], w_in[:, O[13]:O[13] + 4096]], axis=1))
    vecs = np.ascontiguousarray(np.concatenate([colvec(v) for v in (npre, ssdn, npost, nfpre, nfpost)], axis=1))
    maps = []
    for c in range(NCORE):
        sl = slice(c * TOK, (c + 1) * TOK)
        maps.append({
            "hT": fm(h[sl]), "ya": fm(ya[sl]), "yb": fm(yb[sl]), "yc": fm(yc[sl]), "yd": fm(yd[sl]),
            "wzg": wzg, "wbr": np.ascontiguousarray(w_branch), "wout": np.ascontiguousarray(w_out),
            "wfi": np.ascontiguousarray(w_ffn_in), "wfo": np.ascontiguousarray(w_ffn_out),
            "vecs": vecs, "cst": cs["cst"],
        })
    return maps


def gather_T(res):
    return np.concatenate([r["hout"].reshape(D, TOK).T for r in res], axis=0)


_PROG = {}


def kernel(x, meta, w_in, conv_a, ssd_conv_w, ssd_conv_b, ssd_dt_bias, ssd_a_log, ssd_d, ssd_norm,
           w_branch, w_out, w_ffn_in, w_ffn_out, norm_mix_pre, norm_mix_post, norm_ffn_pre, norm_ffn_post):
    f = lambda a: np.asarray(a, dtype=np.float32)
    x, meta = f(x), f(meta)
    h = np.concatenate([np.zeros((L - 16 - x.shape[1], D), np.float32), meta, x[0]], axis=0)
    if "H" not in _PROG:
        _PROG["H"] = build_H()[0]
        _PROG["T"] = build_T()[0]
    cores = list(range(NCORE))
    for l in range(2):
        mH = prep_H(h, f(w_in[l]), f(conv_a[l]), f(ssd_conv_w[l]), f(ssd_conv_b[l]), f(ssd_dt_bias[l]),
                    f(ssd_a_log[l]), f(ssd_d[l]), f(norm_mix_pre[l]))
        rH = run_bass_kernel_spmd(_PROG["H"], mH, core_ids=cores)
        ya, yb, yc, yd = gather_H(rH.results)
        mT = prep_T(h, ya, yb, yc, yd, f(w_in[l]), f(w_branch[l]), f(w_out[l]), f(w_ffn_in[l]), f(w_ffn_out[l]),
                    f(norm_mix_pre[l]), f(ssd_norm[l]), f(norm_mix_post[l]), f(norm_ffn_pre[l]), f(norm_ffn_post[l]))
        rT = run_bass_kernel_spmd(_PROG["T"], mT, core_ids=cores)
        h = gather_T(rT.results)
    return np.ascontiguousarray(h[128:][None]).astype(np.float32)
```
